# Optimizing a Trainium2 kernel written in Bass

```python
import math
import jax, jax.numpy as jnp
from jax import lax
import numpy as np

D_MODEL = 2048
BATCH = 2
SEQ = 4096
DEPTH = 2
DEC_BATCH = 2
DEC_SEQ = 16384
PAST_LEN = 128

GRID_W = 64
HEAD_DIM = 128
GDN_HEADS = 8
ATT_HEADS = 8
ATT_KV_HEADS = 2
ATT_GROUP = ATT_HEADS // ATT_KV_HEADS
GDN_WIDTH = GDN_HEADS * HEAD_DIM
ATT_WIDTH = ATT_HEADS * HEAD_DIM
ATT_KV_WIDTH = ATT_KV_HEADS * HEAD_DIM
MIX_WIDTH = GDN_WIDTH + ATT_WIDTH
CONV_W = 5
CHUNK = 64
Q_BLOCK = 128
ROPE_THETA = 10000.0
POOL_WINDOWS = (2, 4, 8, 16)
N_POOL_GROUPS = 4
POOL_GROUP = D_MODEL // N_POOL_GROUPS
D_FF = 4 * D_MODEL
N_EVEN = (DEPTH + 1) // 2
N_ODD = DEPTH // 2
DEEPNORM_ALPHA = (2 * DEPTH) ** 0.25
DEEPNORM_BETA = (8 * DEPTH) ** -0.25
NORM_EPS = 1e-6
LN_EPS = 1e-5

GDN_QKV = 3 * GDN_WIDTH
OFF_Z = GDN_QKV
OFF_BETA = OFF_Z + GDN_WIDTH
OFF_A = OFF_BETA + 2 * GDN_HEADS
OFF_AQ = OFF_A + 2 * GDN_HEADS
OFF_AK = OFF_AQ + ATT_WIDTH
OFF_AV = OFF_AK + ATT_KV_WIDTH
D_IN = OFF_AV + ATT_KV_WIDTH

kernel_name = 'hybrid_gdn_axialgqa_pool_encoder'


def _layer_norm(x, g, b):
    xf = x.astype(jnp.float32)
    mu = jnp.mean(xf, -1, keepdims=True)
    xc = xf - mu
    var = jnp.mean(xc * xc, -1, keepdims=True)
    return (xc * lax.rsqrt(var + LN_EPS) * g + b).astype(x.dtype)


def _rms_norm(x, w):
    xf = x.astype(jnp.float32)
    return xf * lax.rsqrt(jnp.mean(xf * xf, -1, keepdims=True) + NORM_EPS) * w.astype(jnp.float32)


def _l2_normalize(x):
    return x * lax.rsqrt(jnp.sum(x * x, -1, keepdims=True) + NORM_EPS)


def _modulate(x, shift, scale):
    return x * (1 + scale[:, None, :]) + shift[:, None, :]


def _short_conv(x, w):
    c = x.shape[-1]
    return lax.conv_general_dilated(x, w[:, None, :], window_strides=(1,),
                                    padding=[(CONV_W // 2, CONV_W // 2)],
                                    dimension_numbers=('NWC', 'WIO', 'NWC'),
                                    feature_group_count=c)


def _gated_delta_rule(q, k, v, g, beta):
    bsz, t, h, dk = k.shape
    dv = v.shape[-1]
    n = t // CHUNK

    def to_chunks(a):
        return a.reshape(bsz, n, CHUNK, h, -1).transpose(0, 3, 1, 2, 4)

    q, k, v = to_chunks(q), to_chunks(k), to_chunks(v)
    beta = beta.reshape(bsz, n, CHUNK, h).transpose(0, 3, 1, 2)
    gc = jnp.cumsum(g.reshape(bsz, n, CHUNK, h).transpose(0, 3, 1, 2), axis=-1)
    idx = jnp.arange(CHUNK)
    incl = idx[:, None] >= idx[None, :]
    strict = idx[:, None] > idx[None, :]
    decay = jnp.exp(jnp.where(incl, gc[..., :, None] - gc[..., None, :], -jnp.inf))
    k_beta = k * beta[..., None]
    v_beta = v * beta[..., None]
    lower = jnp.where(strict, jnp.einsum('bhncd,bhnsd->bhncs', k_beta, k) * decay, 0.0)
    eye = jnp.eye(CHUNK, dtype=jnp.float32)
    a = eye + lower
    tinv = lax.linalg.triangular_solve(a, jnp.broadcast_to(eye, a.shape), left_side=True, lower=True)
    u = jnp.einsum('bhncs,bhnsd->bhncd', tinv, v_beta)
    w = jnp.einsum('bhncs,bhnsd->bhncd', tinv, k_beta * jnp.exp(gc)[..., None])
    qk = jnp.einsum('bhncd,bhnsd->bhncs', q, k) * decay

    def step(state, inp):
        q_i, k_i, u_i, w_i, qk_i, g_i = inp
        v_new = u_i - jnp.einsum('bhck,bhkv->bhcv', w_i, state)
        o_i = (jnp.einsum('bhck,bhkv->bhcv', q_i * jnp.exp(g_i)[..., None], state)
               + jnp.einsum('bhcs,bhsv->bhcv', qk_i, v_new))
        g_last = g_i[..., -1]
        state = (state * jnp.exp(g_last)[..., None, None]
                 + jnp.einsum('bhck,bhcv->bhkv', k_i * jnp.exp(g_last[..., None] - g_i)[..., None], v_new))
        return state, o_i

    xs = tuple(jnp.moveaxis(a_, 2, 0) for a_ in (q, k, u, w, qk, gc))
    state0 = jnp.zeros((bsz, h, dk, dv), jnp.float32)
    _, o = lax.scan(step, state0, xs)
    return o.transpose(1, 0, 3, 2, 4).reshape(bsz, t, h, dv)


def _axial_rope(n_tokens):
    rows = n_tokens // GRID_W
    row = jnp.broadcast_to(jnp.arange(rows)[:, None], (rows, GRID_W)).reshape(-1).astype(jnp.float32)
    col = jnp.broadcast_to(jnp.arange(GRID_W)[None, :], (rows, GRID_W)).reshape(-1).astype(jnp.float32)
    half = HEAD_DIM // 2
    inv_freq = ROPE_THETA ** (-jnp.arange(0, half, 2, dtype=jnp.float32) / half)
    ang_r = row[:, None] * inv_freq
    ang_c = col[:, None] * inv_freq
    ang = jnp.concatenate([ang_r, ang_r, ang_c, ang_c], -1)
    return jnp.cos(ang), jnp.sin(ang)


def _apply_rope(x, cos, sin):
    x1, x2, x3, x4 = jnp.split(x, 4, axis=-1)
    rot = jnp.concatenate([-x2, x1, -x4, x3], -1)
    return x * cos[:, None, :] + rot * sin[:, None, :]


def _block_attention(q, k, v):
    bsz, t = q.shape[0], q.shape[1]
    nb = t // Q_BLOCK
    qb = q.reshape(bsz, nb, Q_BLOCK, ATT_KV_HEADS, ATT_GROUP, HEAD_DIM).transpose(1, 0, 2, 3, 4, 5)
    scale = HEAD_DIM ** -0.5

    def one_block(q_blk):
        s = jnp.einsum('bqkgd,bskd->bkgqs', q_blk, k) * scale
        p = jax.nn.softmax(s, axis=-1)
        return jnp.einsum('bkgqs,bskd->bqkgd', p, v)

    o = lax.map(one_block, qb)
    return o.transpose(1, 0, 2, 3, 4, 5).reshape(bsz, t, ATT_WIDTH)


def _mix_delta_attn(h, w_in, conv_w, a_log, dt_bias, gdn_norm_w, q_norm_w, k_norm_w, w_out):
    bsz, t, _ = h.shape
    proj = h @ w_in
    qkv, z, b_raw, a_raw, aq, ak, av = jnp.split(proj, [OFF_Z, OFF_BETA, OFF_A, OFF_AQ, OFF_AK, OFF_AV], axis=-1)
    qkv = jax.nn.silu(_short_conv(qkv.astype(jnp.float32), conv_w.astype(jnp.float32)))
    gq, gk, gv = [a_.reshape(bsz, t, GDN_HEADS, HEAD_DIM) for a_ in jnp.split(qkv, 3, axis=-1)]
    gq = _l2_normalize(gq) * (HEAD_DIM ** -0.5)
    gk = _l2_normalize(gk)
    beta = jax.nn.sigmoid(b_raw.astype(jnp.float32)).reshape(bsz, t, 2, GDN_HEADS)
    logdec = -jnp.exp(a_log.astype(jnp.float32)) * jax.nn.softplus(
        a_raw.astype(jnp.float32).reshape(bsz, t, 2, GDN_HEADS) + dt_bias.astype(jnp.float32))
    o_fwd = _gated_delta_rule(gq, gk, gv, logdec[:, :, 0], beta[:, :, 0])
    flip = lambda a_: jnp.flip(a_, axis=1)
    o_bwd = flip(_gated_delta_rule(flip(gq), flip(gk), flip(gv), flip(logdec[:, :, 1]), flip(beta[:, :, 1])))
    o_gdn = _rms_norm(o_fwd + o_bwd, gdn_norm_w) * jax.nn.silu(
        z.astype(jnp.float32).reshape(bsz, t, GDN_HEADS, HEAD_DIM))
    o_gdn = o_gdn.reshape(bsz, t, GDN_WIDTH)
    aq = _rms_norm(aq.reshape(bsz, t, ATT_HEADS, HEAD_DIM), q_norm_w)
    ak = _rms_norm(ak.reshape(bsz, t, ATT_KV_HEADS, HEAD_DIM), k_norm_w)
    av = av.astype(jnp.float32).reshape(bsz, t, ATT_KV_HEADS, HEAD_DIM)
    cos, sin = _axial_rope(t)
    o_att = _block_attention(_apply_rope(aq, cos, sin), _apply_rope(ak, cos, sin), av)
    o = jnp.concatenate([o_gdn, o_att], axis=-1).astype(h.dtype)
    return o @ w_out


def _mix_pool(h, pool_w, pool_scale):
    bsz, t, d = h.shape
    hf = h.astype(jnp.float32)
    cs = jnp.concatenate([jnp.zeros((bsz, 1, d), jnp.float32), jnp.cumsum(hf, axis=1)], axis=1)
    pos = np.arange(t)
    outs = []
    for gi, win in enumerate(POOL_WINDOWS):
        lo = np.clip(pos - win // 2, 0, t - 1)
        hi = np.clip(pos + (win - 1 - win // 2), 0, t - 1)
        cnt = jnp.asarray(hi - lo + 1, dtype=jnp.float32)[None, :, None]
        sl = slice(gi * POOL_GROUP, (gi + 1) * POOL_GROUP)
        csg = cs[..., sl]
        mean = (csg[:, hi + 1] - csg[:, lo]) / cnt
        outs.append((mean - hf[..., sl]) @ pool_w[gi].astype(jnp.float32))
    return (jnp.concatenate(outs, axis=-1) * pool_scale.astype(jnp.float32)).astype(h.dtype)


def _mlp(h, w1, w2):
    u = jnp.maximum(h @ w1, 0)
    return (u * u) @ w2


def _trunk(x, c, w_in, conv_w, a_log, dt_bias, gdn_norm_w, q_norm_w, k_norm_w, w_out,
           pool_w, pool_scale, mlp_w1, mlp_w2, ada_w, ada_b, ln_g, ln_b):
    for l in range(DEPTH):
        cond = jax.nn.silu(c) @ ada_w[l] + ada_b[l]
        sh_m, sc_m, g_m, sh_f, sc_f, g_f = jnp.split(cond, 6, axis=-1)
        h = _modulate(x, sh_m, sc_m)
        if l % 2 == 0:
            j = l // 2
            m = _mix_delta_attn(h, w_in[j], conv_w[j], a_log[j], dt_bias[j], gdn_norm_w[j],
                                q_norm_w[j], k_norm_w[j], w_out[j])
        else:
            j = l // 2
            m = _mix_pool(h, pool_w[j], pool_scale[j])
        x = _layer_norm(DEEPNORM_ALPHA * x + g_m[:, None, :] * m, ln_g[l, 0], ln_b[l, 0])
        h = _modulate(x, sh_f, sc_f)
        x = _layer_norm(DEEPNORM_ALPHA * x + g_f[:, None, :] * _mlp(h, mlp_w1[l], mlp_w2[l]),
                        ln_g[l, 1], ln_b[l, 1])
    return x


def setup_inputs(seed: int = 0) -> dict:
    key = jax.random.key(seed)
    ks = jax.random.split(key, 20)
    f32 = jnp.float32
    nrm = lambda k_, shape, s: jax.random.normal(k_, shape, f32) * s
    return {
        'x_prompt': nrm(ks[0], (BATCH, SEQ, D_MODEL), 1.0),
        'x_sample': nrm(ks[1], (DEC_BATCH, DEC_SEQ, D_MODEL), 1.0),
        'c_prompt': nrm(ks[2], (BATCH, D_MODEL), 1.0),
        'c_sample': nrm(ks[3], (DEC_BATCH, D_MODEL), 1.0),
        'w_in': nrm(ks[4], (N_EVEN, D_MODEL, D_IN), D_MODEL ** -0.5),
        'conv_w': nrm(ks[5], (N_EVEN, CONV_W, GDN_QKV), CONV_W ** -0.5),
        'a_log': jnp.log(jax.random.uniform(ks[6], (N_EVEN, 2, GDN_HEADS), f32, 1.0, 16.0)),
        'dt_bias': nrm(ks[7], (N_EVEN, 2, GDN_HEADS), 0.1),
        'gdn_norm_w': 1.0 + nrm(ks[8], (N_EVEN, HEAD_DIM), 0.05),
        'q_norm_w': 1.0 + nrm(ks[9], (N_EVEN, HEAD_DIM), 0.05),
        'k_norm_w': 1.0 + nrm(ks[10], (N_EVEN, HEAD_DIM), 0.05),
        'w_out': nrm(ks[11], (N_EVEN, MIX_WIDTH, D_MODEL), MIX_WIDTH ** -0.5 * DEEPNORM_BETA),
        'pool_w': nrm(ks[12], (N_ODD, N_POOL_GROUPS, POOL_GROUP, POOL_GROUP), POOL_GROUP ** -0.5 * DEEPNORM_BETA),
        'pool_scale': 1.0 + nrm(ks[13], (N_ODD, D_MODEL), 0.05),
        'mlp_w1': nrm(ks[14], (DEPTH, D_MODEL, D_FF), D_MODEL ** -0.5),
        'mlp_w2': nrm(ks[15], (DEPTH, D_FF, D_MODEL), D_FF ** -0.5 * DEEPNORM_BETA),
        'ada_w': nrm(ks[16], (DEPTH, D_MODEL, 6 * D_MODEL), D_MODEL ** -0.5),
        'ada_b': nrm(ks[17], (DEPTH, 6 * D_MODEL), 0.02),
        'ln_g': 1.0 + nrm(ks[18], (DEPTH, 2, D_MODEL), 0.05),
        'ln_b': nrm(ks[19], (DEPTH, 2, D_MODEL), 0.02),
    }


def reference(x_prompt, x_sample, c_prompt, c_sample, w_in, conv_w, a_log, dt_bias, gdn_norm_w,
              q_norm_w, k_norm_w, w_out, pool_w, pool_scale, mlp_w1, mlp_w2, ada_w, ada_b, ln_g, ln_b):
    y_prompt = _trunk(x_prompt, c_prompt, w_in, conv_w, a_log, dt_bias, gdn_norm_w, q_norm_w, k_norm_w,
                      w_out, pool_w, pool_scale, mlp_w1, mlp_w2, ada_w, ada_b, ln_g, ln_b)
    y_sample = _trunk(x_sample, c_sample, w_in, conv_w, a_log, dt_bias, gdn_norm_w, q_norm_w, k_norm_w,
                      w_out, pool_w, pool_scale, mlp_w1, mlp_w2, ada_w, ada_b, ln_g, ln_b)
    return (y_prompt, y_sample)
```

```python
import contextlib
import math
import numpy as np
import concourse.bass as bass
import concourse.mybir as mybir
from concourse.bass_utils import run_bass_kernel_spmd

F32 = mybir.dt.float32
BF16 = mybir.dt.bfloat16
AF = mybir.ActivationFunctionType
ALU = mybir.AluOpType
AX = mybir.AxisListType

D = 2048
KC = 16
DFF = 8192
FC = 64
DIN = 5664
HD = 128
NH = 8
ALPHA = 4 ** 0.25
NORM_EPS = 1e-6
LN_EPS = 1e-5
NEGBIG = -60000.0
POOL_WINDOWS = (2, 4, 8, 16)
C_GQ, C_GK, C_GV, C_Z, C_B, C_A, C_AQ, C_AK, C_AV = 0, 1024, 2048, 3072, 4096, 4112, 4128, 5152, 5408


class Cfg:
    def __init__(self, Ts=16384, Tp=4096, debug=False, stop_after=None):
        self.T = [Ts, Tp]
        self.L = [Ts // 4, Tp // 4]
        self.E = [l + 256 for l in self.L]
        self.debug = debug
        self.stop_after = stop_after


class Res:
    __slots__ = ("name", "lw", "rd")

    def __init__(self, name=""):
        self.name = name
        self.lw = None
        self.rd = []


class TT:
    def __init__(self, t, name):
        self.t = t
        self.r = Res(name)

    def __getitem__(self, k):
        return self.t[k]


def _res(x):
    return x.r if isinstance(x, TT) else x


class _Rec:
    def __getattr__(self, name):
        def f(*a, **k):
            self.call = (name, a, k)
            return self
        return f


class Sched:
    NDS = 24

    def __init__(self, nc, st):
        self.nc = nc
        self.csem = {e: st.enter_context(nc.semaphore("c_" + e)) for e in ("pe", "act", "dve", "pool")}
        self.dsem = {(q, j): st.enter_context(nc.semaphore("d_%s_%d" % (q, j))) for q in ("sp", "act") for j in range(self.NDS)}
        self.cnt = {e: 0 for e in ("pe", "act", "dve", "pool")}
        self.dcnt = {q: 0 for q in ("sp", "act")}
        self.ops = []
        self.nres = []

    def begin(self):
        self.ops = []

    def add(self, eng, fn, reads=(), writes=(), dma=False):
        rec = _Rec()
        fn(rec)
        name_, a_, k_ = rec.call
        fn = lambda e, name_=name_, a_=a_, k_=k_: getattr(e, name_)(*a_, **k_)
        i = len(self.ops)
        deps = set()
        for r in reads:
            r = _res(r)
            if r.lw is not None:
                deps.add(r.lw)
        for w in writes:
            w = _res(w)
            if w.lw is not None:
                deps.add(w.lw)
            deps.update(w.rd)
        deps.discard(i)
        for r in reads:
            _res(r).rd.append(i)
        for w in writes:
            w = _res(w)
            w.lw = i
            w.rd = []
        self.ops.append(dict(eng=eng, fn=fn, deps=deps, dma=dma, sig=False))
        self.nres.extend(_res(x) for x in reads)
        self.nres.extend(_res(x) for x in writes)
        return i

    def emit(self):
        nc, ops = self.nc, self.ops
        for o in ops:
            nd = set()
            for d in o["deps"]:
                p = ops[d]
                if (not p["dma"]) and (not o["dma"]) and p["eng"] == "pe" and o["eng"] == "pe":
                    continue
                nd.add(d)
            o["deps"] = nd
        for o in ops:
            for d in o["deps"]:
                ops[d]["sig"] = True
        last = {}
        for i, o in enumerate(ops):
            if not o["dma"]:
                last[o["eng"]] = i
        for i in last.values():
            ops[i]["sig"] = True
        cnt, dcnt = self.cnt, self.dcnt
        for o in ops:
            if o["dma"]:
                q = o["eng"]
                k = dcnt[q]
                dcnt[q] += 1
                o["dsem"] = (q, k % self.NDS)
                o["dval"] = 16 * (k // self.NDS + 1)
            elif o["sig"]:
                cnt[o["eng"]] += 1
                o["val"] = cnt[o["eng"]]
        csem, dsem = self.csem, self.dsem
        engobj = {"pe": nc.tensor, "act": nc.scalar, "dve": nc.vector, "pool": nc.gpsimd, "sp": nc.sync}
        fin_c = dict(cnt)
        fin_d = dict(dcnt)

        def run(eng):
            e = engobj[eng]
            seen, dseen = {}, {}
            for o in ops:
                if o["eng"] != eng:
                    continue
                need, dneed = {}, {}
                for d in o["deps"]:
                    p = ops[d]
                    if p["dma"]:
                        dneed[p["dsem"]] = max(dneed.get(p["dsem"], 0), p["dval"])
                    else:
                        need[p["eng"]] = max(need.get(p["eng"], 0), p["val"])
                if o["dma"] and o["dval"] > 16:
                    dneed[o["dsem"]] = max(dneed.get(o["dsem"], 0), o["dval"] - 16)
                for pe_, v in need.items():
                    if seen.get(pe_, 0) < v:
                        e.wait_ge(csem[pe_], v)
                        seen[pe_] = v
                for ds_, v in dneed.items():
                    if dseen.get(ds_, 0) < v:
                        e.wait_ge(dsem[ds_], v)
                        dseen[ds_] = v
                ins = o["fn"](e)
                if o["dma"]:
                    ins.then_inc(dsem[o["dsem"]], 16)
                elif o["sig"]:
                    ins.then_inc(csem[eng], 1)
            if eng == "sp":
                for q in ("sp", "act"):
                    k = fin_d[q]
                    for j in range(self.NDS):
                        n = k // self.NDS + (1 if (k % self.NDS) > j else 0)
                        if n:
                            e.wait_ge(dsem[(q, j)], 16 * n)
                for ce in ("pe", "act", "dve", "pool"):
                    if fin_c[ce]:
                        e.wait_ge(csem[ce], fin_c[ce])

        with nc.Block() as block:
            @block.sync
            def _(x):
                run("sp")

            @block.tensor
            def _(x):
                run("pe")

            @block.scalar
            def _(x):
                run("act")

            @block.vector
            def _(x):
                run("dve")

            @block.gpsimd
            def _(x):
                run("pool")
        for r in self.nres:
            r.lw = None
            r.rd = []
        self.nres = []
        self.ops = []


class KB:
    def __init__(self, cfg):
        self.cfg = cfg
        self.nc = bass.Bass("TRN2", target_bir_lowering=False)
        self.gst = contextlib.ExitStack()
        self.S = Sched(self.nc, self.gst)
        self.st = None
        self.dres = {}
        self._alt = 0
        self._psi = 0
        self._psn = 8
        self._uid = 0
        self.inputs = {}
        self.outputs = {}

    def din(self, name, shape, dt=F32):
        ap = self.nc.dram_tensor(name, list(shape), dt, kind="ExternalInput").ap()
        self.inputs[name] = ap
        return ap

    def dout(self, name, shape, dt=F32):
        ap = self.nc.dram_tensor(name, list(shape), dt, kind="ExternalOutput").ap()
        self.outputs[name] = ap
        return ap

    def dscr(self, name, shape, dt):
        if self.cfg.debug:
            return self.dout("dbg_" + name, shape, dt)
        return self.nc.dram_tensor(name, list(shape), dt).ap()

    def DR(self, name, idx=0):
        k = (name, idx)
        if k not in self.dres:
            self.dres[k] = Res("%s_%s" % (name, idx))
        return self.dres[k]

    def sb(self, name, shape, dt):
        self._uid += 1
        return TT(self.st.enter_context(self.nc.sbuf_tensor("%s_u%d" % (name, self._uid), list(shape), dt)), name)

    def sbn(self, name, n, shape, dt):
        return [self.sb("%s%d" % (name, i), shape, dt) for i in range(n)]

    @contextlib.contextmanager
    def phase(self, name):
        self.S.begin()
        self._psi = 0
        with contextlib.ExitStack() as st:
            self.st = st
            self._uid += 1
            self.PS = [TT(st.enter_context(self.nc.psum_tensor("ps%d_u%d" % (i, self._uid), [128, 512], F32)), "ps%d" % i) for i in range(8)]
            epsc = self.sb("epsc", [128, 2], F32)
            self.op("dve", lambda e: e.memset(epsc[:, 0:1], NORM_EPS), [], [epsc])
            self.op("dve", lambda e: e.memset(epsc[:, 1:2], LN_EPS), [], [epsc])
            self.epsc = epsc
            yield
            self.S.emit()
        self.st = None

    def ps(self):
        p = self.PS[self._psi % self._psn]
        self._psi += 1
        return p

    def alt(self):
        self._alt += 1
        return "act" if self._alt % 2 else "dve"

    def op(self, eng, fn, R=(), W=()):
        return self.S.add(eng, fn, R, W)

    def dma(self, out, in_, R=(), W=(), q="sp"):
        return self.S.add(q, lambda e: e.dma_start(out=out, in_=in_), R, W, dma=True)

    def copy(self, eng, out, in_, R, W, scale=None):
        if eng == "act":
            if scale is None:
                self.op("act", lambda e: e.activation(out=out, in_=in_, func=AF.Copy), R, W)
            else:
                self.op("dve", lambda e: e.tensor_scalar(out=out, in0=in_, scalar1=scale, scalar2=None, op0=ALU.mult), R, W)
        else:
            if scale is None:
                self.op(eng, lambda e: e.tensor_copy(out=out, in_=in_), R, W)
            else:
                self.op(eng, lambda e: e.tensor_scalar(out=out, in0=in_, scalar1=scale, scalar2=None, op0=ALU.mult), R, W)


def build_program(cfg):
    kb = KB(cfg)
    nc = kb.nc
    T, L, E = cfg.T, cfg.L, cfg.E
    xf = [kb.din("xf%d" % q, [T[q], D]) for q in range(2)]
    xe = [kb.din("xe%d" % q, [E[q], D]) for q in range(2)]
    ccol = kb.din("ccol", [128, KC, 2])
    ada_w = kb.din("ada_w", [2, D, 6 * D])
    ada_bcol = kb.din("ada_bcol", [128, 2, 96])
    w_in = kb.din("w_in", [D, DIN])
    w_out = kb.din("w_out", [D, D])
    pool_w = kb.din("pool_w", [4 * 512, 512])
    mlp_w1 = kb.din("mlp_w1", [2, D, DFF])
    mlp_w2 = kb.din("mlp_w2", [2, DFF, D])
    convcol = kb.din("convcol", [128, 24, 5])
    alog_row = kb.din("alog_row", [1, 16])
    dtb_row = kb.din("dtb_row", [1, 16])
    gnw_row = kb.din("gnw_row", [1, 128])
    qkw_col = kb.din("qkw_col", [128, 2])
    pscale_col = kb.din("pscale_col", [128, KC])
    lng_col = kb.din("lng_col", [128, 4, KC])
    lnb_col = kb.din("lnb_col", [128, 4, KC])
    cst = kb.din("cst", [128, 9, 128])
    ropeK = [kb.din("ropeK%d" % q, [2, 128, T[q]]) for q in range(2)]
    ropeQ = [kb.din("ropeQ%d" % q, [2, 128, E[q]]) for q in range(2)]
    onehot = kb.din("onehot", [128, 4])
    validr = [kb.din("valid%d" % q, [1, E[q]]) for q in range(2)]
    invcnt = [kb.din("invcnt%d" % q, [4, E[q]]) for q in range(2)]
    y_out = [kb.dout("y%d" % q, [L[q], D]) for q in range(2)]

    Win_b = kb.dscr("Win_b", [D, DIN], BF16) if False else nc.dram_tensor("Win_b", [D, DIN], BF16).ap()
    Wout_b = nc.dram_tensor("Wout_b", [D, D], BF16).ap()
    Pool_b = nc.dram_tensor("Pool_b", [4 * 512, 512], BF16).ap()
    W1_b = nc.dram_tensor("W1_b", [2, D, DFF], BF16).ap()
    W2_s = nc.dram_tensor("W2_s", [2, 16, 128, FC * 128], BF16).ap()
    COND = kb.dscr("COND", [128, 2, 2, 96], F32)
    PRE = [kb.dscr("PRE%d" % q, [3072, T[q]], F32) for q in range(2)]
    BGR = [kb.dscr("BGR%d" % q, [T[q], 32], F32) for q in range(2)]
    BG = [kb.dscr("BG%d" % q, [T[q], 32], F32) for q in range(2)]
    KTa = [kb.dscr("KTa%d" % q, [2, 128, T[q]], BF16) for q in range(2)]
    Va = [kb.dscr("Va%d" % q, [2, 128, T[q] // 128, 128], BF16) for q in range(2)]
    QTg = [kb.dscr("QTg%d" % q, [T[q] // 128, 128, NH, 128], BF16) for q in range(2)]
    KTg = [kb.dscr("KTg%d" % q, [T[q] // 128, 128, NH, 128], BF16) for q in range(2)]
    Ktok = [kb.dscr("Ktok%d" % q, [T[q], 1024], BF16) for q in range(2)]
    Vtok = [kb.dscr("Vtok%d" % q, [T[q], 1024], F32) for q in range(2)]
    OSUM = [kb.dscr("OSUM%d" % q, [T[q], 1024], F32) for q in range(2)]
    OSUMB = [kb.dscr("OSUMB%d" % q, [T[q], 1024], F32) for q in range(2)]
    X1T = [kb.dscr("X1T%d" % q, [D, E[q]], F32) for q in range(2)]
    stop = cfg.stop_after

    def cast_phase(items, tag):
        with kb.phase("W" + tag):
            fb = kb.sbn("wf", 3, [128, 2048], F32)
            bb = kb.sbn("wb", 3, [128, 2048], BF16)
            engs = ["act", "dve", "pool"]
            for i, (src, dstfn, C) in enumerate(items):
                f, b = fb[i % 3], bb[i % 3]
                kb.dma(f[:, 0:C], src, W=[f])
                kb.copy(engs[i % 3], b[:, 0:C], f[:, 0:C], [f], [b])
                dstfn(b)

    items = []
    for rt in range(16):
        for c0 in range(0, DIN, 1888):
            def dst(b, rt=rt, c0=c0):
                kb.dma(Win_b[rt * 128:(rt + 1) * 128, c0:c0 + 1888], b[:, 0:1888], R=[b])
            items.append((w_in[rt * 128:(rt + 1) * 128, c0:c0 + 1888], dst, 1888))
        def dst(b, rt=rt):
            kb.dma(Wout_b[rt * 128:(rt + 1) * 128, :], b[:, :], R=[b])
        items.append((w_out[rt * 128:(rt + 1) * 128, :], dst, 2048))
    for rt in range(16):
        def dst(b, rt=rt):
            kb.dma(Pool_b[rt * 128:(rt + 1) * 128, :], b[:, 0:512], R=[b])
        items.append((pool_w[rt * 128:(rt + 1) * 128, :], dst, 512))
    cast_phase(items, "a")
    for l in range(2):
        items = []
        for rt in range(16):
            for c0 in range(0, DFF, 2048):
                def dst(b, rt=rt, c0=c0, l=l):
                    kb.dma(W1_b[l, rt * 128:(rt + 1) * 128, c0:c0 + 2048], b[:, :], R=[b])
                items.append((mlp_w1[l, rt * 128:(rt + 1) * 128, c0:c0 + 2048], dst, 2048))
        for fc in range(FC):
            def dst(b, fc=fc, l=l):
                kb.dma(W2_s[l][:, :, fc * 128:(fc + 1) * 128].rearrange("o p m -> p o m"),
                       b[:, :].rearrange("p (o m) -> p o m", o=16), R=[b])
            items.append((mlp_w2[l, fc * 128:(fc + 1) * 128, :], dst, 2048))
        cast_phase(items, "m%d" % l)

    if stop == "W":
        return kb
    with kb.phase("C0"):
        cc = kb.sb("cc", [128, KC, 2], F32)
        sc = kb.sb("sc", [128, KC, 2], F32)
        ab = kb.sb("ab", [128, 2, 96], F32)
        cd = kb.sb("cd", [128, 2, 2, 96], F32)
        wg = kb.sbn("wg", 2, [128, KC, 512], F32)
        kb.dma(cc[:], ccol, W=[cc])
        kb.dma(ab[:], ada_bcol, W=[ab])
        kb.op("act", lambda e: e.activation(out=sc[:], in_=cc[:], func=AF.Silu), [cc], [sc])
        gi = 0
        for l in range(2):
            pt = kb.ps()
            for cg in range(24):
                w = wg[gi % 2]
                gi += 1
                kb.dma(w[:], ada_w[l][:, cg * 512:(cg + 1) * 512].rearrange("(kc p) n -> p kc n", p=128), W=[w])
                for jj in range(4):
                    j = cg * 4 + jj
                    for kc in range(KC):
                        kb.op("pe", lambda e, w=w, kc=kc, jj=jj, j=j, pt=pt: e.matmul(
                            pt[:, 2 * j:2 * j + 2], lhsT=w[:, kc, jj * 128:(jj + 1) * 128], rhs=sc[:, kc, :],
                            start=(kc == 0), stop=(kc == KC - 1)), [w, sc], [pt])
            kb.op("dve", lambda e, l=l, pt=pt: e.tensor_tensor(
                out=cd[:, l].rearrange("p q j -> p j q"), in0=pt[:, 0:192].rearrange("p (j q) -> p j q", q=2),
                in1=ab[:, l, :].unsqueeze(2).to_broadcast([128, 96, 2]), op=ALU.add), [pt, ab], [cd])
        kb.dma(COND, cd[:], R=[cd], W=[kb.DR("COND")])
    if stop == "C0":
        return kb

    def load_cols(l, q):
        cd = kb.sb("cdl", [128, 96], F32)
        g4 = kb.sb("lg4", [128, 4, KC], F32)
        b4 = kb.sb("lb4", [128, 4, KC], F32)
        kb.dma(cd[:], COND[:, l, q, :], W=[cd])
        kb.dma(g4[:], lng_col, W=[g4])
        kb.dma(b4[:], lnb_col, W=[b4])
        cols = {}
        cols["cd"], cols["g4"], cols["b4"] = cd, g4, b4
        o1 = kb.sb("opsc", [128, 2, KC], F32)
        kb.op("dve", lambda e: e.tensor_scalar(out=o1[:, 0, :], in0=cd[:, 16:32], scalar1=1.0, scalar2=None, op0=ALU.add), [cd], [o1])
        kb.op("dve", lambda e: e.tensor_scalar(out=o1[:, 1, :], in0=cd[:, 64:80], scalar1=1.0, scalar2=None, op0=ALU.add), [cd], [o1])
        cols["opsc"] = o1
        return cols

    def mod_cols(cols, name, which, gsrc=None, bsrc=None):
        cd, o1 = cols["cd"], cols["opsc"]
        sh = cd[:, 0:16] if which == 0 else cd[:, 48:64]
        hs = kb.sb(name + "hs", [128, KC], F32)
        hb = kb.sb(name + "hb", [128, KC], F32)
        if gsrc is None:
            kb.op("dve", lambda e: e.tensor_copy(out=hs[:], in_=o1[:, which, :]), [o1], [hs])
            kb.op("dve", lambda e: e.tensor_copy(out=hb[:], in_=sh), [cd], [hb])
        else:
            kb.op("dve", lambda e: e.tensor_tensor(out=hs[:], in0=gsrc, in1=o1[:, which, :], op=ALU.mult), [o1, cols["g4"]], [hs])
            kb.op("dve", lambda e: e.tensor_tensor(out=hb[:], in0=bsrc, in1=o1[:, which, :], op=ALU.mult), [o1, cols["b4"]], [hb])
            kb.op("dve", lambda e: e.tensor_tensor(out=hb[:], in0=hb[:], in1=sh, op=ALU.add), [hb, cd], [hb])
        return hs, hb

    def res_cols(cols, name, gsrc=None, bsrc=None):
        rs = kb.sb(name + "rs", [128, KC], F32)
        rb = kb.sb(name + "rb", [128, KC], F32)
        if gsrc is None:
            kb.op("dve", lambda e: e.memset(rs[:], ALPHA), [], [rs])
            kb.op("dve", lambda e: e.memset(rb[:], 0.0), [], [rb])
        else:
            kb.op("dve", lambda e: e.tensor_scalar(out=rs[:], in0=gsrc, scalar1=ALPHA, scalar2=None, op0=ALU.mult), [cols["g4"]], [rs])
            kb.op("dve", lambda e: e.tensor_scalar(out=rb[:], in0=bsrc, scalar1=ALPHA, scalar2=None, op0=ALU.mult), [cols["b4"]], [rb])
        return rs, rb

    def load_consts(names):
        idx = dict(ident=0, ones=1, tri_f=2, tri_b=3, madd_f=4, madd_b=5, strict_f=6, strict_b=7, rotperm=8)
        out = {}
        for n in names:
            t = kb.sb("c_" + n, [128, 128], F32)
            kb.dma(t[:], cst[:, idx[n], :], W=[t])
            out[n] = t
            tb = kb.sb("cb_" + n, [128, 128], BF16)
            kb.op("dve", lambda e, t=t, tb=tb: e.tensor_copy(out=tb[:], in_=t[:]), [t], [tb])
            out[n + "_b"] = tb
        return out

    def phase_A1(q):
        Tq = T[q]
        nb = Tq // 512
        with kb.phase("A1_%d" % q):
            import os
            cols = load_cols(0, q)
            hs, hb = mod_cols(cols, "a1", 0)
            C = load_consts(["ident", "ones", "rotperm"])
            qk = kb.sb("qkw", [128, 2], F32)
            kb.dma(qk[:], qkw_col, W=[qk])
            xt = [kb.sbn("xt%d_" % i, 4, [128, D], F32) for i in range(1)] * 2
            hT = kb.sb("hT", [128, KC, 512], BF16)
            wgb = kb.sbn("wg", 2, [128, KC, 1024], BF16)
            wsm = kb.sb("wsm", [128, KC, 640], BF16)
            stg = kb.sbn("stg", 4, [128, 512], F32)
            cs = kb.sbn("cs", 2, [128, 2, 512], F32)
            sq = kb.sbn("sq", 2, [128, 1024], BF16)
            xw = kb.sbn("xw", 2, [128, 512], F32)
            k1 = kb.sbn("k1", 2, [128, 512], F32)
            k2 = kb.sbn("k2", 2, [128, 512], F32)
            rs_ = kb.sbn("rstd", 2, [128, 512], F32)
            kbf = kb.sbn("kbf", 2, [128, 512], BF16)
            vst = kb.sbn("vst", 2, [128, 256], BF16)
            sst = kb.sbn("sst", 2, [128, 32], F32)
            kb.dma(wsm[:, :, 512:544], Win_b[:, C_B:C_B + 32].rearrange("(kc p) n -> p kc n", p=128), W=[wsm])
            kb.dma(wsm[:, :, 0:512], Win_b[:, C_AK:C_AK + 512].rearrange("(kc p) n -> p kc n", p=128), W=[wsm])
            wi = 0
            si = 0
            for b in range(nb):
                t0 = b * 512
                xb = xt[b % 2]
                for ti in range(4):
                    kb.dma(xb[ti][:], xf[q][t0 + ti * 128:t0 + (ti + 1) * 128, :], W=[xb[ti]])
                kb.dma(cs[b % 2][:], ropeK[q][:, :, t0:t0 + 512].rearrange("c p t -> p c t"), W=[cs[b % 2]])
                for kc in range(KC):
                    pt = kb.ps()
                    for ti in range(4):
                        kb.op("pe", lambda e, pt=pt, ti=ti, kc=kc, xb=xb: e.transpose(
                            out=pt[:, ti * 128:(ti + 1) * 128], in_=xb[ti][:, kc * 128:(kc + 1) * 128], identity=C["ident"][:]),
                            [xb[ti], C["ident"]], [pt])
                    if True:
                        kb.op("dve", lambda e, pt=pt, kc=kc: e.tensor_scalar(out=hT[:, kc, :], in0=pt[:], scalar1=hs[:, kc:kc + 1], scalar2=hb[:, kc:kc + 1],
                                                                            op0=ALU.mult, op1=ALU.add), [pt, hs, hb], [hT])
                    else:
                        kb.op("act", lambda e, pt=pt, kc=kc: e.activation(out=hT[:, kc, :], in_=pt[:], func=AF.Identity,
                                                                         scale=hs[:, kc:kc + 1], bias=hb[:, kc:kc + 1]), [pt, hs, hb], [hT])
                parts = os.environ.get("A1PARTS", "123")
                for g3 in (range(3) if "1" in parts else []):
                    w = wgb[wi % 2]
                    wi += 1
                    kb.dma(w[:], Win_b[:, g3 * 1024:(g3 + 1) * 1024].rearrange("(kc p) n -> p kc n", p=128), W=[w])
                    for jj in range(8):
                        j = g3 * 8 + jj
                        pt = kb.ps()
                        for kc in range(KC):
                            kb.op("pe", lambda e, pt=pt, w=w, kc=kc, jj=jj: e.matmul(
                                pt[:], lhsT=w[:, kc, jj * 128:(jj + 1) * 128], rhs=hT[:, kc, :], start=(kc == 0), stop=(kc == KC - 1)),
                                [w, hT], [pt])
                        s = stg[si % 4]
                        si += 1
                        kb.copy(kb.alt(), s[:], pt[:], [pt], [s])
                        kb.dma(PRE[q][j * 128:(j + 1) * 128, t0:t0 + 512], s[:], R=[s], W=[kb.DR("PRE", (j, b))])
                for ti in (range(4) if "2" in parts else []):
                    pt = kb.ps()
                    for kc in range(KC):
                        kb.op("pe", lambda e, pt=pt, kc=kc, ti=ti: e.matmul(
                            pt[:, 0:32], lhsT=hT[:, kc, ti * 128:(ti + 1) * 128], rhs=wsm[:, kc, 512:544], start=(kc == 0), stop=(kc == KC - 1)),
                            [wsm, hT], [pt])
                    pt2 = kb.ps()
                    for kc in range(KC):
                        kb.op("pe", lambda e, pt2=pt2, kc=kc, ti=ti: e.matmul(
                            pt2[:, 0:256], lhsT=hT[:, kc, ti * 128:(ti + 1) * 128], rhs=wsm[:, kc, 256:512], start=(kc == 0), stop=(kc == KC - 1)),
                            [wsm, hT], [pt2])
                    tg = b * 4 + ti
                    vs, ss = vst[tg % 2], sst[tg % 2]
                    kb.copy("dve", ss[:], pt[:, 0:32], [pt], [ss])
                    kb.copy("act", vs[:], pt2[:, 0:256], [pt2], [vs])
                    kb.dma(BGR[q][tg * 128:(tg + 1) * 128, :], ss[:], R=[ss], W=[kb.DR("BGR", tg)])
                    kb.dma(Va[q][:, :, tg, :].rearrange("k p d -> p k d"), vs[:].rearrange("p (k d) -> p k d", k=2), R=[vs], W=[kb.DR("Va", tg)])
                for kv in (range(2) if "3" in parts else []):
                    i2 = (b * 2 + kv) % 2
                    pt = kb.ps()
                    for kc in range(KC):
                        kb.op("pe", lambda e, pt=pt, kc=kc, kv=kv: e.matmul(
                            pt[:], lhsT=wsm[:, kc, kv * 128:(kv + 1) * 128], rhs=hT[:, kc, :], start=(kc == 0), stop=(kc == KC - 1)),
                            [wsm, hT], [pt])
                    rope_norm(kb, C, pt, 512, qk, 1, cs[b % 2], sq[i2], xw[i2], k1[i2], k2[i2], rs_[i2], kbf[i2], 1.0)
                    kst = os.environ.get("KST", "sp")
                    if kst != "none":
                        kb.dma(KTa[q][kv, :, t0:t0 + 512], kbf[i2][:], R=[kbf[i2]], W=[kb.DR("KTa", (kv, b))], q=kst)

    def rope_norm(kb, C, pt, nt, wT, wi, cs, sq, xw, k1, k2, rstd, outb, outscale):
        kb.op("act", lambda e: e.activation(out=k2[:, 0:nt], in_=pt[:, 0:nt], func=AF.Copy), [pt], [k2])
        kb.op("dve", lambda e: e.tensor_tensor(out=sq[:, 0:nt], in0=k2[:, 0:nt], in1=k2[:, 0:nt], op=ALU.mult), [k2], [sq])
        kb.op("dve", lambda e: e.tensor_scalar(out=xw[:, 0:nt], in0=k2[:, 0:nt], scalar1=wT[:, wi:wi + 1], scalar2=None, op0=ALU.mult), [k2, wT], [xw])
        kb.op("dve", lambda e: e.tensor_copy(out=sq[:, 512:512 + nt], in_=xw[:, 0:nt]), [xw], [sq])
        p2 = kb.ps()
        kb.op("pe", lambda e: e.matmul(p2[:, 0:nt], lhsT=C["ones_b"][:], rhs=sq[:, 0:nt], start=True, stop=True), [C["ones_b"], sq], [p2])
        kb.op("dve", lambda e: e.tensor_scalar(out=rstd[:, 0:nt], in0=p2[:, 0:nt], scalar1=1.0 / 128, scalar2=NORM_EPS, op0=ALU.mult, op1=ALU.add), [p2], [rstd])
        kb.op("act", lambda e: e.activation(out=rstd[:, 0:nt], in_=rstd[:, 0:nt], func=AF.Sqrt), [rstd], [rstd])
        kb.op("dve", lambda e: e.reciprocal(out=rstd[:, 0:nt], in_=rstd[:, 0:nt]), [rstd], [rstd])
        p3 = kb.ps()
        kb.op("pe", lambda e: e.matmul(p3[:, 0:nt], lhsT=C["rotperm_b"][:], rhs=sq[:, 512:512 + nt], start=True, stop=True), [C["rotperm_b"], sq], [p3])
        kb.op("dve", lambda e: e.tensor_tensor(out=k1[:, 0:nt], in0=xw[:, 0:nt], in1=cs[:, 0, 0:nt], op=ALU.mult), [xw, cs], [k1])
        kb.op("dve", lambda e: e.tensor_tensor(out=k2[:, 0:nt], in0=p3[:, 0:nt], in1=cs[:, 1, 0:nt], op=ALU.mult), [p3, cs], [k2])
        kb.op("dve", lambda e: e.tensor_tensor(out=k1[:, 0:nt], in0=k1[:, 0:nt], in1=k2[:, 0:nt], op=ALU.add), [k1, k2], [k1])
        if outscale != 1.0:
            kb.op("dve", lambda e: e.scalar_tensor_tensor(out=outb[:, 0:nt], in0=k1[:, 0:nt], scalar=float(outscale), in1=rstd[:, 0:nt], op0=ALU.mult, op1=ALU.mult), [k1, rstd], [outb])
        else:
            kb.op("dve", lambda e: e.tensor_tensor(out=outb[:, 0:nt], in0=k1[:, 0:nt], in1=rstd[:, 0:nt], op=ALU.mult), [k1, rstd], [outb])

    for q in range(2):
        phase_A1(q)
    if stop == "A1":
        return kb

    def phase_A2(q):
        Tq = T[q]
        nb = Tq // 512
        with kb.phase("A2_%d" % q):
            C = load_consts(["ident", "ones"])
            cw = kb.sb("cw", [128, 24, 5], F32)
            kb.dma(cw[:], convcol, W=[cw])
            alr = kb.sb("alr", [128, 16], F32)
            dtr = kb.sb("dtr", [128, 16], F32)
            kb.dma(alr[:], alog_row.partition_broadcast(128), W=[alr])
            kb.dma(dtr[:], dtb_row.partition_broadcast(128), W=[dtr])
            nea = kb.sb("nea", [128, 16], F32)
            kb.op("act", lambda e: e.activation(out=nea[:], in_=alr[:], func=AF.Exp), [alr], [nea])
            kb.op("dve", lambda e: e.tensor_scalar(out=nea[:], in0=nea[:], scalar1=-1.0, scalar2=None, op0=ALU.mult), [nea], [nea])
            pre = kb.sbn("pre", 3, [128, 516], F32)
            acc = kb.sbn("acc", 2, [128, 512], F32)
            sg = kb.sbn("sg", 2, [128, 512], F32)
            sqb = kb.sbn("sqb", 2, [128, 512], BF16)
            rst = kb.sbn("rst", 2, [128, 512], F32)
            qst = kb.sb("qst", [128, NH, 512], BF16)
            kst = kb.sb("kst", [128, NH, 512], BF16)
            vbf = kb.sbn("vbf", 2, [128, 512], F32)
            ktok = kb.sbn("ktok", 4, [128, 1024], BF16)
            vtok = kb.sbn("vtok", 4, [128, 1024], F32)
            br = kb.sbn("br", 2, [128, 32], F32)
            bo = kb.sbn("bo", 2, [128, 32], F32)
            t1 = kb.sbn("t1_", 2, [128, 16], F32)
            t2 = kb.sbn("t2_", 2, [128, 16], F32)
            for tg in range(Tq // 128):
                b_, o_, a1, a2 = br[tg % 2], bo[tg % 2], t1[tg % 2], t2[tg % 2]
                kb.dma(b_[:], BGR[q][tg * 128:(tg + 1) * 128, :], W=[b_])
                kb.op("act", lambda e, b_=b_, o_=o_: e.activation(out=o_[:, 0:16], in_=b_[:, 0:16], func=AF.Exp, scale=-1.0), [b_], [o_])
                kb.op("dve", lambda e, o_=o_: e.tensor_scalar(out=o_[:, 0:16], in0=o_[:, 0:16], scalar1=1.0, scalar2=None, op0=ALU.add), [o_], [o_])
                kb.op("dve", lambda e, o_=o_: e.reciprocal(out=o_[:, 0:16], in_=o_[:, 0:16]), [o_], [o_])
                kb.op("dve", lambda e, b_=b_, a1=a1: e.tensor_tensor(out=a1[:], in0=b_[:, 16:32], in1=dtr[:], op=ALU.add), [b_, dtr], [a1])
                kb.op("dve", lambda e, a1=a1, a2=a2: e.tensor_scalar(out=a2[:], in0=a1[:], scalar1=-1.0, scalar2=None, op0=ALU.mult), [a1], [a2])
                kb.op("dve", lambda e, a1=a1, a2=a2: e.tensor_tensor(out=a2[:], in0=a2[:], in1=a1[:], op=ALU.max), [a1, a2], [a2])
                kb.op("act", lambda e, a2=a2: e.activation(out=a2[:], in_=a2[:], func=AF.Exp, scale=-1.0), [a2], [a2])
                kb.op("dve", lambda e, a2=a2: e.tensor_scalar(out=a2[:], in0=a2[:], scalar1=1.0, scalar2=None, op0=ALU.add), [a2], [a2])
                kb.op("act", lambda e, a2=a2: e.activation(out=a2[:], in_=a2[:], func=AF.Ln), [a2], [a2])
                kb.op("dve", lambda e, a1=a1: e.tensor_scalar(out=a1[:], in0=a1[:], scalar1=0.0, scalar2=None, op0=ALU.max), [a1], [a1])
                kb.op("dve", lambda e, a1=a1, a2=a2: e.tensor_tensor(out=a1[:], in0=a1[:], in1=a2[:], op=ALU.add), [a1, a2], [a1])
                kb.op("dve", lambda e, a1=a1, o_=o_: e.tensor_tensor(out=o_[:, 16:32], in0=a1[:], in1=nea[:], op=ALU.mult), [a1, nea], [o_])
                kb.dma(BG[q][tg * 128:(tg + 1) * 128, :], o_[:], R=[o_], W=[kb.DR("BG", tg)])
            pi = 0
            for b in range(nb):
                t0 = b * 512
                for j in range(24):
                    p_ = pre[pi % 3]
                    pi += 1
                    lo = 2 if b == 0 else 0
                    hi = 514 if b == nb - 1 else 516
                    if b == 0:
                        kb.op("dve", lambda e, p_=p_: e.memset(p_[:, 0:2], 0.0), [], [p_])
                    if b == nb - 1:
                        kb.op("dve", lambda e, p_=p_: e.memset(p_[:, 514:516], 0.0), [], [p_])
                    kb.dma(p_[:, lo:hi], PRE[q][j * 128:(j + 1) * 128, t0 - 2 + lo:t0 - 2 + hi], W=[p_])
                    a_ = acc[j % 2]
                    kb.op("dve", lambda e, p_=p_, a_=a_, j=j: e.tensor_scalar(out=a_[:], in0=p_[:, 0:512], scalar1=cw[:, j, 0:1], scalar2=None, op0=ALU.mult), [p_, cw], [a_])
                    for jj in range(1, 5):
                        kb.op("dve", lambda e, p_=p_, a_=a_, j=j, jj=jj: e.scalar_tensor_tensor(
                            out=a_[:], in0=p_[:, jj:jj + 512], scalar=cw[:, j, jj:jj + 1], in1=a_[:], op0=ALU.mult, op1=ALU.add), [p_, cw, a_], [a_])
                    s_ = sg[j % 2]
                    kb.op("act", lambda e, a_=a_, s_=s_: e.activation(out=s_[:], in_=a_[:], func=AF.Silu), [a_], [s_])
                    if j < 16:
                        h = j % 8
                        q_b, r_ = sqb[j % 2], rst[j % 2]
                        kb.op("dve", lambda e, s_=s_, q_b=q_b: e.tensor_tensor(out=q_b[:], in0=s_[:], in1=s_[:], op=ALU.mult), [s_], [q_b])
                        p2 = kb.ps()
                        kb.op("pe", lambda e, p2=p2, q_b=q_b: e.matmul(p2[:], lhsT=C["ones_b"][:], rhs=q_b[:], start=True, stop=True), [C["ones_b"], q_b], [p2])
                        kb.op("dve", lambda e, p2=p2, r_=r_: e.tensor_scalar(out=r_[:], in0=p2[:], scalar1=NORM_EPS, scalar2=None, op0=ALU.add), [p2], [r_])
                        kb.op("act", lambda e, r_=r_: e.activation(out=r_[:], in_=r_[:], func=AF.Sqrt), [r_], [r_])
                        kb.op("dve", lambda e, r_=r_: e.reciprocal(out=r_[:], in_=r_[:]), [r_], [r_])
                        dst = qst if j < 8 else kst
                        sc_ = (128 ** -0.5) if j < 8 else 1.0
                        kb.op("dve", lambda e, s_=s_, r_=r_, dst=dst, h=h, sc_=sc_: e.scalar_tensor_tensor(
                            out=dst[:, h, :], in0=s_[:], scalar=float(sc_), in1=r_[:], op0=ALU.mult, op1=ALU.mult), [s_, r_], [dst])
                        if j >= 8:
                            for ti in range(4):
                                pb = kb.ps()
                                pbb = pb[:].bitcast(BF16)
                                kb.op("pe", lambda e, pbb=pbb, ti=ti, h=h: e.transpose(out=pbb[:, 0:128], in_=kst[:, h, ti * 128:(ti + 1) * 128], identity=C["ident_b"][:]),
                                      [kst, C["ident_b"]], [pb])
                                kb.op("dve", lambda e, pbb=pbb, ti=ti, h=h: e.tensor_copy(out=ktok[ti][:, h * 128:(h + 1) * 128], in_=pbb[:, 0:128]), [pb], [ktok[ti]])
                    else:
                        h = j - 16
                        for ti in range(4):
                            pb = kb.ps()
                            kb.op("pe", lambda e, pb=pb, ti=ti, s_=s_: e.transpose(out=pb[:, 0:128], in_=s_[:, ti * 128:(ti + 1) * 128], identity=C["ident"][:]),
                                  [s_, C["ident"]], [pb])
                            kb.op("act", lambda e, pb=pb, ti=ti, h=h: e.activation(out=vtok[ti][:, h * 128:(h + 1) * 128], in_=pb[:, 0:128], func=AF.Copy), [pb], [vtok[ti]])
                for ti in range(4):
                    tg = b * 4 + ti
                    kb.dma(QTg[q][tg], qst[:, :, ti * 128:(ti + 1) * 128], R=[qst], W=[kb.DR("QTg", tg)])
                    kb.dma(KTg[q][tg], kst[:, :, ti * 128:(ti + 1) * 128], R=[kst], W=[kb.DR("KTg", tg)])
                    kb.dma(Ktok[q][tg * 128:(tg + 1) * 128, :], ktok[ti][:], R=[ktok[ti]], W=[kb.DR("Ktok", tg)])
                    kb.dma(Vtok[q][tg * 128:(tg + 1) * 128, :], vtok[ti][:], R=[vtok[ti]], W=[kb.DR("Vtok", tg)])

    for q in range(2):
        phase_A2(q)
    if stop == "A2":
        return kb

    def phase_G(q):
        Tq = T[q]
        ntile = Tq // 128
        with kb.phase("G_%d" % q):
          CC = load_consts(["ident", "ones", "tri_f", "madd_f", "strict_f", "tri_b", "madd_b", "strict_b"])
          streams = []
          for d in range(2):
            sfx = "_f" if d == 0 else "_b"
            C = CC
            TRI, MADD, STRICT = C["tri" + sfx + "_b"], C["madd" + sfx], C["strict" + sfx]
            IDB, ONB = C["ident_b"], C["ones_b"]
            qT = kb.sbn("gqT%d_" % d, 2, [128, NH, 128], BF16)
            kT = kb.sbn("gkT%d_" % d, 2, [128, NH, 128], BF16)
            kt_ = kb.sbn("gkt%d_" % d, 2, [128, 1024], BF16)
            vt_ = kb.sbn("gvt%d_" % d, 2, [128, 1024], F32)
            bg = kb.sbn("gbg%d_" % d, 2, [128, 32], F32)
            osl = kb.sbn("gos%d_" % d, 2, [128, 1024], F32)
            ost = kb.sbn("gost%d_" % d, 2, [128, 1024], F32)
            S32 = kb.sb("S32_%d" % d, [128, NH, 128], F32)
            Sb = kb.sb("Sb_%d" % d, [128, NH, 128], BF16)
            kb.op("dve", lambda e: e.memset(S32[:], 0.0), [], [S32])
            kb.op("dve", lambda e: e.memset(Sb[:], 0.0), [], [Sb])
            ghl = kb.sbn("ghl%d_" % d, 2, [128, 16], BF16)
            gtmp = kb.sbn("gtmp%d_" % d, 2, [128, 8], F32)
            pcsb = kb.sbn("pcsb%d_" % d, 2, [128, 32], F32)
            sc3 = kb.sbn("sc3%d_" % d, 2, [128, 24], F32)
            ex3 = kb.sbn("ex3%d_" % d, 2, [128, 24], F32)
            neg = kb.sbn("neg%d_" % d, 2, [128, 16], F32)
            decI = kb.sbn("decI%d_" % d, 2, [128, 512], F32)
            decS = kb.sbn("decS%d_" % d, 2, [128, 512], F32)
            Dm = kb.sbn("Dm%d_" % d, 2, [128, 512], F32)
            M = kb.sbn("M%d_" % d, 2, [128, 512], BF16)
            MT = kb.sbn("MT%d_" % d, 2, [128, 512], BF16)
            Y = kb.sbn("Y%d_" % d, 2, [128, 512], BF16)
            Wb = kb.sbn("Wb%d_" % d, 2, [128, 512], BF16)
            vn = kb.sbn("vn%d_" % d, 2, [128, 512], BF16)
            ktl = kb.sbn("ktl%d_" % d, 2, [128, 512], BF16)
            PTm = kb.sbn("PTm%d_" % d, 2, [128, 512], BF16)
            o1s = kb.sbn("o1s%d_" % d, 2, [128, 512], F32)
            def step(it, tg, d=d, sfx=sfx, TRI=TRI, MADD=MADD, STRICT=STRICT, IDB=IDB, ONB=ONB, qT=qT, kT=kT, kt_=kt_, vt_=vt_, bg=bg, osl=osl, ost=ost, S32=S32, Sb=Sb, ghl=ghl, gtmp=gtmp, pcsb=pcsb, sc3=sc3, ex3=ex3, neg=neg, decI=decI, decS=decS, Dm=Dm, M=M, MT=MT, Y=Y, Wb=Wb, vn=vn, ktl=ktl, PTm=PTm, o1s=o1s):
                i2 = it % 2
                kb.dma(qT[i2][:], QTg[q][tg], W=[qT[i2]])
                kb.dma(kT[i2][:], KTg[q][tg], W=[kT[i2]])
                kb.dma(kt_[i2][:], Ktok[q][tg * 128:(tg + 1) * 128, :], W=[kt_[i2]])
                kb.dma(vt_[i2][:], Vtok[q][tg * 128:(tg + 1) * 128, :], W=[vt_[i2]])
                kb.dma(bg[i2][:], BG[q][tg * 128:(tg + 1) * 128, :], W=[bg[i2]])
                gcol = bg[i2][:, 16 + d * 8:16 + d * 8 + 8]
                bcol = bg[i2][:, d * 8:d * 8 + 8]
                gh, gt_, s3, e3, ng = ghl[i2], gtmp[i2], sc3[i2], ex3[i2], neg[i2]
                kb.op("dve", lambda e, gh=gh, gcol=gcol: e.tensor_copy(out=gh[:, 0:8], in_=gcol), [bg[i2]], [gh])
                kb.op("dve", lambda e, gh=gh, gt_=gt_: e.tensor_copy(out=gt_[:], in_=gh[:, 0:8]), [gh], [gt_])
                kb.op("dve", lambda e, gh=gh, gt_=gt_, gcol=gcol: e.tensor_tensor(out=gh[:, 8:16], in0=gcol, in1=gt_[:], op=ALU.subtract), [bg[i2], gt_], [gh])
                pc = kb.ps()
                kb.op("pe", lambda e, pc=pc, gh=gh: e.matmul(pc[:, 0:16], lhsT=TRI[:], rhs=gh[:], start=True, stop=True), [TRI, gh], [pc])
                kb.op("pe", lambda e, pc=pc, gh=gh: e.matmul(pc[:, 16:32], lhsT=ONB[:], rhs=gh[:], start=True, stop=True), [ONB, gh], [pc])
                pcs = pcsb[i2]
                kb.op("dve", lambda e: e.tensor_copy(out=pcs[:], in_=pc[:, 0:32]), [pc], [pcs])
                kb.op("dve", lambda e: e.tensor_tensor(out=s3[:, 0:8], in0=pcs[:, 0:8], in1=pcs[:, 8:16], op=ALU.add), [pcs], [s3])
                kb.op("dve", lambda e: e.tensor_tensor(out=s3[:, 8:16], in0=pcs[:, 16:24], in1=pcs[:, 24:32], op=ALU.add), [pcs], [s3])
                kb.op("dve", lambda e, s3=s3: e.tensor_tensor(out=s3[:, 16:24], in0=s3[:, 8:16], in1=s3[:, 0:8], op=ALU.subtract), [s3], [s3])
                kb.op("act", lambda e, s3=s3, e3=e3: e.activation(out=e3[:], in_=s3[:], func=AF.Exp), [s3], [e3])
                kb.op("dve", lambda e, ng=ng, bcol=bcol: e.tensor_scalar(out=ng[:, 0:8], in0=bcol, scalar1=-1.0, scalar2=None, op0=ALU.mult), [bg[i2]], [ng])
                kb.op("dve", lambda e, ng=ng, e3=e3: e.tensor_scalar(out=ng[:, 8:16], in0=e3[:, 0:8], scalar1=-1.0, scalar2=None, op0=ALU.mult), [e3], [ng])
                for grp in range(2):
                    gi_ = (it * 2 + grp) % 2
                    dI, dS, D_, M_, MT_, Y_, W_, vn_, ktl_, PT_, o1_ = decI[gi_], decS[gi_], Dm[gi_], M[gi_], MT[gi_], Y[gi_], Wb[gi_], vn[gi_], ktl[gi_], PTm[gi_], o1s[gi_]
                    hs_ = [grp * 4 + k for k in range(4)]
                    pg = kb.ps()
                    for k, h in enumerate(hs_):
                        for part in range(2):
                            kb.op("pe", lambda e, pg=pg, k=k, h=h, part=part, gh=gh: e.matmul(
                                pg[:, k * 128:(k + 1) * 128], lhsT=gh[:, part * 8 + h:part * 8 + h + 1].to_broadcast([128, 128]), rhs=TRI[:],
                                start=(part == 0), stop=(part == 1)), [gh, TRI], [pg])
                    for k, h in enumerate(hs_):
                        kb.op("dve", lambda e, pg=pg, k=k, h=h, D_=D_, s3=s3: e.scalar_tensor_tensor(
                            out=D_[:, k * 128:(k + 1) * 128], in0=pg[:, k * 128:(k + 1) * 128], scalar=s3[:, h:h + 1], in1=MADD[:],
                            op0=ALU.subtract, op1=ALU.add), [pg, s3, MADD], [D_])
                    kb.op("act", lambda e, D_=D_, dI=dI: e.activation(out=dI[:], in_=D_[:], func=AF.Exp), [D_], [dI])
                    kb.op("dve", lambda e, dI=dI, dS=dS: e.tensor_tensor(out=dS[:].rearrange("p (k c) -> p k c", k=4), in0=dI[:].rearrange("p (k c) -> p k c", k=4),
                                                                        in1=STRICT[:].unsqueeze(1).to_broadcast([128, 4, 128]), op=ALU.mult), [dI, STRICT], [dS])
                    pk = kb.ps()
                    for k, h in enumerate(hs_):
                        kb.op("pe", lambda e, pk=pk, k=k, h=h: e.matmul(pk[:, k * 128:(k + 1) * 128], lhsT=kT[i2][:, h, :], rhs=kT[i2][:, h, :], start=True, stop=True), [kT[i2]], [pk])
                    for k, h in enumerate(hs_):
                        kb.op("dve", lambda e, pk=pk, k=k, h=h, M_=M_, dS=dS, ng=ng: e.scalar_tensor_tensor(
                            out=M_[:, k * 128:(k + 1) * 128], in0=pk[:, k * 128:(k + 1) * 128], scalar=ng[:, h:h + 1], in1=dS[:, k * 128:(k + 1) * 128],
                            op0=ALU.mult, op1=ALU.mult), [pk, ng, dS], [M_])
                    pm = kb.ps()
                    pmb = pm[:].bitcast(BF16)
                    for k in range(4):
                        kb.op("pe", lambda e, pmb=pmb, k=k, M_=M_: e.transpose(out=pmb[:, k * 128:(k + 1) * 128], in_=M_[:, k * 128:(k + 1) * 128], identity=IDB[:]), [M_, IDB], [pm])
                    kb.op("act", lambda e, pmb=pmb, MT_=MT_: e.activation(out=MT_[:], in_=pmb[:, 0:512], func=AF.Copy), [pm], [MT_])
                    kb.op("dve", lambda e, M_=M_, Y_=Y_: e.tensor_tensor(out=Y_[:].rearrange("p (k c) -> p k c", k=4), in0=M_[:].rearrange("p (k c) -> p k c", k=4),
                                                                        in1=IDB[:].unsqueeze(1).to_broadcast([128, 4, 128]), op=ALU.add), [M_, IDB], [Y_])
                    for step in range(6):
                        pa, pbk, py = kb.ps(), kb.ps(), kb.ps()
                        lastst = step == 5
                        for k in range(4):
                            sl = slice(k * 128, (k + 1) * 128)
                            kb.op("pe", lambda e, pbk=pbk, sl=sl, M_=M_, MT_=MT_: e.matmul(pbk[:, sl], lhsT=M_[:, sl], rhs=MT_[:, sl], start=True, stop=True), [M_, MT_], [pbk])
                            if not lastst:
                                kb.op("pe", lambda e, pa=pa, sl=sl, M_=M_, MT_=MT_: e.matmul(pa[:, sl], lhsT=MT_[:, sl], rhs=M_[:, sl], start=True, stop=True), [M_, MT_], [pa])
                        kb.op("act", lambda e, pbk=pbk, MT_=MT_: e.activation(out=MT_[:], in_=pbk[:], func=AF.Copy), [pbk], [MT_])
                        if not lastst:
                            kb.op("dve", lambda e, pa=pa, M_=M_: e.tensor_copy(out=M_[:], in_=pa[:]), [pa], [M_])
                        for k in range(4):
                            sl = slice(k * 128, (k + 1) * 128)
                            kb.op("pe", lambda e, py=py, sl=sl, Y_=Y_: e.matmul(py[:, sl], lhsT=IDB[:], rhs=Y_[:, sl], start=True, stop=False, skip_group_check=True), [IDB, Y_], [py])
                            kb.op("pe", lambda e, py=py, sl=sl, Y_=Y_, MT_=MT_: e.matmul(py[:, sl], lhsT=MT_[:, sl], rhs=Y_[:, sl], start=False, stop=True, skip_group_check=True), [MT_, Y_], [py])
                        kb.op("dve", lambda e, py=py, Y_=Y_: e.tensor_copy(out=Y_[:], in_=py[:]), [py], [Y_])
                    pks = kb.ps()
                    po1 = kb.ps()
                    ppt = kb.ps()
                    for k, h in enumerate(hs_):
                        sl = slice(k * 128, (k + 1) * 128)
                        kb.op("pe", lambda e, pks=pks, sl=sl, h=h: e.matmul(pks[:, sl], lhsT=kT[i2][:, h, :], rhs=Sb[:, h, :], start=True, stop=True), [kT[i2], Sb], [pks])
                        kb.op("pe", lambda e, po1=po1, sl=sl, h=h: e.matmul(po1[:, sl], lhsT=qT[i2][:, h, :], rhs=Sb[:, h, :], start=True, stop=True), [qT[i2], Sb], [po1])
                        kb.op("pe", lambda e, ppt=ppt, sl=sl, h=h: e.matmul(ppt[:, sl], lhsT=kT[i2][:, h, :], rhs=qT[i2][:, h, :], start=True, stop=True), [kT[i2], qT[i2]], [ppt])
                    for k, h in enumerate(hs_):
                        sl = slice(k * 128, (k + 1) * 128)
                        kb.op("dve", lambda e, pks=pks, sl=sl, h=h, W_=W_, ng=ng: e.scalar_tensor_tensor(
                            out=W_[:, sl], in0=pks[:, sl], scalar=ng[:, 8 + h:9 + h], in1=vt_[i2][:, h * 128:(h + 1) * 128], op0=ALU.mult, op1=ALU.add), [pks, ng, vt_[i2]], [W_])
                        kb.op("dve", lambda e, po1=po1, sl=sl, h=h, o1_=o1_, e3=e3: e.tensor_scalar(
                            out=o1_[:, sl], in0=po1[:, sl], scalar1=e3[:, h:h + 1], scalar2=None, op0=ALU.mult), [po1, e3], [o1_])
                        kb.op("dve", lambda e, sl=sl, h=h, ktl_=ktl_, e3=e3: e.tensor_scalar(
                            out=ktl_[:, sl], in0=kt_[i2][:, h * 128:(h + 1) * 128], scalar1=e3[:, 16 + h:17 + h], scalar2=None, op0=ALU.mult), [kt_[i2], e3], [ktl_])
                    kb.op("dve", lambda e, ppt=ppt, PT_=PT_, dI=dI: e.tensor_tensor(out=PT_[:], in0=ppt[:], in1=dI[:], op=ALU.mult), [ppt, dI], [PT_])
                    pv = kb.ps()
                    for k, h in enumerate(hs_):
                        sl = slice(k * 128, (k + 1) * 128)
                        kb.op("pe", lambda e, pv=pv, sl=sl, Y_=Y_, W_=W_: e.matmul(pv[:, sl], lhsT=Y_[:, sl], rhs=W_[:, sl], start=True, stop=True), [Y_, W_], [pv])
                    for k, h in enumerate(hs_):
                        sl = slice(k * 128, (k + 1) * 128)
                        kb.op("dve", lambda e, pv=pv, sl=sl, h=h, vn_=vn_: e.tensor_scalar(
                            out=vn_[:, sl], in0=pv[:, sl], scalar1=bg[i2][:, d * 8 + h:d * 8 + h + 1], scalar2=None, op0=ALU.mult), [pv, bg[i2]], [vn_])
                    po2 = kb.ps()
                    pds = kb.ps()
                    for k, h in enumerate(hs_):
                        sl = slice(k * 128, (k + 1) * 128)
                        kb.op("pe", lambda e, po2=po2, sl=sl, PT_=PT_, vn_=vn_: e.matmul(po2[:, sl], lhsT=PT_[:, sl], rhs=vn_[:, sl], start=True, stop=True), [PT_, vn_], [po2])
                        kb.op("pe", lambda e, pds=pds, sl=sl, ktl_=ktl_, vn_=vn_: e.matmul(pds[:, sl], lhsT=ktl_[:, sl], rhs=vn_[:, sl], start=True, stop=True), [ktl_, vn_], [pds])
                    os_ = ost[i2]
                    kb.op("dve", lambda e, po2=po2, o1_=o1_, os_=os_, grp=grp: e.tensor_tensor(out=os_[:, grp * 512:(grp + 1) * 512], in0=po2[:], in1=o1_[:], op=ALU.add), [po2, o1_], [os_])
                    for k, h in enumerate(hs_):
                        sl = slice(k * 128, (k + 1) * 128)
                        kb.op("dve", lambda e, pds=pds, sl=sl, h=h, e3=e3: e.scalar_tensor_tensor(
                            out=S32[:, h, :], in0=S32[:, h, :], scalar=e3[:, 8 + h:9 + h], in1=pds[:, sl], op0=ALU.mult, op1=ALU.add), [S32, e3, pds], [S32])
                    kb.op("act", lambda e, grp=grp: e.activation(out=Sb[:, grp * 4:(grp + 1) * 4, :], in_=S32[:, grp * 4:(grp + 1) * 4, :], func=AF.Copy), [S32], [Sb])
                kb.dma((OSUM if d == 0 else OSUMB)[q][tg * 128:(tg + 1) * 128, :], ost[i2][:], R=[ost[i2]], W=[kb.DR("OSUM%d" % d, tg)])

            streams.append(step)
          for it in range(ntile):
            for d in range(2):
              streams[d](it, it if d == 0 else ntile - 1 - it)

    for q in range(2):
        phase_G(q)
    if stop == "G":
        return kb

    XRES = [kb.dscr("XRES%d" % q, [D, 512], F32) for q in range(2)]
    OTS = [kb.dscr("OTS%d" % q, [D, 512], BF16) for q in range(2)]

    def layernorm(XT, nt, C):
        xb = kb.sbn("lnxb", 2, [128, 512], BF16)
        sb_ = kb.sbn("lnsq", 2, [128, 512], BF16)
        mean = kb.sb("lnmean", [128, 512], F32)
        msq = kb.sb("lnmsq", [128, 512], F32)
        rstd = kb.sb("lnrstd", [128, 512], F32)
        nmr = kb.sb("lnnmr", [128, 512], F32)
        pa, pb = kb.ps(), kb.ps()
        for kc in range(KC):
            a_, b_ = xb[kc % 2], sb_[kc % 2]
            kb.op("dve", lambda e: e.tensor_copy(out=a_[:, 0:nt], in_=XT[:, kc, 0:nt]), [XT], [a_])
            kb.op("dve", lambda e: e.tensor_tensor(out=b_[:, 0:nt], in0=XT[:, kc, 0:nt], in1=XT[:, kc, 0:nt], op=ALU.mult), [XT], [b_])
            kb.op("pe", lambda e: e.matmul(pa[:, 0:nt], lhsT=C["ones_b"][:], rhs=a_[:, 0:nt], start=(kc == 0), stop=(kc == KC - 1)), [C["ones_b"], a_], [pa])
            kb.op("pe", lambda e: e.matmul(pb[:, 0:nt], lhsT=C["ones_b"][:], rhs=b_[:, 0:nt], start=(kc == 0), stop=(kc == KC - 1)), [C["ones_b"], b_], [pb])
        kb.op("dve", lambda e: e.tensor_scalar(out=mean[:, 0:nt], in0=pa[:, 0:nt], scalar1=1.0 / D, scalar2=None, op0=ALU.mult), [pa], [mean])
        kb.op("dve", lambda e: e.tensor_tensor(out=msq[:, 0:nt], in0=mean[:, 0:nt], in1=mean[:, 0:nt], op=ALU.mult), [mean], [msq])
        kb.op("dve", lambda e: e.scalar_tensor_tensor(out=rstd[:, 0:nt], in0=pb[:, 0:nt], scalar=1.0 / D, in1=msq[:, 0:nt], op0=ALU.mult, op1=ALU.subtract), [pb, msq], [rstd])
        kb.op("dve", lambda e: e.tensor_scalar(out=rstd[:, 0:nt], in0=rstd[:, 0:nt], scalar1=LN_EPS, scalar2=None, op0=ALU.add), [rstd], [rstd])
        kb.op("act", lambda e: e.activation(out=rstd[:, 0:nt], in_=rstd[:, 0:nt], func=AF.Sqrt), [rstd], [rstd])
        kb.op("dve", lambda e: e.reciprocal(out=rstd[:, 0:nt], in_=rstd[:, 0:nt]), [rstd], [rstd])
        kb.op("dve", lambda e: e.scalar_tensor_tensor(out=nmr[:, 0:nt], in0=mean[:, 0:nt], scalar=-1.0, in1=rstd[:, 0:nt], op0=ALU.mult, op1=ALU.mult), [mean, rstd], [nmr])
        for kc in range(KC):
            kb.op("dve", lambda e: e.tensor_tensor(out=XT[:, kc, 0:nt], in0=XT[:, kc, 0:nt], in1=rstd[:, 0:nt], op=ALU.mult), [XT, rstd], [XT])
            kb.op("dve", lambda e: e.tensor_tensor(out=XT[:, kc, 0:nt], in0=XT[:, kc, 0:nt], in1=nmr[:, 0:nt], op=ALU.add), [XT, nmr], [XT])

    def ffn_part(l, XT, hT, nt, cols, C, li):
        g4, b4, cd = cols["g4"], cols["b4"], cols["cd"]
        hs, hb = mod_cols(cols, "f%d" % li, 1, g4[:, 2 * l, :], b4[:, 2 * l, :])
        rs, rb = res_cols(cols, "f%d" % li, g4[:, 2 * l, :], b4[:, 2 * l, :])
        for kc in range(KC):
            kb.op("dve", lambda e: e.tensor_scalar(out=hT[:, kc, 0:nt], in0=XT[:, kc, 0:nt], scalar1=hs[:, kc:kc + 1], scalar2=hb[:, kc:kc + 1], op0=ALU.mult, op1=ALU.add), [XT, hs, hb], [hT])
            kb.op("dve", lambda e: e.tensor_scalar(out=XT[:, kc, 0:nt], in0=XT[:, kc, 0:nt], scalar1=rs[:, kc:kc + 1], scalar2=rb[:, kc:kc + 1], op0=ALU.mult, op1=ALU.add), [XT, rs, rb], [XT])
        uT = kb.sb("uT", [128, 32, 512], BF16)
        w1b = kb.sbn("w1b", 2, [128, KC, 512], BF16)
        w2b = kb.sbn("w2b", 2, [128, 32 * 128], BF16)
        ust = kb.sbn("ust", 2, [128, 512], F32)
        wi = 0
        for half in range(2):
            for fg in range(8):
                w = w1b[wi % 2]
                wi += 1
                c0 = (half * 8 + fg) * 512
                kb.dma(w[:], W1_b[l][:, c0:c0 + 512].rearrange("(kc p) n -> p kc n", p=128), W=[w])
                for fcl in range(4):
                    fci = fg * 4 + fcl
                    pt = kb.ps()
                    for kc in range(KC):
                        kb.op("pe", lambda e: e.matmul(pt[:, 0:nt], lhsT=w[:, kc, fcl * 128:(fcl + 1) * 128], rhs=hT[:, kc, 0:nt], start=(kc == 0), stop=(kc == KC - 1)), [w, hT], [pt])
                    u_ = ust[fci % 2]
                    kb.op("act", lambda e: e.activation(out=u_[:, 0:nt], in_=pt[:, 0:nt], func=AF.Copy), [pt], [u_])
                    kb.op("dve", lambda e: e.scalar_tensor_tensor(out=uT[:, fci, 0:nt], in0=u_[:, 0:nt], scalar=0.0, in1=u_[:, 0:nt], op0=ALU.max, op1=ALU.mult), [u_], [uT])
            for o in range(KC):
                w2 = w2b[o % 2]
                kb.dma(w2[:], W2_s[l, o][:, half * 4096:(half + 1) * 4096], W=[w2])
                pt = kb.ps()
                for fc in range(32):
                    kb.op("pe", lambda e: e.matmul(pt[:, 0:nt], lhsT=w2[:, fc * 128:(fc + 1) * 128], rhs=uT[:, fc, 0:nt], start=(fc == 0), stop=(fc == 31)), [w2, uT], [pt])
                kb.op("dve", lambda e: e.scalar_tensor_tensor(out=XT[:, o, 0:nt], in0=pt[:, 0:nt], scalar=cd[:, 80 + o:81 + o], in1=XT[:, o, 0:nt], op0=ALU.mult, op1=ALU.add), [pt, cd, XT], [XT])
        layernorm(XT, nt, C)

    def phase_B1(q, e0, nt, uid):
        Tq, Lq = T[q], L[q]
        nq = nt // 128
        with kb.phase("B1_%d_%d" % (q, uid)):
            cols = load_cols(0, q)
            hs, hb = mod_cols(cols, "b1", 0)
            C = load_consts(["ident", "ones", "rotperm"])
            qk = kb.sb("qkw", [128, 2], F32)
            kb.dma(qk[:], qkw_col, W=[qk])
            oh = kb.sb("oh", [128, 4], F32)
            kb.dma(oh[:], onehot, W=[oh])
            gnw = kb.sb("gnw", [128, 128], F32)
            kb.dma(gnw[:], gnw_row.partition_broadcast(128), W=[gnw])
            xt = kb.sbn("bxt", 2, [128, D], F32)
            XT = kb.sb("XT", [128, KC, 512], F32)
            hT = kb.sb("hT", [128, KC, 512], BF16)
            OT = kb.sb("OT", [128, KC, 512], BF16)
            QT = kb.sb("QT", [128, NH, 512], BF16)
            WB = kb.sb("WB", [128, KC, 1024], BF16)
            cs = kb.sb("cs", [128, 2, 512], F32)
            kb.dma(cs[:, :, 0:nt], ropeQ[q][:, :, e0:e0 + nt].rearrange("c p t -> p c t"), W=[cs])
            sq = kb.sbn("sq", 1, [128, 1024], BF16) * 2
            xw = kb.sbn("xw", 1, [128, 512], F32) * 2
            k1 = kb.sbn("k1", 1, [128, 512], F32) * 2
            k2 = kb.sbn("k2", 1, [128, 512], F32) * 2
            rs_ = kb.sbn("rstd", 1, [128, 512], F32) * 2
            for ti in range(nq):
                x_ = xt[ti % 2]
                kb.dma(x_[:], xe[q][e0 + ti * 128:e0 + (ti + 1) * 128, :], W=[x_])
                for kc4 in range(4):
                    pt = kb.ps()
                    for k in range(4):
                        kc = kc4 * 4 + k
                        kb.op("pe", lambda e: e.transpose(out=pt[:, k * 128:(k + 1) * 128], in_=x_[:, kc * 128:(kc + 1) * 128], identity=C["ident"][:]), [x_, C["ident"]], [pt])
                    kb.op("act", lambda e: e.activation(out=XT[:, kc4 * 4:(kc4 + 1) * 4, ti * 128:(ti + 1) * 128], in_=pt[:].rearrange("p (k t) -> p k t", k=4), func=AF.Copy), [pt], [XT])
            for kc in range(KC):
                kb.op("dve", lambda e: e.tensor_scalar(out=hT[:, kc, 0:nt], in0=XT[:, kc, 0:nt], scalar1=hs[:, kc:kc + 1], scalar2=hb[:, kc:kc + 1], op0=ALU.mult, op1=ALU.add), [XT, hs, hb], [hT])
            kb.dma(XRES[q][:, 0:nt].rearrange("(kc p) t -> p kc t", p=128), XT[:, :, 0:nt], R=[XT], W=[kb.DR("XRES")])
            kb.dma(WB[:], Win_b[:, C_AQ:C_AQ + 1024].rearrange("(kc p) n -> p kc n", p=128), W=[WB])
            for h in range(NH):
                pt = kb.ps()
                for kc in range(KC):
                    kb.op("pe", lambda e: e.matmul(pt[:, 0:nt], lhsT=WB[:, kc, h * 128:(h + 1) * 128], rhs=hT[:, kc, 0:nt], start=(kc == 0), stop=(kc == KC - 1)), [WB, hT], [pt])
                i2 = h % 2
                qv = TT(QT.t[:, h, :], "QTv")
                qv.r = QT.r
                rope_norm(kb, C, pt, nt, qk, 0, cs, sq[i2], xw[i2], k1[i2], k2[i2], rs_[i2], qv, 128 ** -0.5)
            kb.dma(WB[:], Win_b[:, C_Z:C_Z + 1024].rearrange("(kc p) n -> p kc n", p=128), W=[WB])
            zs = kb.sbn("zs", 1, [128, 1024], F32) * 2
            cand = kb.sbn("cand", 2, [128, 1024], F32) * 2
            osel = kb.sbn("osel", 1, [128, 1024], F32) * 2
            osq = kb.sb("osq", [128, 1024], F32)
            ss = kb.sbn("ss", 2, [128, 8], F32)
            ogb = kb.sbn("ogb", 2, [128, 1024], BF16)
            for ti in range(nq):
                z_, os_, s_, og_ = zs[ti % 2], osel[ti % 2], ss[ti % 2], ogb[ti % 2]
                for hf in range(2):
                    pt = kb.ps()
                    for kc in range(KC):
                        kb.op("pe", lambda e: e.matmul(pt[:], lhsT=hT[:, kc, ti * 128:(ti + 1) * 128], rhs=WB[:, kc, hf * 512:(hf + 1) * 512], start=(kc == 0), stop=(kc == KC - 1)), [WB, hT], [pt])
                    kb.op("act", lambda e: e.activation(out=z_[:, hf * 512:(hf + 1) * 512], in_=pt[:], func=AF.Silu), [pt], [z_])
                ee = e0 + ti * 128
                if ee < Lq:
                    cl = [(j, j * Lq + ee) for j in range(4)]
                elif ee == Lq:
                    cl = [(j, j * Lq - 128) for j in range(1, 4)]
                else:
                    cl = [(j, (j + 1) * Lq) for j in range(0, 3)]
                ci = 0
                for (j, row) in cl:
                    for src_ in (OSUM, OSUMB):
                        kb.dma(cand[ci % 2][:], src_[q][row:row + 128, :], W=[cand[ci % 2]])
                        if ci == 0:
                            kb.op("dve", lambda e: e.tensor_scalar(out=os_[:], in0=cand[ci % 2][:], scalar1=oh[:, j:j + 1], scalar2=None, op0=ALU.mult), [cand[ci % 2], oh], [os_])
                        else:
                            kb.op("dve", lambda e: e.scalar_tensor_tensor(out=os_[:], in0=cand[ci % 2][:], scalar=oh[:, j:j + 1], in1=os_[:], op0=ALU.mult, op1=ALU.add), [cand[ci % 2], oh, os_], [os_])
                        ci += 1
                kb.op("dve", lambda e: e.tensor_tensor(out=osq[:], in0=os_[:], in1=os_[:], op=ALU.mult), [os_], [osq])
                kb.op("dve", lambda e: e.reduce_sum(out=s_[:], in_=osq[:].rearrange("p (h d) -> p h d", h=8), axis=AX.X), [osq], [s_])
                kb.op("dve", lambda e: e.tensor_scalar(out=s_[:], in0=s_[:], scalar1=1.0 / 128, scalar2=NORM_EPS, op0=ALU.mult, op1=ALU.add), [s_], [s_])
                kb.op("act", lambda e: e.activation(out=s_[:], in_=s_[:], func=AF.Sqrt), [s_], [s_])
                kb.op("dve", lambda e: e.reciprocal(out=s_[:], in_=s_[:]), [s_], [s_])
                v3 = lambda a: a[:].rearrange("p (h d) -> p h d", h=8)
                kb.op("dve", lambda e: e.tensor_tensor(out=v3(os_), in0=v3(os_), in1=s_[:].unsqueeze(2).to_broadcast([128, 8, 128]), op=ALU.mult), [os_, s_], [os_])
                kb.op("dve", lambda e: e.tensor_tensor(out=v3(os_), in0=v3(os_), in1=gnw[:].unsqueeze(1).to_broadcast([128, 8, 128]), op=ALU.mult), [os_, gnw], [os_])
                kb.op("dve", lambda e: e.tensor_tensor(out=og_[:], in0=os_[:], in1=z_[:], op=ALU.mult), [os_, z_], [og_])
                for hp in range(2):
                    pb_ = kb.ps()
                    pbb = pb_[:].bitcast(BF16)
                    for k in range(4):
                        h = hp * 4 + k
                        kb.op("pe", lambda e: e.transpose(out=pbb[:, k * 128:(k + 1) * 128], in_=og_[:, h * 128:(h + 1) * 128], identity=C["ident_b"][:]), [og_, C["ident_b"]], [pb_])
                    kb.op("dve", lambda e: e.tensor_copy(out=OT[:, hp * 4:(hp + 1) * 4, ti * 128:(ti + 1) * 128], in_=pbb[:, 0:512].rearrange("p (k t) -> p k t", k=4)), [pb_], [OT])
            G = min(16, Tq // 128)
            ng = Tq // (128 * G)
            ktb = kb.sbn("ktb", 2, [128, G * 128], BF16)
            vab = kb.sbn("vab", 2, [128, G, 129], BF16)
            for v_ in vab:
                kb.op("dve", lambda e: e.memset(v_[:, :, 128:129], 1.0), [], [v_])
            ptb = kb.sbn("ptb", 4, [128, 512], BF16)
            rden = kb.sbn("rden", 2, [128, 1], F32)
            onb = kb.sbn("onb", 2, [128, 128], BF16)
            kb._psn = 5
            ACC = kb.PS[5:8]
            gi_ = 0
            pi_ = 0
            for pr in range(4):
                kvh = pr // 2
                first = [True, True, True]
                for kg in range(ng):
                    kt_, va_ = ktb[gi_ % 2], vab[gi_ % 2]
                    gi_ += 1
                    kb.dma(kt_[:], KTa[q][kvh][:, kg * G * 128:(kg + 1) * G * 128], W=[kt_])
                    kb.dma(va_[:, :, 0:128], Va[q][kvh][:, kg * G:(kg + 1) * G, :], W=[va_])
                    for kc in range(G):
                        for hh in range(2):
                            head = pr * 2 + hh
                            pt = kb.ps()
                            kb.op("pe", lambda e: e.matmul(pt[:, 0:nt], lhsT=kt_[:, kc * 128:(kc + 1) * 128], rhs=QT[:, head, 0:nt], start=True, stop=True), [kt_, QT], [pt])
                            p_ = ptb[pi_ % 4]
                            pi_ += 1
                            kb.op("act", lambda e: e.activation(out=p_[:, 0:nt], in_=pt[:, 0:nt], func=AF.Exp), [pt], [p_])
                            for qi in range(nq):
                                sl = hh * nq + qi
                                bank, off = sl // 3, (sl % 3) * 129
                                st_ = first[bank]
                                first[bank] = False
                                lastk = (kg == ng - 1 and kc == G - 1)
                                kb.op("pe", lambda e: e.matmul(ACC[bank][:, off:off + 129], lhsT=p_[:, qi * 128:(qi + 1) * 128], rhs=va_[:, kc, :],
                                                               start=st_, stop=lastk, skip_group_check=True), [p_, va_], [ACC[bank]])
                for hh in range(2):
                    head = pr * 2 + hh
                    pb_ = kb.ps()
                    pbb = pb_[:].bitcast(BF16)
                    for qi in range(nq):
                        sl = hh * nq + qi
                        bank, off = sl // 3, (sl % 3) * 129
                        r_, o_ = rden[sl % 2], onb[sl % 2]
                        kb.op("dve", lambda e: e.reciprocal(out=r_[:], in_=ACC[bank][:, off + 128:off + 129]), [ACC[bank]], [r_])
                        kb.op("dve", lambda e: e.tensor_scalar(out=o_[:], in0=ACC[bank][:, off:off + 128], scalar1=r_[:, 0:1], scalar2=None, op0=ALU.mult), [ACC[bank], r_], [o_])
                        kb.op("pe", lambda e: e.transpose(out=pbb[:, qi * 128:(qi + 1) * 128], in_=o_[:], identity=C["ident_b"][:]), [o_, C["ident_b"]], [pb_])
                    kb.op("dve", lambda e: e.tensor_copy(out=OT[:, 8 + head, 0:nt], in_=pbb[:, 0:nt]), [pb_], [OT])
            kb._psn = 8
            kb.dma(OTS[q][:, 0:nt].rearrange("(kc p) t -> p kc t", p=128), OT[:, :, 0:nt], R=[OT], W=[kb.DR("OTS")])

    def phase_B2(q, xcols, nt, uid):
        with kb.phase("B2_%d_%d" % (q, uid)):
            cols = load_cols(0, q)
            cd, g4, b4 = cols["cd"], cols["g4"], cols["b4"]
            C = load_consts(["ones"])
            XT = kb.sb("XT", [128, KC, 512], F32)
            OT = kb.sb("OT", [128, KC, 512], BF16)
            hT = kb.sb("hT", [128, KC, 512], BF16)
            kb.dma(XT[:, :, 0:nt], XRES[q][:, 0:nt].rearrange("(kc p) t -> p kc t", p=128), W=[XT])
            kb.dma(OT[:, :, 0:nt], OTS[q][:, 0:nt].rearrange("(kc p) t -> p kc t", p=128), W=[OT])
            wob = kb.sbn("wob", 2, [128, KC, 512], BF16)
            for og in range(4):
                w = wob[og % 2]
                kb.dma(w[:], Wout_b[:, og * 512:(og + 1) * 512].rearrange("(kc p) n -> p kc n", p=128), W=[w])
                for oc in range(4):
                    o = og * 4 + oc
                    pt = kb.ps()
                    for kc in range(KC):
                        kb.op("pe", lambda e: e.matmul(pt[:, 0:nt], lhsT=w[:, kc, oc * 128:(oc + 1) * 128], rhs=OT[:, kc, 0:nt], start=(kc == 0), stop=(kc == KC - 1)), [w, OT], [pt])
                    kb.op("dve", lambda e: e.tensor_scalar(out=XT[:, o, 0:nt], in0=XT[:, o, 0:nt], scalar1=ALPHA, scalar2=None, op0=ALU.mult), [XT], [XT])
                    kb.op("dve", lambda e: e.scalar_tensor_tensor(out=XT[:, o, 0:nt], in0=pt[:, 0:nt], scalar=cd[:, 32 + o:33 + o], in1=XT[:, o, 0:nt], op0=ALU.mult, op1=ALU.add), [pt, cd, XT], [XT])
            layernorm(XT, nt, C)
            ffn_part(0, XT, hT, nt, cols, C, uid)
            for kc in range(KC):
                kb.op("dve", lambda e: e.tensor_scalar(out=XT[:, kc, 0:nt], in0=XT[:, kc, 0:nt], scalar1=g4[:, 1, kc:kc + 1], scalar2=b4[:, 1, kc:kc + 1], op0=ALU.mult, op1=ALU.add), [XT, g4, b4], [XT])
            for (c0, t0_, n_) in xcols:
                kb.dma(X1T[q][:, c0:c0 + n_].rearrange("(kc p) t -> p kc t", p=128), XT[:, :, t0_:t0_ + n_], R=[XT], W=[kb.DR("X1T", c0)])

    def phase_C(q, blk, nt, uid):
        Lq = L[q]
        c0 = 128 + blk * 512
        n = nt + 16
        with kb.phase("C_%d_%d" % (q, uid)):
            cols = load_cols(1, q)
            cd, g4, b4 = cols["cd"], cols["g4"], cols["b4"]
            hs, hb = mod_cols(cols, "c", 0)
            C = load_consts(["ident", "ones"])
            psc = kb.sb("psc", [128, KC], F32)
            kb.dma(psc[:], pscale_col, W=[psc])
            gp = kb.sb("gp", [128, KC], F32)
            kb.op("dve", lambda e: e.tensor_tensor(out=gp[:], in0=psc[:], in1=cd[:, 32:48], op=ALU.mult), [psc, cd], [gp])
            vr = kb.sb("vr", [128, 528], F32)
            kb.dma(vr[:, 0:n], validr[q][:, c0 - 8:c0 - 8 + n].partition_broadcast(128), W=[vr])
            ic = kb.sb("ic", [128, 4, 512], F32)
            for gi in range(4):
                kb.dma(ic[:, gi, 0:nt], invcnt[q][gi:gi + 1, c0:c0 + nt].partition_broadcast(128), W=[ic])
            pw = kb.sb("pw", [128, 4, 4, 512], BF16)
            for gi in range(4):
                kb.dma(pw[:, gi], Pool_b[gi * 512:(gi + 1) * 512, :].rearrange("(kc p) n -> p kc n", p=128), W=[pw])
            XT = kb.sb("XT", [128, KC, 512], F32)
            hT = kb.sb("hT", [128, KC, 512], BF16)
            xh = kb.sbn("xh", 2, [128, 528], F32)
            hm = kb.sbn("hm", 2, [128, 528], F32)
            A_ = kb.sbn("pA", 1, [128, 528], F32) * 2
            B_ = kb.sbn("pB", 1, [128, 528], F32) * 2
            for kc in range(KC):
                gi = kc // 4
                x_, h_, a_, b_ = xh[kc % 2], hm[kc % 2], A_[kc % 2], B_[kc % 2]
                kb.dma(x_[:, 0:n], X1T[q][kc * 128:(kc + 1) * 128, c0 - 8:c0 - 8 + n], W=[x_])
                kb.op("dve", lambda e: e.tensor_scalar(out=XT[:, kc, 0:nt], in0=x_[:, 8:8 + nt], scalar1=ALPHA, scalar2=None, op0=ALU.mult), [x_], [XT])
                kb.op("dve", lambda e: e.tensor_scalar(out=h_[:, 0:n], in0=x_[:, 0:n], scalar1=hs[:, kc:kc + 1], scalar2=hb[:, kc:kc + 1], op0=ALU.mult, op1=ALU.add), [x_, hs, hb], [h_])
                kb.op("dve", lambda e: e.tensor_tensor(out=h_[:, 0:n], in0=h_[:, 0:n], in1=vr[:, 0:n], op=ALU.mult), [h_, vr], [h_])
                kb.op("dve", lambda e: e.tensor_tensor(out=a_[:, 1:n], in0=h_[:, 1:n], in1=h_[:, 0:n - 1], op=ALU.add), [h_], [a_])
                src = a_
                if gi >= 1:
                    kb.op("dve", lambda e: e.tensor_tensor(out=b_[:, 2:n - 1], in0=a_[:, 1:n - 2], in1=a_[:, 3:n], op=ALU.add), [a_], [b_])
                    src = b_
                if gi >= 2:
                    kb.op("dve", lambda e: e.tensor_tensor(out=a_[:, 4:n - 3], in0=b_[:, 2:n - 5], in1=b_[:, 6:n - 1], op=ALU.add), [b_], [a_])
                    src = a_
                if gi >= 3:
                    kb.op("dve", lambda e: e.tensor_tensor(out=b_[:, 8:n - 7], in0=a_[:, 4:n - 11], in1=a_[:, 12:n - 3], op=ALU.add), [a_], [b_])
                    src = b_
                dst = a_ if src is b_ else b_
                kb.op("dve", lambda e: e.tensor_tensor(out=dst[:, 8:8 + nt], in0=src[:, 8:8 + nt], in1=ic[:, gi, 0:nt], op=ALU.mult), [src, ic], [dst])
                kb.op("dve", lambda e: e.tensor_tensor(out=hT[:, kc, 0:nt], in0=dst[:, 8:8 + nt], in1=h_[:, 8:8 + nt], op=ALU.subtract), [dst, h_], [hT])
            for gi in range(4):
                for oc in range(4):
                    o = gi * 4 + oc
                    pt = kb.ps()
                    for k4 in range(4):
                        kb.op("pe", lambda e: e.matmul(pt[:, 0:nt], lhsT=pw[:, gi, k4, oc * 128:(oc + 1) * 128], rhs=hT[:, gi * 4 + k4, 0:nt], start=(k4 == 0), stop=(k4 == 3)), [pw, hT], [pt])
                    kb.op("dve", lambda e: e.scalar_tensor_tensor(out=XT[:, o, 0:nt], in0=pt[:, 0:nt], scalar=gp[:, o:o + 1], in1=XT[:, o, 0:nt], op0=ALU.mult, op1=ALU.add), [pt, gp, XT], [XT])
            layernorm(XT, nt, C)
            ffn_part(1, XT, hT, nt, cols, C, uid)
            for kc in range(KC):
                kb.op("dve", lambda e: e.tensor_scalar(out=XT[:, kc, 0:nt], in0=XT[:, kc, 0:nt], scalar1=g4[:, 3, kc:kc + 1], scalar2=b4[:, 3, kc:kc + 1], op0=ALU.mult, op1=ALU.add), [XT, g4, b4], [XT])
            yt = kb.sbn("yt", 1, [128, D], F32) * 2
            for ti in range(nt // 128):
                y_ = yt[ti % 2]
                for kc4 in range(4):
                    pt = kb.ps()
                    for k in range(4):
                        kc = kc4 * 4 + k
                        kb.op("pe", lambda e: e.transpose(out=pt[:, k * 128:(k + 1) * 128], in_=XT[:, kc, ti * 128:(ti + 1) * 128], identity=C["ident"][:]), [XT, C["ident"]], [pt])
                    kb.op("act", lambda e: e.activation(out=y_[:, kc4 * 512:(kc4 + 1) * 512], in_=pt[:], func=AF.Copy), [pt], [y_])
                r0 = blk * 512 + ti * 128
                kb.dma(y_out[q][r0:r0 + 128, :], y_[:], R=[y_], W=[kb.DR("y", (q, r0))])

    uid = 0
    for q in range(2):
        Lq = L[q]
        bs = min(512, Lq)
        for blk in range(Lq // bs):
            uid += 1
            phase_B1(q, blk * bs, bs, uid)
            phase_B2(q, [(128 + blk * bs, 0, bs)], bs, uid)
        uid += 1
        phase_B1(q, Lq, 256, uid)
        phase_B2(q, [(0, 0, 128), (128 + Lq, 128, 128)], 256, uid)
    if stop == "B":
        return kb
    for q in range(2):
        Lq = L[q]
        bs = min(512, Lq)
        for blk in range(Lq // bs):
            uid += 1
            phase_C(q, blk, bs, uid)
    return kb


def _col(v):
    v = np.asarray(v, np.float32)
    return np.ascontiguousarray(v.reshape(-1, 128).T)


def rope_tables(pos):
    pos = np.asarray(pos)
    row = (pos // 64).astype(np.float32)
    col = (pos % 64).astype(np.float32)
    half = 64
    inv_freq = (np.float32(10000.0) ** (-np.arange(0, half, 2, dtype=np.float32) / np.float32(half))).astype(np.float32)
    ar = row[:, None] * inv_freq
    ac = col[:, None] * inv_freq
    ang = np.concatenate([ar, ar, ac, ac], -1).astype(np.float32)
    return np.cos(ang).astype(np.float32), np.sin(ang).astype(np.float32)


def make_consts():
    i = np.arange(128)
    ident = np.eye(128, dtype=np.float32)
    ones = np.ones((128, 128), np.float32)
    tri_f = (i[:, None] <= i[None, :]).astype(np.float32)
    tri_b = (i[:, None] >= i[None, :]).astype(np.float32)
    madd_f = np.where(i[:, None] <= i[None, :], 0.0, NEGBIG).astype(np.float32)
    madd_b = np.where(i[:, None] >= i[None, :], 0.0, NEGBIG).astype(np.float32)
    strict_f = (i[:, None] < i[None, :]).astype(np.float32)
    strict_b = (i[:, None] > i[None, :]).astype(np.float32)
    rot = np.zeros((128, 128), np.float32)
    for k in range(32):
        rot[32 + k, k] = -1.0
        rot[k, 32 + k] = 1.0
        rot[96 + k, 64 + k] = -1.0
        rot[64 + k, 96 + k] = 1.0
    c = np.stack([ident, ones, tri_f, tri_b, madd_f, madd_b, strict_f, strict_b, rot], 1)
    return np.ascontiguousarray(c)


def ext_positions(Tq, Lq, s):
    own = np.arange(s * Lq, (s + 1) * Lq)
    left = np.arange(s * Lq - 128, s * Lq)
    right = np.arange((s + 1) * Lq, (s + 1) * Lq + 128)
    return own, left, right


def prepare_inputs(cfg, inp):
    T, L, E = cfg.T, cfg.L, cfg.E
    xs = [np.asarray(inp["x_sample"], np.float32), np.asarray(inp["x_prompt"], np.float32)]
    cs = [np.asarray(inp["c_sample"], np.float32), np.asarray(inp["c_prompt"], np.float32)]
    consts = make_consts()
    w_in = np.ascontiguousarray(np.asarray(inp["w_in"], np.float32)[0])
    conv = np.asarray(inp["conv_w"], np.float32)[0]
    convcol = np.ascontiguousarray(conv.T.reshape(24, 128, 5).transpose(1, 0, 2))
    ada_b = np.asarray(inp["ada_b"], np.float32)
    ada_bcol = np.ascontiguousarray(ada_b.reshape(2, 96, 128).transpose(2, 0, 1))
    lng = np.asarray(inp["ln_g"], np.float32).reshape(4, KC, 128).transpose(2, 0, 1)
    lnb = np.asarray(inp["ln_b"], np.float32).reshape(4, KC, 128).transpose(2, 0, 1)
    common = dict(
        ada_w=np.ascontiguousarray(np.asarray(inp["ada_w"], np.float32)),
        ada_bcol=ada_bcol, w_in=w_in,
        w_out=np.ascontiguousarray(np.asarray(inp["w_out"], np.float32)[0]),
        pool_w=np.ascontiguousarray(np.asarray(inp["pool_w"], np.float32)[0].reshape(2048, 512)),
        mlp_w1=np.ascontiguousarray(np.asarray(inp["mlp_w1"], np.float32)),
        mlp_w2=np.ascontiguousarray(np.asarray(inp["mlp_w2"], np.float32)),
        convcol=convcol,
        alog_row=np.ascontiguousarray(np.asarray(inp["a_log"], np.float32)[0].reshape(1, 16)),
        dtb_row=np.ascontiguousarray(np.asarray(inp["dt_bias"], np.float32)[0].reshape(1, 16)),
        gnw_row=np.ascontiguousarray(np.asarray(inp["gdn_norm_w"], np.float32)[0].reshape(1, 128)),
        qkw_col=np.ascontiguousarray(np.stack([np.asarray(inp["q_norm_w"], np.float32)[0], np.asarray(inp["k_norm_w"], np.float32)[0]], 1)),
        pscale_col=_col(np.asarray(inp["pool_scale"], np.float32)[0]),
        lng_col=np.ascontiguousarray(lng), lnb_col=np.ascontiguousarray(lnb),
        cst=consts,
    )
    ropeK = []
    for q in range(2):
        c, s_ = rope_tables(np.arange(T[q]))
        ropeK.append(np.ascontiguousarray(np.stack([c.T, s_.T], 0)))
    maps = []
    for core in range(8):
        g, s = core // 4, core % 4
        m = dict(common)
        m["ccol"] = np.ascontiguousarray(np.stack([_col(cs[0][g]), _col(cs[1][g])], 2))
        oh = np.zeros((128, 4), np.float32)
        oh[:, s] = 1.0
        m["onehot"] = oh
        for q in range(2):
            Tq, Lq = T[q], L[q]
            m["xf%d" % q] = np.ascontiguousarray(xs[q][g])
            own, left, right = ext_positions(Tq, Lq, s)
            pos = np.concatenate([own, left, right])
            ok = (pos >= 0) & (pos < Tq)
            xe = np.zeros((E[q], D), np.float32)
            xe[ok] = xs[q][g][pos[ok]]
            m["xe%d" % q] = xe
            m["ropeK%d" % q] = ropeK[q]
            c, s_ = rope_tables(np.clip(pos, 0, Tq - 1))
            m["ropeQ%d" % q] = np.ascontiguousarray(np.stack([c.T, s_.T], 0))
            posn = np.concatenate([left, own, right])
            okn = ((posn >= 0) & (posn < Tq)).astype(np.float32)
            m["valid%d" % q] = np.ascontiguousarray(okn.reshape(1, -1))
            ic = np.zeros((4, E[q]), np.float32)
            pc = np.clip(posn, 0, Tq - 1)
            for gi, win in enumerate(POOL_WINDOWS):
                lo = np.clip(pc - win // 2, 0, Tq - 1)
                hi = np.clip(pc + (win - 1 - win // 2), 0, Tq - 1)
                ic[gi] = 1.0 / (hi - lo + 1).astype(np.float32)
            m["invcnt%d" % q] = ic
        maps.append(m)
    return maps


_CACHE = {}


def kernel(**inputs):
    cfg = Cfg()
    kb = build_program(cfg)
    maps = prepare_inputs(cfg, inputs)
    maps = [{k: v for k, v in m.items() if k in kb.inputs} for m in maps]
    res = run_bass_kernel_spmd(kb.nc, maps, core_ids=list(range(8)))
    ys = np.zeros((2, cfg.T[0], D), np.float32)
    yp = np.zeros((2, cfg.T[1], D), np.float32)
    for core in range(8):
        g, s = core // 4, core % 4
        r = res.results[core]
        ys[g, s * cfg.L[0]:(s + 1) * cfg.L[0]] = r["y0"]
        yp[g, s * cfg.L[1]:(s + 1) * cfg.L[1]] = r["y1"]
    return (yp, ys)
```

```python
import contextlib
import math
import numpy as np
import concourse.bass as bass
import concourse.mybir as mybir
from concourse.bass_utils import run_bass_kernel_spmd

F32 = mybir.dt.float32
BF16 = mybir.dt.bfloat16
AF = mybir.ActivationFunctionType
ALU = mybir.AluOpType
AX = mybir.AxisListType

D = 2048
KC = 16
DFF = 8192
FC = 64
DIN = 5664
HD = 128
NH = 8
ALPHA = 4 ** 0.25
NORM_EPS = 1e-6
LN_EPS = 1e-5
NEGBIG = -60000.0
POOL_WINDOWS = (2, 4, 8, 16)
C_GQ, C_GK, C_GV, C_Z, C_B, C_A, C_AQ, C_AK, C_AV = 0, 1024, 2048, 3072, 4096, 4112, 4128, 5152, 5408


class Cfg:
    def __init__(self, Ts=16384, Tp=4096, debug=False, stop_after=None):
        self.T = [Ts, Tp]
        self.L = [Ts // 4, Tp // 4]
        self.E = [l + 256 for l in self.L]
        self.debug = debug
        self.stop_after = stop_after


class Res:
    __slots__ = ("name", "lw", "rd")

    def __init__(self, name=""):
        self.name = name
        self.lw = None
        self.rd = []


class TT:
    def __init__(self, t, name):
        self.t = t
        self.r = Res(name)

    def __getitem__(self, k):
        return self.t[k]


def _res(x):
    return x.r if isinstance(x, TT) else x


class _Rec:
    def __getattr__(self, name):
        def f(*a, **k):
            self.call = (name, a, k)
            return self
        return f


class Sched:
    NDS = 24

    def __init__(self, nc, st):
        self.nc = nc
        self.csem = {e: st.enter_context(nc.semaphore("c_" + e)) for e in ("pe", "act", "dve", "pool")}
        self.dsem = {(q, j): st.enter_context(nc.semaphore("d_%s_%d" % (q, j))) for q in ("sp", "act") for j in range(self.NDS)}
        self.cnt = {e: 0 for e in ("pe", "act", "dve", "pool")}
        self.dcnt = {q: 0 for q in ("sp", "act")}
        self.ops = []
        self.nres = []

    def begin(self):
        self.ops = []

    def add(self, eng, fn, reads=(), writes=(), dma=False):
        rec = _Rec()
        fn(rec)
        name_, a_, k_ = rec.call
        fn = lambda e, name_=name_, a_=a_, k_=k_: getattr(e, name_)(*a_, **k_)
        i = len(self.ops)
        deps = set()
        for r in reads:
            r = _res(r)
            if r.lw is not None:
                deps.add(r.lw)
        for w in writes:
            w = _res(w)
            if w.lw is not None:
                deps.add(w.lw)
            deps.update(w.rd)
        deps.discard(i)
        for r in reads:
            _res(r).rd.append(i)
        for w in writes:
            w = _res(w)
            w.lw = i
            w.rd = []
        self.ops.append(dict(eng=eng, fn=fn, deps=deps, dma=dma, sig=False))
        self.nres.extend(_res(x) for x in reads)
        self.nres.extend(_res(x) for x in writes)
        return i

    def emit(self):
        nc, ops = self.nc, self.ops
        for o in ops:
            nd = set()
            for d in o["deps"]:
                p = ops[d]
                if (not p["dma"]) and (not o["dma"]) and p["eng"] == "pe" and o["eng"] == "pe":
                    continue
                nd.add(d)
            o["deps"] = nd
        for o in ops:
            for d in o["deps"]:
                ops[d]["sig"] = True
        last = {}
        for i, o in enumerate(ops):
            if not o["dma"]:
                last[o["eng"]] = i
        for i in last.values():
            ops[i]["sig"] = True
        cnt, dcnt = self.cnt, self.dcnt
        for o in ops:
            if o["dma"]:
                q = o["eng"]
                k = dcnt[q]
                dcnt[q] += 1
                o["dsem"] = (q, k % self.NDS)
                o["dval"] = 16 * (k // self.NDS + 1)
            elif o["sig"]:
                cnt[o["eng"]] += 1
                o["val"] = cnt[o["eng"]]
        csem, dsem = self.csem, self.dsem
        engobj = {"pe": nc.tensor, "act": nc.scalar, "dve": nc.vector, "pool": nc.gpsimd, "sp": nc.sync}
        fin_c = dict(cnt)
        fin_d = dict(dcnt)

        def run(eng):
            e = engobj[eng]
            seen, dseen = {}, {}
            for o in ops:
                if o["eng"] != eng:
                    continue
                need, dneed = {}, {}
                for d in o["deps"]:
                    p = ops[d]
                    if p["dma"]:
                        dneed[p["dsem"]] = max(dneed.get(p["dsem"], 0), p["dval"])
                    else:
                        need[p["eng"]] = max(need.get(p["eng"], 0), p["val"])
                if o["dma"] and o["dval"] > 16:
                    dneed[o["dsem"]] = max(dneed.get(o["dsem"], 0), o["dval"] - 16)
                for pe_, v in need.items():
                    if seen.get(pe_, 0) < v:
                        e.wait_ge(csem[pe_], v)
                        seen[pe_] = v
                for ds_, v in dneed.items():
                    if dseen.get(ds_, 0) < v:
                        e.wait_ge(dsem[ds_], v)
                        dseen[ds_] = v
                ins = o["fn"](e)
                if o["dma"]:
                    ins.then_inc(dsem[o["dsem"]], 16)
                elif o["sig"]:
                    ins.then_inc(csem[eng], 1)
            if eng == "sp":
                for q in ("sp", "act"):
                    k = fin_d[q]
                    for j in range(self.NDS):
                        n = k // self.NDS + (1 if (k % self.NDS) > j else 0)
                        if n:
                            e.wait_ge(dsem[(q, j)], 16 * n)
                for ce in ("pe", "act", "dve", "pool"):
                    if fin_c[ce]:
                        e.wait_ge(csem[ce], fin_c[ce])

        with nc.Block() as block:
            @block.sync
            def _(x):
                run("sp")

            @block.tensor
            def _(x):
                run("pe")

            @block.scalar
            def _(x):
                run("act")

            @block.vector
            def _(x):
                run("dve")

            @block.gpsimd
            def _(x):
                run("pool")
        for r in self.nres:
            r.lw = None
            r.rd = []
        self.nres = []
        self.ops = []


class KB:
    def __init__(self, cfg):
        self.cfg = cfg
        self.nc = bass.Bass("TRN2", target_bir_lowering=False)
        self.gst = contextlib.ExitStack()
        self.S = Sched(self.nc, self.gst)
        self.st = None
        self.dres = {}
        self._alt = 0
        self._psi = 0
        self._psn = 8
        self._uid = 0
        self.inputs = {}
        self.outputs = {}

    def din(self, name, shape, dt=F32):
        ap = self.nc.dram_tensor(name, list(shape), dt, kind="ExternalInput").ap()
        self.inputs[name] = ap
        return ap

    def dout(self, name, shape, dt=F32):
        ap = self.nc.dram_tensor(name, list(shape), dt, kind="ExternalOutput").ap()
        self.outputs[name] = ap
        return ap

    def dscr(self, name, shape, dt):
        if self.cfg.debug:
            return self.dout("dbg_" + name, shape, dt)
        return self.nc.dram_tensor(name, list(shape), dt).ap()

    def DR(self, name, idx=0):
        k = (name, idx)
        if k not in self.dres:
            self.dres[k] = Res("%s_%s" % (name, idx))
        return self.dres[k]

    def sb(self, name, shape, dt):
        self._uid += 1
        return TT(self.st.enter_context(self.nc.sbuf_tensor("%s_u%d" % (name, self._uid), list(shape), dt)), name)

    def sbn(self, name, n, shape, dt):
        return [self.sb("%s%d" % (name, i), shape, dt) for i in range(n)]

    @contextlib.contextmanager
    def phase(self, name):
        self.S.begin()
        self._psi = 0
        with contextlib.ExitStack() as st:
            self.st = st
            self._uid += 1
            self.PS = [TT(st.enter_context(self.nc.psum_tensor("ps%d_u%d" % (i, self._uid), [128, 512], F32)), "ps%d" % i) for i in range(8)]
            epsc = self.sb("epsc", [128, 2], F32)
            self.op("dve", lambda e: e.memset(epsc[:, 0:1], NORM_EPS), [], [epsc])
            self.op("dve", lambda e: e.memset(epsc[:, 1:2], LN_EPS), [], [epsc])
            self.epsc = epsc
            yield
            self.S.emit()
        self.st = None

    def ps(self):
        p = self.PS[self._psi % self._psn]
        self._psi += 1
        return p

    def alt(self):
        self._alt += 1
        return "act" if self._alt % 2 else "dve"

    def op(self, eng, fn, R=(), W=()):
        return self.S.add(eng, fn, R, W)

    def dma(self, out, in_, R=(), W=(), q="sp"):
        return self.S.add(q, lambda e: e.dma_start(out=out, in_=in_), R, W, dma=True)

    def copy(self, eng, out, in_, R, W, scale=None):
        if eng == "act":
            if scale is None:
                self.op("act", lambda e: e.activation(out=out, in_=in_, func=AF.Copy), R, W)
            else:
                self.op("dve", lambda e: e.tensor_scalar(out=out, in0=in_, scalar1=scale, scalar2=None, op0=ALU.mult), R, W)
        else:
            if scale is None:
                self.op(eng, lambda e: e.tensor_copy(out=out, in_=in_), R, W)
            else:
                self.op(eng, lambda e: e.tensor_scalar(out=out, in0=in_, scalar1=scale, scalar2=None, op0=ALU.mult), R, W)


def build_program(cfg):
    kb = KB(cfg)
    nc = kb.nc
    T, L, E = cfg.T, cfg.L, cfg.E
    xf = [kb.din("xf%d" % q, [T[q], D]) for q in range(2)]
    xe = [kb.din("xe%d" % q, [E[q], D]) for q in range(2)]
    ccol = kb.din("ccol", [128, KC, 2])
    ada_w = kb.din("ada_w", [2, D, 6 * D])
    ada_bcol = kb.din("ada_bcol", [128, 2, 96])
    w_in = kb.din("w_in", [D, DIN])
    w_out = kb.din("w_out", [D, D])
    pool_w = kb.din("pool_w", [4 * 512, 512])
    mlp_w1 = kb.din("mlp_w1", [2, D, DFF])
    mlp_w2 = kb.din("mlp_w2", [2, DFF, D])
    convcol = kb.din("convcol", [128, 24, 5])
    alog_row = kb.din("alog_row", [1, 16])
    dtb_row = kb.din("dtb_row", [1, 16])
    gnw_row = kb.din("gnw_row", [1, 128])
    qkw_col = kb.din("qkw_col", [128, 2])
    pscale_col = kb.din("pscale_col", [128, KC])
    lng_col = kb.din("lng_col", [128, 4, KC])
    lnb_col = kb.din("lnb_col", [128, 4, KC])
    cst = kb.din("cst", [128, 9, 128])
    ropeK = [kb.din("ropeK%d" % q, [2, 128, T[q]]) for q in range(2)]
    ropeQ = [kb.din("ropeQ%d" % q, [2, 128, E[q]]) for q in range(2)]
    onehot = kb.din("onehot", [128, 4])
    validr = [kb.din("valid%d" % q, [1, E[q]]) for q in range(2)]
    invcnt = [kb.din("invcnt%d" % q, [4, E[q]]) for q in range(2)]
    y_out = [kb.dout("y%d" % q, [L[q], D]) for q in range(2)]

    Win_b = kb.dscr("Win_b", [D, DIN], BF16) if False else nc.dram_tensor("Win_b", [D, DIN], BF16).ap()
    Wout_b = nc.dram_tensor("Wout_b", [D, D], BF16).ap()
    Pool_b = nc.dram_tensor("Pool_b", [4 * 512, 512], BF16).ap()
    W1_b = nc.dram_tensor("W1_b", [2, D, DFF], BF16).ap()
    W2_s = nc.dram_tensor("W2_s", [2, 16, 128, FC * 128], BF16).ap()
    COND = kb.dscr("COND", [128, 2, 2, 96], F32)
    PRE = [kb.dscr("PRE%d" % q, [3072, T[q]], F32) for q in range(2)]
    BGR = [kb.dscr("BGR%d" % q, [T[q], 32], F32) for q in range(2)]
    BG = [kb.dscr("BG%d" % q, [T[q], 32], F32) for q in range(2)]
    KTa = [kb.dscr("KTa%d" % q, [2, 128, T[q]], BF16) for q in range(2)]
    Va = [kb.dscr("Va%d" % q, [2, 128, T[q] // 128, 128], BF16) for q in range(2)]
    QTg = [kb.dscr("QTg%d" % q, [T[q] // 128, 128, NH, 128], BF16) for q in range(2)]
    KTg = [kb.dscr("KTg%d" % q, [T[q] // 128, 128, NH, 128], BF16) for q in range(2)]
    Ktok = [kb.dscr("Ktok%d" % q, [T[q], 1024], BF16) for q in range(2)]
    Vtok = [kb.dscr("Vtok%d" % q, [T[q], 1024], F32) for q in range(2)]
    OSUM = [kb.dscr("OSUM%d" % q, [T[q], 1024], F32) for q in range(2)]
    OSUMB = [kb.dscr("OSUMB%d" % q, [T[q], 1024], F32) for q in range(2)]
    X1T = [kb.dscr("X1T%d" % q, [D, E[q]], F32) for q in range(2)]
    stop = cfg.stop_after

    def cast_phase(items, tag):
        with kb.phase("W" + tag):
            fb = kb.sbn("wf", 3, [128, 2048], F32)
            bb = kb.sbn("wb", 3, [128, 2048], BF16)
            engs = ["act", "dve", "pool"]
            for i, (src, dstfn, C) in enumerate(items):
                f, b = fb[i % 3], bb[i % 3]
                kb.dma(f[:, 0:C], src, W=[f])
                kb.copy(engs[i % 3], b[:, 0:C], f[:, 0:C], [f], [b])
                dstfn(b)

    items = []
    for rt in range(16):
        for c0 in range(0, DIN, 1888):
            def dst(b, rt=rt, c0=c0):
                kb.dma(Win_b[rt * 128:(rt + 1) * 128, c0:c0 + 1888], b[:, 0:1888], R=[b])
            items.append((w_in[rt * 128:(rt + 1) * 128, c0:c0 + 1888], dst, 1888))
        def dst(b, rt=rt):
            kb.dma(Wout_b[rt * 128:(rt + 1) * 128, :], b[:, :], R=[b])
        items.append((w_out[rt * 128:(rt + 1) * 128, :], dst, 2048))
    for rt in range(16):
        def dst(b, rt=rt):
            kb.dma(Pool_b[rt * 128:(rt + 1) * 128, :], b[:, 0:512], R=[b])
        items.append((pool_w[rt * 128:(rt + 1) * 128, :], dst, 512))
    cast_phase(items, "a")
    for l in range(2):
        items = []
        for rt in range(16):
            for c0 in range(0, DFF, 2048):
                def dst(b, rt=rt, c0=c0, l=l):
                    kb.dma(W1_b[l, rt * 128:(rt + 1) * 128, c0:c0 + 2048], b[:, :], R=[b])
                items.append((mlp_w1[l, rt * 128:(rt + 1) * 128, c0:c0 + 2048], dst, 2048))
        for fc in range(FC):
            def dst(b, fc=fc, l=l):
                kb.dma(W2_s[l][:, :, fc * 128:(fc + 1) * 128].rearrange("o p m -> p o m"),
                       b[:, :].rearrange("p (o m) -> p o m", o=16), R=[b])
            items.append((mlp_w2[l, fc * 128:(fc + 1) * 128, :], dst, 2048))
        cast_phase(items, "m%d" % l)

    if stop == "W":
        return kb
    with kb.phase("C0"):
        cc = kb.sb("cc", [128, KC, 2], F32)
        sc = kb.sb("sc", [128, KC, 2], F32)
        ab = kb.sb("ab", [128, 2, 96], F32)
        cd = kb.sb("cd", [128, 2, 2, 96], F32)
        wg = kb.sbn("wg", 2, [128, KC, 512], F32)
        kb.dma(cc[:], ccol, W=[cc])
        kb.dma(ab[:], ada_bcol, W=[ab])
        kb.op("act", lambda e: e.activation(out=sc[:], in_=cc[:], func=AF.Silu), [cc], [sc])
        gi = 0
        for l in range(2):
            pt = kb.ps()
            for cg in range(24):
                w = wg[gi % 2]
                gi += 1
                kb.dma(w[:], ada_w[l][:, cg * 512:(cg + 1) * 512].rearrange("(kc p) n -> p kc n", p=128), W=[w])
                for jj in range(4):
                    j = cg * 4 + jj
                    for kc in range(KC):
                        kb.op("pe", lambda e, w=w, kc=kc, jj=jj, j=j, pt=pt: e.matmul(
                            pt[:, 2 * j:2 * j + 2], lhsT=w[:, kc, jj * 128:(jj + 1) * 128], rhs=sc[:, kc, :],
                            start=(kc == 0), stop=(kc == KC - 1)), [w, sc], [pt])
            kb.op("dve", lambda e, l=l, pt=pt: e.tensor_tensor(
                out=cd[:, l].rearrange("p q j -> p j q"), in0=pt[:, 0:192].rearrange("p (j q) -> p j q", q=2),
                in1=ab[:, l, :].unsqueeze(2).to_broadcast([128, 96, 2]), op=ALU.add), [pt, ab], [cd])
        kb.dma(COND, cd[:], R=[cd], W=[kb.DR("COND")])
    if stop == "C0":
        return kb

    def load_cols(l, q):
        cd = kb.sb("cdl", [128, 96], F32)
        g4 = kb.sb("lg4", [128, 4, KC], F32)
        b4 = kb.sb("lb4", [128, 4, KC], F32)
        kb.dma(cd[:], COND[:, l, q, :], W=[cd])
        kb.dma(g4[:], lng_col, W=[g4])
        kb.dma(b4[:], lnb_col, W=[b4])
        cols = {}
        cols["cd"], cols["g4"], cols["b4"] = cd, g4, b4
        o1 = kb.sb("opsc", [128, 2, KC], F32)
        kb.op("dve", lambda e: e.tensor_scalar(out=o1[:, 0, :], in0=cd[:, 16:32], scalar1=1.0, scalar2=None, op0=ALU.add), [cd], [o1])
        kb.op("dve", lambda e: e.tensor_scalar(out=o1[:, 1, :], in0=cd[:, 64:80], scalar1=1.0, scalar2=None, op0=ALU.add), [cd], [o1])
        cols["opsc"] = o1
        return cols

    def mod_cols(cols, name, which, gsrc=None, bsrc=None):
        cd, o1 = cols["cd"], cols["opsc"]
        sh = cd[:, 0:16] if which == 0 else cd[:, 48:64]
        hs = kb.sb(name + "hs", [128, KC], F32)
        hb = kb.sb(name + "hb", [128, KC], F32)
        if gsrc is None:
            kb.op("dve", lambda e: e.tensor_copy(out=hs[:], in_=o1[:, which, :]), [o1], [hs])
            kb.op("dve", lambda e: e.tensor_copy(out=hb[:], in_=sh), [cd], [hb])
        else:
            kb.op("dve", lambda e: e.tensor_tensor(out=hs[:], in0=gsrc, in1=o1[:, which, :], op=ALU.mult), [o1, cols["g4"]], [hs])
            kb.op("dve", lambda e: e.tensor_tensor(out=hb[:], in0=bsrc, in1=o1[:, which, :], op=ALU.mult), [o1, cols["b4"]], [hb])
            kb.op("dve", lambda e: e.tensor_tensor(out=hb[:], in0=hb[:], in1=sh, op=ALU.add), [hb, cd], [hb])
        return hs, hb

    def res_cols(cols, name, gsrc=None, bsrc=None):
        rs = kb.sb(name + "rs", [128, KC], F32)
        rb = kb.sb(name + "rb", [128, KC], F32)
        if gsrc is None:
            kb.op("dve", lambda e: e.memset(rs[:], ALPHA), [], [rs])
            kb.op("dve", lambda e: e.memset(rb[:], 0.0), [], [rb])
        else:
            kb.op("dve", lambda e: e.tensor_scalar(out=rs[:], in0=gsrc, scalar1=ALPHA, scalar2=None, op0=ALU.mult), [cols["g4"]], [rs])
            kb.op("dve", lambda e: e.tensor_scalar(out=rb[:], in0=bsrc, scalar1=ALPHA, scalar2=None, op0=ALU.mult), [cols["b4"]], [rb])
        return rs, rb

    def load_consts(names):
        idx = dict(ident=0, ones=1, tri_f=2, tri_b=3, madd_f=4, madd_b=5, strict_f=6, strict_b=7, rotperm=8)
        out = {}
        for n in names:
            t = kb.sb("c_" + n, [128, 128], F32)
            kb.dma(t[:], cst[:, idx[n], :], W=[t])
            out[n] = t
            tb = kb.sb("cb_" + n, [128, 128], BF16)
            kb.op("dve", lambda e, t=t, tb=tb: e.tensor_copy(out=tb[:], in_=t[:]), [t], [tb])
            out[n + "_b"] = tb
        return out

    def phase_A1(q):
        Tq = T[q]
        nb = Tq // 512
        with kb.phase("A1_%d" % q):
            import os
            cols = load_cols(0, q)
            hs, hb = mod_cols(cols, "a1", 0)
            C = load_consts(["ident", "ones", "rotperm"])
            qk = kb.sb("qkw", [128, 2], F32)
            kb.dma(qk[:], qkw_col, W=[qk])
            xt = [kb.sbn("xt%d_" % i, 4, [128, D], F32) for i in range(1)] * 2
            hT = kb.sb("hT", [128, KC, 512], BF16)
            wgb = kb.sbn("wg", 2, [128, KC, 1024], BF16)
            wsm = kb.sb("wsm", [128, KC, 640], BF16)
            stg = kb.sbn("stg", 4, [128, 512], F32)
            cs = kb.sbn("cs", 2, [128, 2, 512], F32)
            sq = kb.sbn("sq", 2, [128, 1024], BF16)
            xw = kb.sbn("xw", 2, [128, 512], F32)
            k1 = kb.sbn("k1", 2, [128, 512], F32)
            k2 = kb.sbn("k2", 2, [128, 512], F32)
            rs_ = kb.sbn("rstd", 2, [128, 512], F32)
            kbf = kb.sbn("kbf", 2, [128, 512], BF16)
            vst = kb.sbn("vst", 2, [128, 256], BF16)
            sst = kb.sbn("sst", 2, [128, 32], F32)
            kb.dma(wsm[:, :, 512:544], Win_b[:, C_B:C_B + 32].rearrange("(kc p) n -> p kc n", p=128), W=[wsm])
            kb.dma(wsm[:, :, 0:512], Win_b[:, C_AK:C_AK + 512].rearrange("(kc p) n -> p kc n", p=128), W=[wsm])
            wi = 0
            si = 0
            for b in range(nb):
                t0 = b * 512
                xb = xt[b % 2]
                for ti in range(4):
                    kb.dma(xb[ti][:], xf[q][t0 + ti * 128:t0 + (ti + 1) * 128, :], W=[xb[ti]])
                kb.dma(cs[b % 2][:], ropeK[q][:, :, t0:t0 + 512].rearrange("c p t -> p c t"), W=[cs[b % 2]])
                for kc in range(KC):
                    pt = kb.ps()
                    for ti in range(4):
                        kb.op("pe", lambda e, pt=pt, ti=ti, kc=kc, xb=xb: e.transpose(
                            out=pt[:, ti * 128:(ti + 1) * 128], in_=xb[ti][:, kc * 128:(kc + 1) * 128], identity=C["ident"][:]),
                            [xb[ti], C["ident"]], [pt])
                    if True:
                        kb.op("dve", lambda e, pt=pt, kc=kc: e.tensor_scalar(out=hT[:, kc, :], in0=pt[:], scalar1=hs[:, kc:kc + 1], scalar2=hb[:, kc:kc + 1],
                                                                            op0=ALU.mult, op1=ALU.add), [pt, hs, hb], [hT])
                    else:
                        kb.op("act", lambda e, pt=pt, kc=kc: e.activation(out=hT[:, kc, :], in_=pt[:], func=AF.Identity,
                                                                         scale=hs[:, kc:kc + 1], bias=hb[:, kc:kc + 1]), [pt, hs, hb], [hT])
                parts = os.environ.get("A1PARTS", "123")
                for g3 in (range(3) if "1" in parts else []):
                    w = wgb[wi % 2]
                    wi += 1
                    kb.dma(w[:], Win_b[:, g3 * 1024:(g3 + 1) * 1024].rearrange("(kc p) n -> p kc n", p=128), W=[w])
                    for jj in range(8):
                        j = g3 * 8 + jj
                        pt = kb.ps()
                        for kc in range(KC):
                            kb.op("pe", lambda e, pt=pt, w=w, kc=kc, jj=jj: e.matmul(
                                pt[:], lhsT=w[:, kc, jj * 128:(jj + 1) * 128], rhs=hT[:, kc, :], start=(kc == 0), stop=(kc == KC - 1)),
                                [w, hT], [pt])
                        s = stg[si % 4]
                        si += 1
                        kb.copy(kb.alt(), s[:], pt[:], [pt], [s])
                        kb.dma(PRE[q][j * 128:(j + 1) * 128, t0:t0 + 512], s[:], R=[s], W=[kb.DR("PRE", (j, b))])
                for ti in (range(4) if "2" in parts else []):
                    pt = kb.ps()
                    for kc in range(KC):
                        kb.op("pe", lambda e, pt=pt, kc=kc, ti=ti: e.matmul(
                            pt[:, 0:32], lhsT=hT[:, kc, ti * 128:(ti + 1) * 128], rhs=wsm[:, kc, 512:544], start=(kc == 0), stop=(kc == KC - 1)),
                            [wsm, hT], [pt])
                    pt2 = kb.ps()
                    for kc in range(KC):
                        kb.op("pe", lambda e, pt2=pt2, kc=kc, ti=ti: e.matmul(
                            pt2[:, 0:256], lhsT=hT[:, kc, ti * 128:(ti + 1) * 128], rhs=wsm[:, kc, 256:512], start=(kc == 0), stop=(kc == KC - 1)),
                            [wsm, hT], [pt2])
                    tg = b * 4 + ti
                    vs, ss = vst[tg % 2], sst[tg % 2]
                    kb.copy("dve", ss[:], pt[:, 0:32], [pt], [ss])
                    kb.copy("act", vs[:], pt2[:, 0:256], [pt2], [vs])
                    kb.dma(BGR[q][tg * 128:(tg + 1) * 128, :], ss[:], R=[ss], W=[kb.DR("BGR", tg)])
                    kb.dma(Va[q][:, :, tg, :].rearrange("k p d -> p k d"), vs[:].rearrange("p (k d) -> p k d", k=2), R=[vs], W=[kb.DR("Va", tg)])
                for kv in (range(2) if "3" in parts else []):
                    i2 = (b * 2 + kv) % 2
                    pt = kb.ps()
                    for kc in range(KC):
                        kb.op("pe", lambda e, pt=pt, kc=kc, kv=kv: e.matmul(
                            pt[:], lhsT=wsm[:, kc, kv * 128:(kv + 1) * 128], rhs=hT[:, kc, :], start=(kc == 0), stop=(kc == KC - 1)),
                            [wsm, hT], [pt])
                    rope_norm(kb, C, pt, 512, qk, 1, cs[b % 2], sq[i2], xw[i2], k1[i2], k2[i2], rs_[i2], kbf[i2], 1.0)
                    kst = os.environ.get("KST", "sp")
                    if kst != "none":
                        kb.dma(KTa[q][kv, :, t0:t0 + 512], kbf[i2][:], R=[kbf[i2]], W=[kb.DR("KTa", (kv, b))], q=kst)

    def rope_norm(kb, C, pt, nt, wT, wi, cs, sq, xw, k1, k2, rstd, outb, outscale):
        kb.op("act", lambda e: e.activation(out=k2[:, 0:nt], in_=pt[:, 0:nt], func=AF.Copy), [pt], [k2])
        kb.op("dve", lambda e: e.tensor_tensor(out=sq[:, 0:nt], in0=k2[:, 0:nt], in1=k2[:, 0:nt], op=ALU.mult), [k2], [sq])
        kb.op("dve", lambda e: e.tensor_scalar(out=xw[:, 0:nt], in0=k2[:, 0:nt], scalar1=wT[:, wi:wi + 1], scalar2=None, op0=ALU.mult), [k2, wT], [xw])
        kb.op("dve", lambda e: e.tensor_copy(out=sq[:, 512:512 + nt], in_=xw[:, 0:nt]), [xw], [sq])
        p2 = kb.ps()
        kb.op("pe", lambda e: e.matmul(p2[:, 0:nt], lhsT=C["ones_b"][:], rhs=sq[:, 0:nt], start=True, stop=True), [C["ones_b"], sq], [p2])
        kb.op("dve", lambda e: e.tensor_scalar(out=rstd[:, 0:nt], in0=p2[:, 0:nt], scalar1=1.0 / 128, scalar2=NORM_EPS, op0=ALU.mult, op1=ALU.add), [p2], [rstd])
        kb.op("act", lambda e: e.activation(out=rstd[:, 0:nt], in_=rstd[:, 0:nt], func=AF.Sqrt), [rstd], [rstd])
        kb.op("dve", lambda e: e.reciprocal(out=rstd[:, 0:nt], in_=rstd[:, 0:nt]), [rstd], [rstd])
        p3 = kb.ps()
        kb.op("pe", lambda e: e.matmul(p3[:, 0:nt], lhsT=C["rotperm_b"][:], rhs=sq[:, 512:512 + nt], start=True, stop=True), [C["rotperm_b"], sq], [p3])
        kb.op("dve", lambda e: e.tensor_tensor(out=k1[:, 0:nt], in0=xw[:, 0:nt], in1=cs[:, 0, 0:nt], op=ALU.mult), [xw, cs], [k1])
        kb.op("dve", lambda e: e.tensor_tensor(out=k2[:, 0:nt], in0=p3[:, 0:nt], in1=cs[:, 1, 0:nt], op=ALU.mult), [p3, cs], [k2])
        kb.op("dve", lambda e: e.tensor_tensor(out=k1[:, 0:nt], in0=k1[:, 0:nt], in1=k2[:, 0:nt], op=ALU.add), [k1, k2], [k1])
        if outscale != 1.0:
            kb.op("dve", lambda e: e.scalar_tensor_tensor(out=outb[:, 0:nt], in0=k1[:, 0:nt], scalar=float(outscale), in1=rstd[:, 0:nt], op0=ALU.mult, op1=ALU.mult), [k1, rstd], [outb])
        else:
            kb.op("dve", lambda e: e.tensor_tensor(out=outb[:, 0:nt], in0=k1[:, 0:nt], in1=rstd[:, 0:nt], op=ALU.mult), [k1, rstd], [outb])

    for q in range(2):
        phase_A1(q)
    if stop == "A1":
        return kb

    def phase_A2(q):
        Tq = T[q]
        nb = Tq // 512
        with kb.phase("A2_%d" % q):
            C = load_consts(["ident", "ones"])
            cw = kb.sb("cw", [128, 24, 5], F32)
            kb.dma(cw[:], convcol, W=[cw])
            alr = kb.sb("alr", [128, 16], F32)
            dtr = kb.sb("dtr", [128, 16], F32)
            kb.dma(alr[:], alog_row.partition_broadcast(128), W=[alr])
            kb.dma(dtr[:], dtb_row.partition_broadcast(128), W=[dtr])
            nea = kb.sb("nea", [128, 16], F32)
            kb.op("act", lambda e: e.activation(out=nea[:], in_=alr[:], func=AF.Exp), [alr], [nea])
            kb.op("dve", lambda e: e.tensor_scalar(out=nea[:], in0=nea[:], scalar1=-1.0, scalar2=None, op0=ALU.mult), [nea], [nea])
            pre = kb.sbn("pre", 3, [128, 516], F32)
            acc = kb.sbn("acc", 2, [128, 512], F32)
            sg = kb.sbn("sg", 2, [128, 512], F32)
            sqb = kb.sbn("sqb", 2, [128, 512], BF16)
            rst = kb.sbn("rst", 2, [128, 512], F32)
            qst = kb.sb("qst", [128, NH, 512], BF16)
            kst = kb.sb("kst", [128, NH, 512], BF16)
            vbf = kb.sbn("vbf", 2, [128, 512], F32)
            ktok = kb.sbn("ktok", 4, [128, 1024], BF16)
            vtok = kb.sbn("vtok", 4, [128, 1024], F32)
            br = kb.sbn("br", 2, [128, 32], F32)
            bo = kb.sbn("bo", 2, [128, 32], F32)
            t1 = kb.sbn("t1_", 2, [128, 16], F32)
            t2 = kb.sbn("t2_", 2, [128, 16], F32)
            for tg in range(Tq // 128):
                b_, o_, a1, a2 = br[tg % 2], bo[tg % 2], t1[tg % 2], t2[tg % 2]
                kb.dma(b_[:], BGR[q][tg * 128:(tg + 1) * 128, :], W=[b_])
                kb.op("act", lambda e, b_=b_, o_=o_: e.activation(out=o_[:, 0:16], in_=b_[:, 0:16], func=AF.Exp, scale=-1.0), [b_], [o_])
                kb.op("dve", lambda e, o_=o_: e.tensor_scalar(out=o_[:, 0:16], in0=o_[:, 0:16], scalar1=1.0, scalar2=None, op0=ALU.add), [o_], [o_])
                kb.op("dve", lambda e, o_=o_: e.reciprocal(out=o_[:, 0:16], in_=o_[:, 0:16]), [o_], [o_])
                kb.op("dve", lambda e, b_=b_, a1=a1: e.tensor_tensor(out=a1[:], in0=b_[:, 16:32], in1=dtr[:], op=ALU.add), [b_, dtr], [a1])
                kb.op("dve", lambda e, a1=a1, a2=a2: e.tensor_scalar(out=a2[:], in0=a1[:], scalar1=-1.0, scalar2=None, op0=ALU.mult), [a1], [a2])
                kb.op("dve", lambda e, a1=a1, a2=a2: e.tensor_tensor(out=a2[:], in0=a2[:], in1=a1[:], op=ALU.max), [a1, a2], [a2])
                kb.op("act", lambda e, a2=a2: e.activation(out=a2[:], in_=a2[:], func=AF.Exp, scale=-1.0), [a2], [a2])
                kb.op("dve", lambda e, a2=a2: e.tensor_scalar(out=a2[:], in0=a2[:], scalar1=1.0, scalar2=None, op0=ALU.add), [a2], [a2])
                kb.op("act", lambda e, a2=a2: e.activation(out=a2[:], in_=a2[:], func=AF.Ln), [a2], [a2])
                kb.op("dve", lambda e, a1=a1: e.tensor_scalar(out=a1[:], in0=a1[:], scalar1=0.0, scalar2=None, op0=ALU.max), [a1], [a1])
                kb.op("dve", lambda e, a1=a1, a2=a2: e.tensor_tensor(out=a1[:], in0=a1[:], in1=a2[:], op=ALU.add), [a1, a2], [a1])
                kb.op("dve", lambda e, a1=a1, o_=o_: e.tensor_tensor(out=o_[:, 16:32], in0=a1[:], in1=nea[:], op=ALU.mult), [a1, nea], [o_])
                kb.dma(BG[q][tg * 128:(tg + 1) * 128, :], o_[:], R=[o_], W=[kb.DR("BG", tg)])
            pi = 0
            for b in range(nb):
                t0 = b * 512
                for j in range(24):
                    p_ = pre[pi % 3]
                    pi += 1
                    lo = 2 if b == 0 else 0
                    hi = 514 if b == nb - 1 else 516
                    if b == 0:
                        kb.op("dve", lambda e, p_=p_: e.memset(p_[:, 0:2], 0.0), [], [p_])
                    if b == nb - 1:
                        kb.op("dve", lambda e, p_=p_: e.memset(p_[:, 514:516], 0.0), [], [p_])
                    kb.dma(p_[:, lo:hi], PRE[q][j * 128:(j + 1) * 128, t0 - 2 + lo:t0 - 2 + hi], W=[p_])
                    a_ = acc[j % 2]
                    kb.op("dve", lambda e, p_=p_, a_=a_, j=j: e.tensor_scalar(out=a_[:], in0=p_[:, 0:512], scalar1=cw[:, j, 0:1], scalar2=None, op0=ALU.mult), [p_, cw], [a_])
                    for jj in range(1, 5):
                        kb.op("dve", lambda e, p_=p_, a_=a_, j=j, jj=jj: e.scalar_tensor_tensor(
                            out=a_[:], in0=p_[:, jj:jj + 512], scalar=cw[:, j, jj:jj + 1], in1=a_[:], op0=ALU.mult, op1=ALU.add), [p_, cw, a_], [a_])
                    s_ = sg[j % 2]
                    kb.op("act", lambda e, a_=a_, s_=s_: e.activation(out=s_[:], in_=a_[:], func=AF.Silu), [a_], [s_])
                    if j < 16:
                        h = j % 8
                        q_b, r_ = sqb[j % 2], rst[j % 2]
                        kb.op("dve", lambda e, s_=s_, q_b=q_b: e.tensor_tensor(out=q_b[:], in0=s_[:], in1=s_[:], op=ALU.mult), [s_], [q_b])
                        p2 = kb.ps()
                        kb.op("pe", lambda e, p2=p2, q_b=q_b: e.matmul(p2[:], lhsT=C["ones_b"][:], rhs=q_b[:], start=True, stop=True), [C["ones_b"], q_b], [p2])
                        kb.op("dve", lambda e, p2=p2, r_=r_: e.tensor_scalar(out=r_[:], in0=p2[:], scalar1=NORM_EPS, scalar2=None, op0=ALU.add), [p2], [r_])
                        kb.op("act", lambda e, r_=r_: e.activation(out=r_[:], in_=r_[:], func=AF.Sqrt), [r_], [r_])
                        kb.op("dve", lambda e, r_=r_: e.reciprocal(out=r_[:], in_=r_[:]), [r_], [r_])
                        dst = qst if j < 8 else kst
                        sc_ = (128 ** -0.5) if j < 8 else 1.0
                        kb.op("dve", lambda e, s_=s_, r_=r_, dst=dst, h=h, sc_=sc_: e.scalar_tensor_tensor(
                            out=dst[:, h, :], in0=s_[:], scalar=float(sc_), in1=r_[:], op0=ALU.mult, op1=ALU.mult), [s_, r_], [dst])
                        if j >= 8:
                            for ti in range(4):
                                pb = kb.ps()
                                pbb = pb[:].bitcast(BF16)
                                kb.op("pe", lambda e, pbb=pbb, ti=ti, h=h: e.transpose(out=pbb[:, 0:128], in_=kst[:, h, ti * 128:(ti + 1) * 128], identity=C["ident_b"][:]),
                                      [kst, C["ident_b"]], [pb])
                                kb.op("dve", lambda e, pbb=pbb, ti=ti, h=h: e.tensor_copy(out=ktok[ti][:, h * 128:(h + 1) * 128], in_=pbb[:, 0:128]), [pb], [ktok[ti]])
                    else:
                        h = j - 16
                        for ti in range(4):
                            pb = kb.ps()
                            kb.op("pe", lambda e, pb=pb, ti=ti, s_=s_: e.transpose(out=pb[:, 0:128], in_=s_[:, ti * 128:(ti + 1) * 128], identity=C["ident"][:]),
                                  [s_, C["ident"]], [pb])
                            kb.op("act", lambda e, pb=pb, ti=ti, h=h: e.activation(out=vtok[ti][:, h * 128:(h + 1) * 128], in_=pb[:, 0:128], func=AF.Copy), [pb], [vtok[ti]])
                for ti in range(4):
                    tg = b * 4 + ti
                    kb.dma(QTg[q][tg], qst[:, :, ti * 128:(ti + 1) * 128], R=[qst], W=[kb.DR("QTg", tg)])
                    kb.dma(KTg[q][tg], kst[:, :, ti * 128:(ti + 1) * 128], R=[kst], W=[kb.DR("KTg", tg)])
                    kb.dma(Ktok[q][tg * 128:(tg + 1) * 128, :], ktok[ti][:], R=[ktok[ti]], W=[kb.DR("Ktok", tg)])
                    kb.dma(Vtok[q][tg * 128:(tg + 1) * 128, :], vtok[ti][:], R=[vtok[ti]], W=[kb.DR("Vtok", tg)])

    for q in range(2):
        phase_A2(q)
    if stop == "A2":
        return kb

    def phase_G(q):
        Tq = T[q]
        ntile = Tq // 128
        with kb.phase("G_%d" % q):
          CC = load_consts(["ident", "ones", "tri_f", "madd_f", "strict_f", "tri_b", "madd_b", "strict_b"])
          streams = []
          for d in range(2):
            sfx = "_f" if d == 0 else "_b"
            C = CC
            TRI, MADD, STRICT = C["tri" + sfx + "_b"], C["madd" + sfx], C["strict" + sfx]
            IDB, ONB = C["ident_b"], C["ones_b"]
            qT = kb.sbn("gqT%d_" % d, 2, [128, NH, 128], BF16)
            kT = kb.sbn("gkT%d_" % d, 2, [128, NH, 128], BF16)
            kt_ = kb.sbn("gkt%d_" % d, 2, [128, 1024], BF16)
            vt_ = kb.sbn("gvt%d_" % d, 2, [128, 1024], F32)
            bg = kb.sbn("gbg%d_" % d, 2, [128, 32], F32)
            osl = kb.sbn("gos%d_" % d, 2, [128, 1024], F32)
            ost = kb.sbn("gost%d_" % d, 2, [128, 1024], F32)
            S32 = kb.sb("S32_%d" % d, [128, NH, 128], F32)
            Sb = kb.sb("Sb_%d" % d, [128, NH, 128], BF16)
            kb.op("dve", lambda e: e.memset(S32[:], 0.0), [], [S32])
            kb.op("dve", lambda e: e.memset(Sb[:], 0.0), [], [Sb])
            ghl = kb.sbn("ghl%d_" % d, 2, [128, 16], BF16)
            gtmp = kb.sbn("gtmp%d_" % d, 2, [128, 8], F32)
            pcsb = kb.sbn("pcsb%d_" % d, 2, [128, 32], F32)
            sc3 = kb.sbn("sc3%d_" % d, 2, [128, 24], F32)
            ex3 = kb.sbn("ex3%d_" % d, 2, [128, 24], F32)
            neg = kb.sbn("neg%d_" % d, 2, [128, 16], F32)
            decI = kb.sbn("decI%d_" % d, 2, [128, 512], F32)
            decS = kb.sbn("decS%d_" % d, 2, [128, 512], F32)
            Dm = kb.sbn("Dm%d_" % d, 2, [128, 512], F32)
            M = kb.sbn("M%d_" % d, 2, [128, 512], BF16)
            MT = kb.sbn("MT%d_" % d, 2, [128, 512], BF16)
            Y = kb.sbn("Y%d_" % d, 2, [128, 512], BF16)
            Wb = kb.sbn("Wb%d_" % d, 2, [128, 512], BF16)
            vn = kb.sbn("vn%d_" % d, 2, [128, 512], BF16)
            ktl = kb.sbn("ktl%d_" % d, 2, [128, 512], BF16)
            PTm = kb.sbn("PTm%d_" % d, 2, [128, 512], BF16)
            o1s = kb.sbn("o1s%d_" % d, 2, [128, 512], F32)
            def step(it, tg, d=d, sfx=sfx, TRI=TRI, MADD=MADD, STRICT=STRICT, IDB=IDB, ONB=ONB, qT=qT, kT=kT, kt_=kt_, vt_=vt_, bg=bg, osl=osl, ost=ost, S32=S32, Sb=Sb, ghl=ghl, gtmp=gtmp, pcsb=pcsb, sc3=sc3, ex3=ex3, neg=neg, decI=decI, decS=decS, Dm=Dm, M=M, MT=MT, Y=Y, Wb=Wb, vn=vn, ktl=ktl, PTm=PTm, o1s=o1s):
                i2 = it % 2
                kb.dma(qT[i2][:], QTg[q][tg], W=[qT[i2]])
                kb.dma(kT[i2][:], KTg[q][tg], W=[kT[i2]])
                kb.dma(kt_[i2][:], Ktok[q][tg * 128:(tg + 1) * 128, :], W=[kt_[i2]])
                kb.dma(vt_[i2][:], Vtok[q][tg * 128:(tg + 1) * 128, :], W=[vt_[i2]])
                kb.dma(bg[i2][:], BG[q][tg * 128:(tg + 1) * 128, :], W=[bg[i2]])
                gcol = bg[i2][:, 16 + d * 8:16 + d * 8 + 8]
                bcol = bg[i2][:, d * 8:d * 8 + 8]
                gh, gt_, s3, e3, ng = ghl[i2], gtmp[i2], sc3[i2], ex3[i2], neg[i2]
                kb.op("dve", lambda e, gh=gh, gcol=gcol: e.tensor_copy(out=gh[:, 0:8], in_=gcol), [bg[i2]], [gh])
                kb.op("dve", lambda e, gh=gh, gt_=gt_: e.tensor_copy(out=gt_[:], in_=gh[:, 0:8]), [gh], [gt_])
                kb.op("dve", lambda e, gh=gh, gt_=gt_, gcol=gcol: e.tensor_tensor(out=gh[:, 8:16], in0=gcol, in1=gt_[:], op=ALU.subtract), [bg[i2], gt_], [gh])
                pc = kb.ps()
                kb.op("pe", lambda e, pc=pc, gh=gh: e.matmul(pc[:, 0:16], lhsT=TRI[:], rhs=gh[:], start=True, stop=True), [TRI, gh], [pc])
                kb.op("pe", lambda e, pc=pc, gh=gh: e.matmul(pc[:, 16:32], lhsT=ONB[:], rhs=gh[:], start=True, stop=True), [ONB, gh], [pc])
                pcs = pcsb[i2]
                kb.op("dve", lambda e: e.tensor_copy(out=pcs[:], in_=pc[:, 0:32]), [pc], [pcs])
                kb.op("dve", lambda e: e.tensor_tensor(out=s3[:, 0:8], in0=pcs[:, 0:8], in1=pcs[:, 8:16], op=ALU.add), [pcs], [s3])
                kb.op("dve", lambda e: e.tensor_tensor(out=s3[:, 8:16], in0=pcs[:, 16:24], in1=pcs[:, 24:32], op=ALU.add), [pcs], [s3])
                kb.op("dve", lambda e, s3=s3: e.tensor_tensor(out=s3[:, 16:24], in0=s3[:, 8:16], in1=s3[:, 0:8], op=ALU.subtract), [s3], [s3])
                kb.op("act", lambda e, s3=s3, e3=e3: e.activation(out=e3[:], in_=s3[:], func=AF.Exp), [s3], [e3])
                kb.op("dve", lambda e, ng=ng, bcol=bcol: e.tensor_scalar(out=ng[:, 0:8], in0=bcol, scalar1=-1.0, scalar2=None, op0=ALU.mult), [bg[i2]], [ng])
                kb.op("dve", lambda e, ng=ng, e3=e3: e.tensor_scalar(out=ng[:, 8:16], in0=e3[:, 0:8], scalar1=-1.0, scalar2=None, op0=ALU.mult), [e3], [ng])
                for grp in range(2):
                    gi_ = (it * 2 + grp) % 2
                    dI, dS, D_, M_, MT_, Y_, W_, vn_, ktl_, PT_, o1_ = decI[gi_], decS[gi_], Dm[gi_], M[gi_], MT[gi_], Y[gi_], Wb[gi_], vn[gi_], ktl[gi_], PTm[gi_], o1s[gi_]
                    hs_ = [grp * 4 + k for k in range(4)]
                    pg = kb.ps()
                    for k, h in enumerate(hs_):
                        for part in range(2):
                            kb.op("pe", lambda e, pg=pg, k=k, h=h, part=part, gh=gh: e.matmul(
                                pg[:, k * 128:(k + 1) * 128], lhsT=gh[:, part * 8 + h:part * 8 + h + 1].to_broadcast([128, 128]), rhs=TRI[:],
                                start=(part == 0), stop=(part == 1)), [gh, TRI], [pg])
                    for k, h in enumerate(hs_):
                        kb.op("dve", lambda e, pg=pg, k=k, h=h, D_=D_, s3=s3: e.scalar_tensor_tensor(
                            out=D_[:, k * 128:(k + 1) * 128], in0=pg[:, k * 128:(k + 1) * 128], scalar=s3[:, h:h + 1], in1=MADD[:],
                            op0=ALU.subtract, op1=ALU.add), [pg, s3, MADD], [D_])
                    kb.op("act", lambda e, D_=D_, dI=dI: e.activation(out=dI[:], in_=D_[:], func=AF.Exp), [D_], [dI])
                    kb.op("dve", lambda e, dI=dI, dS=dS: e.tensor_tensor(out=dS[:].rearrange("p (k c) -> p k c", k=4), in0=dI[:].rearrange("p (k c) -> p k c", k=4),
                                                                        in1=STRICT[:].unsqueeze(1).to_broadcast([128, 4, 128]), op=ALU.mult), [dI, STRICT], [dS])
                    pk = kb.ps()
                    for k, h in enumerate(hs_):
                        kb.op("pe", lambda e, pk=pk, k=k, h=h: e.matmul(pk[:, k * 128:(k + 1) * 128], lhsT=kT[i2][:, h, :], rhs=kT[i2][:, h, :], start=True, stop=True), [kT[i2]], [pk])
                    for k, h in enumerate(hs_):
                        kb.op("dve", lambda e, pk=pk, k=k, h=h, M_=M_, dS=dS, ng=ng: e.scalar_tensor_tensor(
                            out=M_[:, k * 128:(k + 1) * 128], in0=pk[:, k * 128:(k + 1) * 128], scalar=ng[:, h:h + 1], in1=dS[:, k * 128:(k + 1) * 128],
                            op0=ALU.mult, op1=ALU.mult), [pk, ng, dS], [M_])
                    pm = kb.ps()
                    pmb = pm[:].bitcast(BF16)
                    for k in range(4):
                        kb.op("pe", lambda e, pmb=pmb, k=k, M_=M_: e.transpose(out=pmb[:, k * 128:(k + 1) * 128], in_=M_[:, k * 128:(k + 1) * 128], identity=IDB[:]), [M_, IDB], [pm])
                    kb.op("act", lambda e, pmb=pmb, MT_=MT_: e.activation(out=MT_[:], in_=pmb[:, 0:512], func=AF.Copy), [pm], [MT_])
                    kb.op("dve", lambda e, M_=M_, Y_=Y_: e.tensor_tensor(out=Y_[:].rearrange("p (k c) -> p k c", k=4), in0=M_[:].rearrange("p (k c) -> p k c", k=4),
                                                                        in1=IDB[:].unsqueeze(1).to_broadcast([128, 4, 128]), op=ALU.add), [M_, IDB], [Y_])
                    for step in range(6):
                        pa, pbk, py = kb.ps(), kb.ps(), kb.ps()
                        lastst = step == 5
                        for k in range(4):
                            sl = slice(k * 128, (k + 1) * 128)
                            kb.op("pe", lambda e, pbk=pbk, sl=sl, M_=M_, MT_=MT_: e.matmul(pbk[:, sl], lhsT=M_[:, sl], rhs=MT_[:, sl], start=True, stop=True), [M_, MT_], [pbk])
                            if not lastst:
                                kb.op("pe", lambda e, pa=pa, sl=sl, M_=M_, MT_=MT_: e.matmul(pa[:, sl], lhsT=MT_[:, sl], rhs=M_[:, sl], start=True, stop=True), [M_, MT_], [pa])
                        kb.op("act", lambda e, pbk=pbk, MT_=MT_: e.activation(out=MT_[:], in_=pbk[:], func=AF.Copy), [pbk], [MT_])
                        if not lastst:
                            kb.op("dve", lambda e, pa=pa, M_=M_: e.tensor_copy(out=M_[:], in_=pa[:]), [pa], [M_])
                        for k in range(4):
                            sl = slice(k * 128, (k + 1) * 128)
                            kb.op("pe", lambda e, py=py, sl=sl, Y_=Y_: e.matmul(py[:, sl], lhsT=IDB[:], rhs=Y_[:, sl], start=True, stop=False, skip_group_check=True), [IDB, Y_], [py])
                            kb.op("pe", lambda e, py=py, sl=sl, Y_=Y_, MT_=MT_: e.matmul(py[:, sl], lhsT=MT_[:, sl], rhs=Y_[:, sl], start=False, stop=True, skip_group_check=True), [MT_, Y_], [py])
                        kb.op("dve", lambda e, py=py, Y_=Y_: e.tensor_copy(out=Y_[:], in_=py[:]), [py], [Y_])
                    pks = kb.ps()
                    po1 = kb.ps()
                    ppt = kb.ps()
                    for k, h in enumerate(hs_):
                        sl = slice(k * 128, (k + 1) * 128)
                        kb.op("pe", lambda e, pks=pks, sl=sl, h=h: e.matmul(pks[:, sl], lhsT=kT[i2][:, h, :], rhs=Sb[:, h, :], start=True, stop=True), [kT[i2], Sb], [pks])
                        kb.op("pe", lambda e, po1=po1, sl=sl, h=h: e.matmul(po1[:, sl], lhsT=qT[i2][:, h, :], rhs=Sb[:, h, :], start=True, stop=True), [qT[i2], Sb], [po1])
                        kb.op("pe", lambda e, ppt=ppt, sl=sl, h=h: e.matmul(ppt[:, sl], lhsT=kT[i2][:, h, :], rhs=qT[i2][:, h, :], start=True, stop=True), [kT[i2], qT[i2]], [ppt])
                    for k, h in enumerate(hs_):
                        sl = slice(k * 128, (k + 1) * 128)
                        kb.op("dve", lambda e, pks=pks, sl=sl, h=h, W_=W_, ng=ng: e.scalar_tensor_tensor(
                            out=W_[:, sl], in0=pks[:, sl], scalar=ng[:, 8 + h:9 + h], in1=vt_[i2][:, h * 128:(h + 1) * 128], op0=ALU.mult, op1=ALU.add), [pks, ng, vt_[i2]], [W_])
                        kb.op("dve", lambda e, po1=po1, sl=sl, h=h, o1_=o1_, e3=e3: e.tensor_scalar(
                            out=o1_[:, sl], in0=po1[:, sl], scalar1=e3[:, h:h + 1], scalar2=None, op0=ALU.mult), [po1, e3], [o1_])
                        kb.op("dve", lambda e, sl=sl, h=h, ktl_=ktl_, e3=e3: e.tensor_scalar(
                            out=ktl_[:, sl], in0=kt_[i2][:, h * 128:(h + 1) * 128], scalar1=e3[:, 16 + h:17 + h], scalar2=None, op0=ALU.mult), [kt_[i2], e3], [ktl_])
                    kb.op("dve", lambda e, ppt=ppt, PT_=PT_, dI=dI: e.tensor_tensor(out=PT_[:], in0=ppt[:], in1=dI[:], op=ALU.mult), [ppt, dI], [PT_])
                    pv = kb.ps()
                    for k, h in enumerate(hs_):
                        sl = slice(k * 128, (k + 1) * 128)
                        kb.op("pe", lambda e, pv=pv, sl=sl, Y_=Y_, W_=W_: e.matmul(pv[:, sl], lhsT=Y_[:, sl], rhs=W_[:, sl], start=True, stop=True), [Y_, W_], [pv])
                    for k, h in enumerate(hs_):
                        sl = slice(k * 128, (k + 1) * 128)
                        kb.op("dve", lambda e, pv=pv, sl=sl, h=h, vn_=vn_: e.tensor_scalar(
                            out=vn_[:, sl], in0=pv[:, sl], scalar1=bg[i2][:, d * 8 + h:d * 8 + h + 1], scalar2=None, op0=ALU.mult), [pv, bg[i2]], [vn_])
                    po2 = kb.ps()
                    pds = kb.ps()
                    for k, h in enumerate(hs_):
                        sl = slice(k * 128, (k + 1) * 128)
                        kb.op("pe", lambda e, po2=po2, sl=sl, PT_=PT_, vn_=vn_: e.matmul(po2[:, sl], lhsT=PT_[:, sl], rhs=vn_[:, sl], start=True, stop=True), [PT_, vn_], [po2])
                        kb.op("pe", lambda e, pds=pds, sl=sl, ktl_=ktl_, vn_=vn_: e.matmul(pds[:, sl], lhsT=ktl_[:, sl], rhs=vn_[:, sl], start=True, stop=True), [ktl_, vn_], [pds])
                    os_ = ost[i2]
                    kb.op("dve", lambda e, po2=po2, o1_=o1_, os_=os_, grp=grp: e.tensor_tensor(out=os_[:, grp * 512:(grp + 1) * 512], in0=po2[:], in1=o1_[:], op=ALU.add), [po2, o1_], [os_])
                    for k, h in enumerate(hs_):
                        sl = slice(k * 128, (k + 1) * 128)
                        kb.op("dve", lambda e, pds=pds, sl=sl, h=h, e3=e3: e.scalar_tensor_tensor(
                            out=S32[:, h, :], in0=S32[:, h, :], scalar=e3[:, 8 + h:9 + h], in1=pds[:, sl], op0=ALU.mult, op1=ALU.add), [S32, e3, pds], [S32])
                    kb.op("act", lambda e, grp=grp: e.activation(out=Sb[:, grp * 4:(grp + 1) * 4, :], in_=S32[:, grp * 4:(grp + 1) * 4, :], func=AF.Copy), [S32], [Sb])
                kb.dma((OSUM if d == 0 else OSUMB)[q][tg * 128:(tg + 1) * 128, :], ost[i2][:], R=[ost[i2]], W=[kb.DR("OSUM%d" % d, tg)])

            streams.append(step)
          for it in range(ntile):
            for d in range(2):
              streams[d](it, it if d == 0 else ntile - 1 - it)

    for q in range(2):
        phase_G(q)
    if stop == "G":
        return kb

    XRES = [kb.dscr("XRES%d" % q, [D, 512], F32) for q in range(2)]
    OTS = [kb.dscr("OTS%d" % q, [D, 512], BF16) for q in range(2)]

    def layernorm(XT, nt, C):
        xb = kb.sbn("lnxb", 2, [128, 512], BF16)
        sb_ = kb.sbn("lnsq", 2, [128, 512], BF16)
        mean = kb.sb("lnmean", [128, 512], F32)
        msq = kb.sb("lnmsq", [128, 512], F32)
        rstd = kb.sb("lnrstd", [128, 512], F32)
        nmr = kb.sb("lnnmr", [128, 512], F32)
        pa, pb = kb.ps(), kb.ps()
        for kc in range(KC):
            a_, b_ = xb[kc % 2], sb_[kc % 2]
            kb.op("dve", lambda e: e.tensor_copy(out=a_[:, 0:nt], in_=XT[:, kc, 0:nt]), [XT], [a_])
            kb.op("dve", lambda e: e.tensor_tensor(out=b_[:, 0:nt], in0=XT[:, kc, 0:nt], in1=XT[:, kc, 0:nt], op=ALU.mult), [XT], [b_])
            kb.op("pe", lambda e: e.matmul(pa[:, 0:nt], lhsT=C["ones_b"][:], rhs=a_[:, 0:nt], start=(kc == 0), stop=(kc == KC - 1)), [C["ones_b"], a_], [pa])
            kb.op("pe", lambda e: e.matmul(pb[:, 0:nt], lhsT=C["ones_b"][:], rhs=b_[:, 0:nt], start=(kc == 0), stop=(kc == KC - 1)), [C["ones_b"], b_], [pb])
        kb.op("dve", lambda e: e.tensor_scalar(out=mean[:, 0:nt], in0=pa[:, 0:nt], scalar1=1.0 / D, scalar2=None, op0=ALU.mult), [pa], [mean])
        kb.op("dve", lambda e: e.tensor_tensor(out=msq[:, 0:nt], in0=mean[:, 0:nt], in1=mean[:, 0:nt], op=ALU.mult), [mean], [msq])
        kb.op("dve", lambda e: e.scalar_tensor_tensor(out=rstd[:, 0:nt], in0=pb[:, 0:nt], scalar=1.0 / D, in1=msq[:, 0:nt], op0=ALU.mult, op1=ALU.subtract), [pb, msq], [rstd])
        kb.op("dve", lambda e: e.tensor_scalar(out=rstd[:, 0:nt], in0=rstd[:, 0:nt], scalar1=LN_EPS, scalar2=None, op0=ALU.add), [rstd], [rstd])
        kb.op("act", lambda e: e.activation(out=rstd[:, 0:nt], in_=rstd[:, 0:nt], func=AF.Sqrt), [rstd], [rstd])
        kb.op("dve", lambda e: e.reciprocal(out=rstd[:, 0:nt], in_=rstd[:, 0:nt]), [rstd], [rstd])
        kb.op("dve", lambda e: e.scalar_tensor_tensor(out=nmr[:, 0:nt], in0=mean[:, 0:nt], scalar=-1.0, in1=rstd[:, 0:nt], op0=ALU.mult, op1=ALU.mult), [mean, rstd], [nmr])
        for kc in range(KC):
            kb.op("dve", lambda e: e.tensor_tensor(out=XT[:, kc, 0:nt], in0=XT[:, kc, 0:nt], in1=rstd[:, 0:nt], op=ALU.mult), [XT, rstd], [XT])
            kb.op("dve", lambda e: e.tensor_tensor(out=XT[:, kc, 0:nt], in0=XT[:, kc, 0:nt], in1=nmr[:, 0:nt], op=ALU.add), [XT, nmr], [XT])

    def ffn_part(l, XT, hT, nt, cols, C, li):
        g4, b4, cd = cols["g4"], cols["b4"], cols["cd"]
        hs, hb = mod_cols(cols, "f%d" % li, 1, g4[:, 2 * l, :], b4[:, 2 * l, :])
        rs, rb = res_cols(cols, "f%d" % li, g4[:, 2 * l, :], b4[:, 2 * l, :])
        for kc in range(KC):
            kb.op("dve", lambda e: e.tensor_scalar(out=hT[:, kc, 0:nt], in0=XT[:, kc, 0:nt], scalar1=hs[:, kc:kc + 1], scalar2=hb[:, kc:kc + 1], op0=ALU.mult, op1=ALU.add), [XT, hs, hb], [hT])
            kb.op("dve", lambda e: e.tensor_scalar(out=XT[:, kc, 0:nt], in0=XT[:, kc, 0:nt], scalar1=rs[:, kc:kc + 1], scalar2=rb[:, kc:kc + 1], op0=ALU.mult, op1=ALU.add), [XT, rs, rb], [XT])
        uT = kb.sb("uT", [128, 32, 512], BF16)
        w1b = kb.sbn("w1b", 2, [128, KC, 512], BF16)
        w2b = kb.sbn("w2b", 2, [128, 32 * 128], BF16)
        ust = kb.sbn("ust", 2, [128, 512], F32)
        wi = 0
        for half in range(2):
            for fg in range(8):
                w = w1b[wi % 2]
                wi += 1
                c0 = (half * 8 + fg) * 512
                kb.dma(w[:], W1_b[l][:, c0:c0 + 512].rearrange("(kc p) n -> p kc n", p=128), W=[w])
                for fcl in range(4):
                    fci = fg * 4 + fcl
                    pt = kb.ps()
                    for kc in range(KC):
                        kb.op("pe", lambda e: e.matmul(pt[:, 0:nt], lhsT=w[:, kc, fcl * 128:(fcl + 1) * 128], rhs=hT[:, kc, 0:nt], start=(kc == 0), stop=(kc == KC - 1)), [w, hT], [pt])
                    u_ = ust[fci % 2]
                    kb.op("act", lambda e: e.activation(out=u_[:, 0:nt], in_=pt[:, 0:nt], func=AF.Copy), [pt], [u_])
                    kb.op("dve", lambda e: e.scalar_tensor_tensor(out=uT[:, fci, 0:nt], in0=u_[:, 0:nt], scalar=0.0, in1=u_[:, 0:nt], op0=ALU.max, op1=ALU.mult), [u_], [uT])
            for o in range(KC):
                w2 = w2b[o % 2]
                kb.dma(w2[:], W2_s[l, o][:, half * 4096:(half + 1) * 4096], W=[w2])
                pt = kb.ps()
                for fc in range(32):
                    kb.op("pe", lambda e: e.matmul(pt[:, 0:nt], lhsT=w2[:, fc * 128:(fc + 1) * 128], rhs=uT[:, fc, 0:nt], start=(fc == 0), stop=(fc == 31)), [w2, uT], [pt])
                kb.op("dve", lambda e: e.scalar_tensor_tensor(out=XT[:, o, 0:nt], in0=pt[:, 0:nt], scalar=cd[:, 80 + o:81 + o], in1=XT[:, o, 0:nt], op0=ALU.mult, op1=ALU.add), [pt, cd, XT], [XT])
        layernorm(XT, nt, C)

    def phase_B1(q, e0, nt, uid):
        Tq, Lq = T[q], L[q]
        nq = nt // 128
        with kb.phase("B1_%d_%d" % (q, uid)):
            cols = load_cols(0, q)
            hs, hb = mod_cols(cols, "b1", 0)
            C = load_consts(["ident", "ones", "rotperm"])
            qk = kb.sb("qkw", [128, 2], F32)
            kb.dma(qk[:], qkw_col, W=[qk])
            oh = kb.sb("oh", [128, 4], F32)
            kb.dma(oh[:], onehot, W=[oh])
            gnw = kb.sb("gnw", [128, 128], F32)
            kb.dma(gnw[:], gnw_row.partition_broadcast(128), W=[gnw])
            xt = kb.sbn("bxt", 2, [128, D], F32)
            XT = kb.sb("XT", [128, KC, 512], F32)
            hT = kb.sb("hT", [128, KC, 512], BF16)
            OT = kb.sb("OT", [128, KC, 512], BF16)
            QT = kb.sb("QT", [128, NH, 512], BF16)
            WB = kb.sb("WB", [128, KC, 1024], BF16)
            cs = kb.sb("cs", [128, 2, 512], F32)
            kb.dma(cs[:, :, 0:nt], ropeQ[q][:, :, e0:e0 + nt].rearrange("c p t -> p c t"), W=[cs])
            sq = kb.sbn("sq", 1, [128, 1024], BF16) * 2
            xw = kb.sbn("xw", 1, [128, 512], F32) * 2
            k1 = kb.sbn("k1", 1, [128, 512], F32) * 2
            k2 = kb.sbn("k2", 1, [128, 512], F32) * 2
            rs_ = kb.sbn("rstd", 1, [128, 512], F32) * 2
            for ti in range(nq):
                x_ = xt[ti % 2]
                kb.dma(x_[:], xe[q][e0 + ti * 128:e0 + (ti + 1) * 128, :], W=[x_])
                for kc4 in range(4):
                    pt = kb.ps()
                    for k in range(4):
                        kc = kc4 * 4 + k
                        kb.op("pe", lambda e: e.transpose(out=pt[:, k * 128:(k + 1) * 128], in_=x_[:, kc * 128:(kc + 1) * 128], identity=C["ident"][:]), [x_, C["ident"]], [pt])
                    kb.op("act", lambda e: e.activation(out=XT[:, kc4 * 4:(kc4 + 1) * 4, ti * 128:(ti + 1) * 128], in_=pt[:].rearrange("p (k t) -> p k t", k=4), func=AF.Copy), [pt], [XT])
            for kc in range(KC):
                kb.op("dve", lambda e: e.tensor_scalar(out=hT[:, kc, 0:nt], in0=XT[:, kc, 0:nt], scalar1=hs[:, kc:kc + 1], scalar2=hb[:, kc:kc + 1], op0=ALU.mult, op1=ALU.add), [XT, hs, hb], [hT])
            kb.dma(XRES[q][:, 0:nt].rearrange("(kc p) t -> p kc t", p=128), XT[:, :, 0:nt], R=[XT], W=[kb.DR("XRES")])
            kb.dma(WB[:], Win_b[:, C_AQ:C_AQ + 1024].rearrange("(kc p) n -> p kc n", p=128), W=[WB])
            for h in range(NH):
                pt = kb.ps()
                for kc in range(KC):
                    kb.op("pe", lambda e: e.matmul(pt[:, 0:nt], lhsT=WB[:, kc, h * 128:(h + 1) * 128], rhs=hT[:, kc, 0:nt], start=(kc == 0), stop=(kc == KC - 1)), [WB, hT], [pt])
                i2 = h % 2
                qv = TT(QT.t[:, h, :], "QTv")
                qv.r = QT.r
                rope_norm(kb, C, pt, nt, qk, 0, cs, sq[i2], xw[i2], k1[i2], k2[i2], rs_[i2], qv, 128 ** -0.5)
            kb.dma(WB[:], Win_b[:, C_Z:C_Z + 1024].rearrange("(kc p) n -> p kc n", p=128), W=[WB])
            zs = kb.sbn("zs", 1, [128, 1024], F32) * 2
            cand = kb.sbn("cand", 2, [128, 1024], F32) * 2
            osel = kb.sbn("osel", 1, [128, 1024], F32) * 2
            osq = kb.sb("osq", [128, 1024], F32)
            ss = kb.sbn("ss", 2, [128, 8], F32)
            ogb = kb.sbn("ogb", 2, [128, 1024], BF16)
            for ti in range(nq):
                z_, os_, s_, og_ = zs[ti % 2], osel[ti % 2], ss[ti % 2], ogb[ti % 2]
                for hf in range(2):
                    pt = kb.ps()
                    for kc in range(KC):
                        kb.op("pe", lambda e: e.matmul(pt[:], lhsT=hT[:, kc, ti * 128:(ti + 1) * 128], rhs=WB[:, kc, hf * 512:(hf + 1) * 512], start=(kc == 0), stop=(kc == KC - 1)), [WB, hT], [pt])
                    kb.op("act", lambda e: e.activation(out=z_[:, hf * 512:(hf + 1) * 512], in_=pt[:], func=AF.Silu), [pt], [z_])
                ee = e0 + ti * 128
                if ee < Lq:
                    cl = [(j, j * Lq + ee) for j in range(4)]
                elif ee == Lq:
                    cl = [(j, j * Lq - 128) for j in range(1, 4)]
                else:
                    cl = [(j, (j + 1) * Lq) for j in range(0, 3)]
                ci = 0
                for (j, row) in cl:
                    for src_ in (OSUM, OSUMB):
                        kb.dma(cand[ci % 2][:], src_[q][row:row + 128, :], W=[cand[ci % 2]])
                        if ci == 0:
                            kb.op("dve", lambda e: e.tensor_scalar(out=os_[:], in0=cand[ci % 2][:], scalar1=oh[:, j:j + 1], scalar2=None, op0=ALU.mult), [cand[ci % 2], oh], [os_])
                        else:
                            kb.op("dve", lambda e: e.scalar_tensor_tensor(out=os_[:], in0=cand[ci % 2][:], scalar=oh[:, j:j + 1], in1=os_[:], op0=ALU.mult, op1=ALU.add), [cand[ci % 2], oh, os_], [os_])
                        ci += 1
                kb.op("dve", lambda e: e.tensor_tensor(out=osq[:], in0=os_[:], in1=os_[:], op=ALU.mult), [os_], [osq])
                kb.op("dve", lambda e: e.reduce_sum(out=s_[:], in_=osq[:].rearrange("p (h d) -> p h d", h=8), axis=AX.X), [osq], [s_])
                kb.op("dve", lambda e: e.tensor_scalar(out=s_[:], in0=s_[:], scalar1=1.0 / 128, scalar2=NORM_EPS, op0=ALU.mult, op1=ALU.add), [s_], [s_])
                kb.op("act", lambda e: e.activation(out=s_[:], in_=s_[:], func=AF.Sqrt), [s_], [s_])
                kb.op("dve", lambda e: e.reciprocal(out=s_[:], in_=s_[:]), [s_], [s_])
                v3 = lambda a: a[:].rearrange("p (h d) -> p h d", h=8)
                kb.op("dve", lambda e: e.tensor_tensor(out=v3(os_), in0=v3(os_), in1=s_[:].unsqueeze(2).to_broadcast([128, 8, 128]), op=ALU.mult), [os_, s_], [os_])
                kb.op("dve", lambda e: e.tensor_tensor(out=v3(os_), in0=v3(os_), in1=gnw[:].unsqueeze(1).to_broadcast([128, 8, 128]), op=ALU.mult), [os_, gnw], [os_])
                kb.op("dve", lambda e: e.tensor_tensor(out=og_[:], in0=os_[:], in1=z_[:], op=ALU.mult), [os_, z_], [og_])
                for hp in range(2):
                    pb_ = kb.ps()
                    pbb = pb_[:].bitcast(BF16)
                    for k in range(4):
                        h = hp * 4 + k
                        kb.op("pe", lambda e: e.transpose(out=pbb[:, k * 128:(k + 1) * 128], in_=og_[:, h * 128:(h + 1) * 128], identity=C["ident_b"][:]), [og_, C["ident_b"]], [pb_])
                    kb.op("dve", lambda e: e.tensor_copy(out=OT[:, hp * 4:(hp + 1) * 4, ti * 128:(ti + 1) * 128], in_=pbb[:, 0:512].rearrange("p (k t) -> p k t", k=4)), [pb_], [OT])
            G = min(16, Tq // 128)
            ng = Tq // (128 * G)
            ktb = kb.sbn("ktb", 2, [128, G * 128], BF16)
            vab = kb.sbn("vab", 2, [128, G, 129], BF16)
            for v_ in vab:
                kb.op("dve", lambda e: e.memset(v_[:, :, 128:129], 1.0), [], [v_])
            ptb = kb.sbn("ptb", 4, [128, 512], BF16)
            rden = kb.sbn("rden", 2, [128, 1], F32)
            onb = kb.sbn("onb", 2, [128, 128], BF16)
            kb._psn = 5
            ACC = kb.PS[5:8]
            gi_ = 0
            pi_ = 0
            for pr in range(4):
                kvh = pr // 2
                first = [True, True, True]
                pend = []
                for kg in range(ng):
                    kt_, va_ = ktb[gi_ % 2], vab[gi_ % 2]
                    gi_ += 1
                    kb.dma(kt_[:], KTa[q][kvh][:, kg * G * 128:(kg + 1) * G * 128], W=[kt_])
                    kb.dma(va_[:, :, 0:128], Va[q][kvh][:, kg * G:(kg + 1) * G, :], W=[va_])
                    for kc in range(G):
                        for hh in range(2):
                            head = pr * 2 + hh
                            pt = kb.ps()
                            kb.op("pe", lambda e: e.matmul(pt[:, 0:nt], lhsT=kt_[:, kc * 128:(kc + 1) * 128], rhs=QT[:, head, 0:nt], start=True, stop=True), [kt_, QT], [pt])
                            p_ = ptb[pi_ % 4]
                            pi_ += 1
                            kb.op("act", lambda e: e.activation(out=p_[:, 0:nt], in_=pt[:, 0:nt], func=AF.Exp), [pt], [p_])

                            def pv(p_=p_, va_=va_, kc=kc, hh=hh, lastk=(kg == ng - 1 and kc == G - 1)):
                                for qi in range(nq):
                                    sl = hh * nq + qi
                                    bank, off = sl // 3, (sl % 3) * 129
                                    st_ = first[bank]
                                    first[bank] = False
                                    kb.op("pe", lambda e: e.matmul(ACC[bank][:, off:off + 129], lhsT=p_[:, qi * 128:(qi + 1) * 128], rhs=va_[:, kc, :],
                                                                   start=st_, stop=lastk, skip_group_check=True), [p_, va_], [ACC[bank]])
                            if pend:
                                pend.pop()()
                            pend.append(pv)
                if pend:
                    pend.pop()()
                for hh in range(2):
                    head = pr * 2 + hh
                    pb_ = kb.ps()
                    pbb = pb_[:].bitcast(BF16)
                    for qi in range(nq):
                        sl = hh * nq + qi
                        bank, off = sl // 3, (sl % 3) * 129
                        r_, o_ = rden[sl % 2], onb[sl % 2]
                        kb.op("dve", lambda e: e.reciprocal(out=r_[:], in_=ACC[bank][:, off + 128:off + 129]), [ACC[bank]], [r_])
                        kb.op("dve", lambda e: e.tensor_scalar(out=o_[:], in0=ACC[bank][:, off:off + 128], scalar1=r_[:, 0:1], scalar2=None, op0=ALU.mult), [ACC[bank], r_], [o_])
                        kb.op("pe", lambda e: e.transpose(out=pbb[:, qi * 128:(qi + 1) * 128], in_=o_[:], identity=C["ident_b"][:]), [o_, C["ident_b"]], [pb_])
                    kb.op("dve", lambda e: e.tensor_copy(out=OT[:, 8 + head, 0:nt], in_=pbb[:, 0:nt]), [pb_], [OT])
            kb._psn = 8
            kb.dma(OTS[q][:, 0:nt].rearrange("(kc p) t -> p kc t", p=128), OT[:, :, 0:nt], R=[OT], W=[kb.DR("OTS")])

    def phase_B2(q, xcols, nt, uid):
        with kb.phase("B2_%d_%d" % (q, uid)):
            cols = load_cols(0, q)
            cd, g4, b4 = cols["cd"], cols["g4"], cols["b4"]
            C = load_consts(["ones"])
            XT = kb.sb("XT", [128, KC, 512], F32)
            OT = kb.sb("OT", [128, KC, 512], BF16)
            hT = kb.sb("hT", [128, KC, 512], BF16)
            kb.dma(XT[:, :, 0:nt], XRES[q][:, 0:nt].rearrange("(kc p) t -> p kc t", p=128), W=[XT])
            kb.dma(OT[:, :, 0:nt], OTS[q][:, 0:nt].rearrange("(kc p) t -> p kc t", p=128), W=[OT])
            wob = kb.sbn("wob", 2, [128, KC, 512], BF16)
            for og in range(4):
                w = wob[og % 2]
                kb.dma(w[:], Wout_b[:, og * 512:(og + 1) * 512].rearrange("(kc p) n -> p kc n", p=128), W=[w])
                for oc in range(4):
                    o = og * 4 + oc
                    pt = kb.ps()
                    for kc in range(KC):
                        kb.op("pe", lambda e: e.matmul(pt[:, 0:nt], lhsT=w[:, kc, oc * 128:(oc + 1) * 128], rhs=OT[:, kc, 0:nt], start=(kc == 0), stop=(kc == KC - 1)), [w, OT], [pt])
                    kb.op("dve", lambda e: e.tensor_scalar(out=XT[:, o, 0:nt], in0=XT[:, o, 0:nt], scalar1=ALPHA, scalar2=None, op0=ALU.mult), [XT], [XT])
                    kb.op("dve", lambda e: e.scalar_tensor_tensor(out=XT[:, o, 0:nt], in0=pt[:, 0:nt], scalar=cd[:, 32 + o:33 + o], in1=XT[:, o, 0:nt], op0=ALU.mult, op1=ALU.add), [pt, cd, XT], [XT])
            layernorm(XT, nt, C)
            ffn_part(0, XT, hT, nt, cols, C, uid)
            for kc in range(KC):
                kb.op("dve", lambda e: e.tensor_scalar(out=XT[:, kc, 0:nt], in0=XT[:, kc, 0:nt], scalar1=g4[:, 1, kc:kc + 1], scalar2=b4[:, 1, kc:kc + 1], op0=ALU.mult, op1=ALU.add), [XT, g4, b4], [XT])
            for (c0, t0_, n_) in xcols:
                kb.dma(X1T[q][:, c0:c0 + n_].rearrange("(kc p) t -> p kc t", p=128), XT[:, :, t0_:t0_ + n_], R=[XT], W=[kb.DR("X1T", c0)])

    def phase_C(q, blk, nt, uid):
        Lq = L[q]
        c0 = 128 + blk * 512
        n = nt + 16
        with kb.phase("C_%d_%d" % (q, uid)):
            cols = load_cols(1, q)
            cd, g4, b4 = cols["cd"], cols["g4"], cols["b4"]
            hs, hb = mod_cols(cols, "c", 0)
            C = load_consts(["ident", "ones"])
            psc = kb.sb("psc", [128, KC], F32)
            kb.dma(psc[:], pscale_col, W=[psc])
            gp = kb.sb("gp", [128, KC], F32)
            kb.op("dve", lambda e: e.tensor_tensor(out=gp[:], in0=psc[:], in1=cd[:, 32:48], op=ALU.mult), [psc, cd], [gp])
            vr = kb.sb("vr", [128, 528], F32)
            kb.dma(vr[:, 0:n], validr[q][:, c0 - 8:c0 - 8 + n].partition_broadcast(128), W=[vr])
            ic = kb.sb("ic", [128, 4, 512], F32)
            for gi in range(4):
                kb.dma(ic[:, gi, 0:nt], invcnt[q][gi:gi + 1, c0:c0 + nt].partition_broadcast(128), W=[ic])
            pw = kb.sb("pw", [128, 4, 4, 512], BF16)
            for gi in range(4):
                kb.dma(pw[:, gi], Pool_b[gi * 512:(gi + 1) * 512, :].rearrange("(kc p) n -> p kc n", p=128), W=[pw])
            XT = kb.sb("XT", [128, KC, 512], F32)
            hT = kb.sb("hT", [128, KC, 512], BF16)
            xh = kb.sbn("xh", 2, [128, 528], F32)
            hm = kb.sbn("hm", 2, [128, 528], F32)
            A_ = kb.sbn("pA", 1, [128, 528], F32) * 2
            B_ = kb.sbn("pB", 1, [128, 528], F32) * 2
            for kc in range(KC):
                gi = kc // 4
                x_, h_, a_, b_ = xh[kc % 2], hm[kc % 2], A_[kc % 2], B_[kc % 2]
                kb.dma(x_[:, 0:n], X1T[q][kc * 128:(kc + 1) * 128, c0 - 8:c0 - 8 + n], W=[x_])
                kb.op("dve", lambda e: e.tensor_scalar(out=XT[:, kc, 0:nt], in0=x_[:, 8:8 + nt], scalar1=ALPHA, scalar2=None, op0=ALU.mult), [x_], [XT])
                kb.op("dve", lambda e: e.tensor_scalar(out=h_[:, 0:n], in0=x_[:, 0:n], scalar1=hs[:, kc:kc + 1], scalar2=hb[:, kc:kc + 1], op0=ALU.mult, op1=ALU.add), [x_, hs, hb], [h_])
                kb.op("dve", lambda e: e.tensor_tensor(out=h_[:, 0:n], in0=h_[:, 0:n], in1=vr[:, 0:n], op=ALU.mult), [h_, vr], [h_])
                kb.op("dve", lambda e: e.tensor_tensor(out=a_[:, 1:n], in0=h_[:, 1:n], in1=h_[:, 0:n - 1], op=ALU.add), [h_], [a_])
                src = a_
                if gi >= 1:
                    kb.op("dve", lambda e: e.tensor_tensor(out=b_[:, 2:n - 1], in0=a_[:, 1:n - 2], in1=a_[:, 3:n], op=ALU.add), [a_], [b_])
                    src = b_
                if gi >= 2:
                    kb.op("dve", lambda e: e.tensor_tensor(out=a_[:, 4:n - 3], in0=b_[:, 2:n - 5], in1=b_[:, 6:n - 1], op=ALU.add), [b_], [a_])
                    src = a_
                if gi >= 3:
                    kb.op("dve", lambda e: e.tensor_tensor(out=b_[:, 8:n - 7], in0=a_[:, 4:n - 11], in1=a_[:, 12:n - 3], op=ALU.add), [a_], [b_])
                    src = b_
                dst = a_ if src is b_ else b_
                kb.op("dve", lambda e: e.tensor_tensor(out=dst[:, 8:8 + nt], in0=src[:, 8:8 + nt], in1=ic[:, gi, 0:nt], op=ALU.mult), [src, ic], [dst])
                kb.op("dve", lambda e: e.tensor_tensor(out=hT[:, kc, 0:nt], in0=dst[:, 8:8 + nt], in1=h_[:, 8:8 + nt], op=ALU.subtract), [dst, h_], [hT])
            for gi in range(4):
                for oc in range(4):
                    o = gi * 4 + oc
                    pt = kb.ps()
                    for k4 in range(4):
                        kb.op("pe", lambda e: e.matmul(pt[:, 0:nt], lhsT=pw[:, gi, k4, oc * 128:(oc + 1) * 128], rhs=hT[:, gi * 4 + k4, 0:nt], start=(k4 == 0), stop=(k4 == 3)), [pw, hT], [pt])
                    kb.op("dve", lambda e: e.scalar_tensor_tensor(out=XT[:, o, 0:nt], in0=pt[:, 0:nt], scalar=gp[:, o:o + 1], in1=XT[:, o, 0:nt], op0=ALU.mult, op1=ALU.add), [pt, gp, XT], [XT])
            layernorm(XT, nt, C)
            ffn_part(1, XT, hT, nt, cols, C, uid)
            for kc in range(KC):
                kb.op("dve", lambda e: e.tensor_scalar(out=XT[:, kc, 0:nt], in0=XT[:, kc, 0:nt], scalar1=g4[:, 3, kc:kc + 1], scalar2=b4[:, 3, kc:kc + 1], op0=ALU.mult, op1=ALU.add), [XT, g4, b4], [XT])
            yt = kb.sbn("yt", 1, [128, D], F32) * 2
            for ti in range(nt // 128):
                y_ = yt[ti % 2]
                for kc4 in range(4):
                    pt = kb.ps()
                    for k in range(4):
                        kc = kc4 * 4 + k
                        kb.op("pe", lambda e: e.transpose(out=pt[:, k * 128:(k + 1) * 128], in_=XT[:, kc, ti * 128:(ti + 1) * 128], identity=C["ident"][:]), [XT, C["ident"]], [pt])
                    kb.op("act", lambda e: e.activation(out=y_[:, kc4 * 512:(kc4 + 1) * 512], in_=pt[:], func=AF.Copy), [pt], [y_])
                r0 = blk * 512 + ti * 128
                kb.dma(y_out[q][r0:r0 + 128, :], y_[:], R=[y_], W=[kb.DR("y", (q, r0))])

    uid = 0
    for q in range(2):
        Lq = L[q]
        bs = min(512, Lq)
        for blk in range(Lq // bs):
            uid += 1
            phase_B1(q, blk * bs, bs, uid)
            phase_B2(q, [(128 + blk * bs, 0, bs)], bs, uid)
        uid += 1
        phase_B1(q, Lq, 256, uid)
        phase_B2(q, [(0, 0, 128), (128 + Lq, 128, 128)], 256, uid)
    if stop == "B":
        return kb
    for q in range(2):
        Lq = L[q]
        bs = min(512, Lq)
        for blk in range(Lq // bs):
            uid += 1
            phase_C(q, blk, bs, uid)
    return kb


def _col(v):
    v = np.asarray(v, np.float32)
    return np.ascontiguousarray(v.reshape(-1, 128).T)


def rope_tables(pos):
    pos = np.asarray(pos)
    row = (pos // 64).astype(np.float32)
    col = (pos % 64).astype(np.float32)
    half = 64
    inv_freq = (np.float32(10000.0) ** (-np.arange(0, half, 2, dtype=np.float32) / np.float32(half))).astype(np.float32)
    ar = row[:, None] * inv_freq
    ac = col[:, None] * inv_freq
    ang = np.concatenate([ar, ar, ac, ac], -1).astype(np.float32)
    return np.cos(ang).astype(np.float32), np.sin(ang).astype(np.float32)


def make_consts():
    i = np.arange(128)
    ident = np.eye(128, dtype=np.float32)
    ones = np.ones((128, 128), np.float32)
    tri_f = (i[:, None] <= i[None, :]).astype(np.float32)
    tri_b = (i[:, None] >= i[None, :]).astype(np.float32)
    madd_f = np.where(i[:, None] <= i[None, :], 0.0, NEGBIG).astype(np.float32)
    madd_b = np.where(i[:, None] >= i[None, :], 0.0, NEGBIG).astype(np.float32)
    strict_f = (i[:, None] < i[None, :]).astype(np.float32)
    strict_b = (i[:, None] > i[None, :]).astype(np.float32)
    rot = np.zeros((128, 128), np.float32)
    for k in range(32):
        rot[32 + k, k] = -1.0
        rot[k, 32 + k] = 1.0
        rot[96 + k, 64 + k] = -1.0
        rot[64 + k, 96 + k] = 1.0
    c = np.stack([ident, ones, tri_f, tri_b, madd_f, madd_b, strict_f, strict_b, rot], 1)
    return np.ascontiguousarray(c)


def ext_positions(Tq, Lq, s):
    own = np.arange(s * Lq, (s + 1) * Lq)
    left = np.arange(s * Lq - 128, s * Lq)
    right = np.arange((s + 1) * Lq, (s + 1) * Lq + 128)
    return own, left, right


def prepare_inputs(cfg, inp):
    T, L, E = cfg.T, cfg.L, cfg.E
    xs = [np.asarray(inp["x_sample"], np.float32), np.asarray(inp["x_prompt"], np.float32)]
    cs = [np.asarray(inp["c_sample"], np.float32), np.asarray(inp["c_prompt"], np.float32)]
    consts = make_consts()
    w_in = np.ascontiguousarray(np.asarray(inp["w_in"], np.float32)[0])
    conv = np.asarray(inp["conv_w"], np.float32)[0]
    convcol = np.ascontiguousarray(conv.T.reshape(24, 128, 5).transpose(1, 0, 2))
    ada_b = np.asarray(inp["ada_b"], np.float32)
    ada_bcol = np.ascontiguousarray(ada_b.reshape(2, 96, 128).transpose(2, 0, 1))
    lng = np.asarray(inp["ln_g"], np.float32).reshape(4, KC, 128).transpose(2, 0, 1)
    lnb = np.asarray(inp["ln_b"], np.float32).reshape(4, KC, 128).transpose(2, 0, 1)
    common = dict(
        ada_w=np.ascontiguousarray(np.asarray(inp["ada_w"], np.float32)),
        ada_bcol=ada_bcol, w_in=w_in,
        w_out=np.ascontiguousarray(np.asarray(inp["w_out"], np.float32)[0]),
        pool_w=np.ascontiguousarray(np.asarray(inp["pool_w"], np.float32)[0].reshape(2048, 512)),
        mlp_w1=np.ascontiguousarray(np.asarray(inp["mlp_w1"], np.float32)),
        mlp_w2=np.ascontiguousarray(np.asarray(inp["mlp_w2"], np.float32)),
        convcol=convcol,
        alog_row=np.ascontiguousarray(np.asarray(inp["a_log"], np.float32)[0].reshape(1, 16)),
        dtb_row=np.ascontiguousarray(np.asarray(inp["dt_bias"], np.float32)[0].reshape(1, 16)),
        gnw_row=np.ascontiguousarray(np.asarray(inp["gdn_norm_w"], np.float32)[0].reshape(1, 128)),
        qkw_col=np.ascontiguousarray(np.stack([np.asarray(inp["q_norm_w"], np.float32)[0], np.asarray(inp["k_norm_w"], np.float32)[0]], 1)),
        pscale_col=_col(np.asarray(inp["pool_scale"], np.float32)[0]),
        lng_col=np.ascontiguousarray(lng), lnb_col=np.ascontiguousarray(lnb),
        cst=consts,
    )
    ropeK = []
    for q in range(2):
        c, s_ = rope_tables(np.arange(T[q]))
        ropeK.append(np.ascontiguousarray(np.stack([c.T, s_.T], 0)))
    maps = []
    for core in range(8):
        g, s = core // 4, core % 4
        m = dict(common)
        m["ccol"] = np.ascontiguousarray(np.stack([_col(cs[0][g]), _col(cs[1][g])], 2))
        oh = np.zeros((128, 4), np.float32)
        oh[:, s] = 1.0
        m["onehot"] = oh
        for q in range(2):
            Tq, Lq = T[q], L[q]
            m["xf%d" % q] = np.ascontiguousarray(xs[q][g])
            own, left, right = ext_positions(Tq, Lq, s)
            pos = np.concatenate([own, left, right])
            ok = (pos >= 0) & (pos < Tq)
            xe = np.zeros((E[q], D), np.float32)
            xe[ok] = xs[q][g][pos[ok]]
            m["xe%d" % q] = xe
            m["ropeK%d" % q] = ropeK[q]
            c, s_ = rope_tables(np.clip(pos, 0, Tq - 1))
            m["ropeQ%d" % q] = np.ascontiguousarray(np.stack([c.T, s_.T], 0))
            posn = np.concatenate([left, own, right])
            okn = ((posn >= 0) & (posn < Tq)).astype(np.float32)
            m["valid%d" % q] = np.ascontiguousarray(okn.reshape(1, -1))
            ic = np.zeros((4, E[q]), np.float32)
            pc = np.clip(posn, 0, Tq - 1)
            for gi, win in enumerate(POOL_WINDOWS):
                lo = np.clip(pc - win // 2, 0, Tq - 1)
                hi = np.clip(pc + (win - 1 - win // 2), 0, Tq - 1)
                ic[gi] = 1.0 / (hi - lo + 1).astype(np.float32)
            m["invcnt%d" % q] = ic
        maps.append(m)
    return maps


_CACHE = {}


def kernel(**inputs):
    cfg = Cfg()
    kb = build_program(cfg)
    maps = prepare_inputs(cfg, inputs)
    maps = [{k: v for k, v in m.items() if k in kb.inputs} for m in maps]
    res = run_bass_kernel_spmd(kb.nc, maps, core_ids=list(range(8)))
    ys = np.zeros((2, cfg.T[0], D), np.float32)
    yp = np.zeros((2, cfg.T[1], D), np.float32)
    for core in range(8):
        g, s = core // 4, core % 4
        r = res.results[core]
        ys[g, s * cfg.L[0]:(s + 1) * cfg.L[0]] = r["y0"]
        yp[g, s * cfg.L[1]:(s + 1) * cfg.L[1]] = r["y1"]
    return (yp, ys)
```

```python
import contextlib
import math
import numpy as np
import concourse.bass as bass
import concourse.mybir as mybir
from concourse.bass_utils import run_bass_kernel_spmd

F32 = mybir.dt.float32
BF16 = mybir.dt.bfloat16
AF = mybir.ActivationFunctionType
ALU = mybir.AluOpType
AX = mybir.AxisListType

D = 2048
KC = 16
DFF = 8192
FC = 64
DIN = 5664
HD = 128
NH = 8
ALPHA = 4 ** 0.25
NORM_EPS = 1e-6
LN_EPS = 1e-5
NEGBIG = -60000.0
POOL_WINDOWS = (2, 4, 8, 16)
C_GQ, C_GK, C_GV, C_Z, C_B, C_A, C_AQ, C_AK, C_AV = 0, 1024, 2048, 3072, 4096, 4112, 4128, 5152, 5408


class Cfg:
    def __init__(self, Ts=16384, Tp=4096, debug=False, stop_after=None):
        self.T = [Ts, Tp]
        self.L = [Ts // 4, Tp // 4]
        self.E = [l + 256 for l in self.L]
        self.debug = debug
        self.stop_after = stop_after


class Res:
    __slots__ = ("name", "lw", "rd")

    def __init__(self, name=""):
        self.name = name
        self.lw = None
        self.rd = []


class TT:
    def __init__(self, t, name):
        self.t = t
        self.r = Res(name)

    def __getitem__(self, k):
        return self.t[k]


def _res(x):
    return x.r if isinstance(x, TT) else x


class _Rec:
    def __getattr__(self, name):
        def f(*a, **k):
            self.call = (name, a, k)
            return self
        return f


class Sched:
    NDS = 24

    def __init__(self, nc, st):
        self.nc = nc
        self.csem = {e: st.enter_context(nc.semaphore("c_" + e)) for e in ("pe", "act", "dve", "pool")}
        self.dsem = {(q, j): st.enter_context(nc.semaphore("d_%s_%d" % (q, j))) for q in ("sp", "act") for j in range(self.NDS)}
        self.cnt = {e: 0 for e in ("pe", "act", "dve", "pool")}
        self.dcnt = {q: 0 for q in ("sp", "act")}
        self.ops = []
        self.nres = []

    def begin(self):
        self.ops = []

    def add(self, eng, fn, reads=(), writes=(), dma=False):
        rec = _Rec()
        fn(rec)
        name_, a_, k_ = rec.call
        fn = lambda e, name_=name_, a_=a_, k_=k_: getattr(e, name_)(*a_, **k_)
        i = len(self.ops)
        deps = set()
        for r in reads:
            r = _res(r)
            if r.lw is not None:
                deps.add(r.lw)
        for w in writes:
            w = _res(w)
            if w.lw is not None:
                deps.add(w.lw)
            deps.update(w.rd)
        deps.discard(i)
        for r in reads:
            _res(r).rd.append(i)
        for w in writes:
            w = _res(w)
            w.lw = i
            w.rd = []
        self.ops.append(dict(eng=eng, fn=fn, deps=deps, dma=dma, sig=False))
        self.nres.extend(_res(x) for x in reads)
        self.nres.extend(_res(x) for x in writes)
        return i

    def emit(self):
        nc, ops = self.nc, self.ops
        for o in ops:
            nd = set()
            for d in o["deps"]:
                p = ops[d]
                if (not p["dma"]) and (not o["dma"]) and p["eng"] == "pe" and o["eng"] == "pe":
                    continue
                nd.add(d)
            o["deps"] = nd
        for o in ops:
            for d in o["deps"]:
                ops[d]["sig"] = True
        last = {}
        for i, o in enumerate(ops):
            if not o["dma"]:
                last[o["eng"]] = i
        for i in last.values():
            ops[i]["sig"] = True
        cnt, dcnt = self.cnt, self.dcnt
        for o in ops:
            if o["dma"]:
                q = o["eng"]
                k = dcnt[q]
                dcnt[q] += 1
                o["dsem"] = (q, k % self.NDS)
                o["dval"] = 16 * (k // self.NDS + 1)
            elif o["sig"]:
                cnt[o["eng"]] += 1
                o["val"] = cnt[o["eng"]]
        csem, dsem = self.csem, self.dsem
        engobj = {"pe": nc.tensor, "act": nc.scalar, "dve": nc.vector, "pool": nc.gpsimd, "sp": nc.sync}
        fin_c = dict(cnt)
        fin_d = dict(dcnt)

        def run(eng):
            e = engobj[eng]
            seen, dseen = {}, {}
            for o in ops:
                if o["eng"] != eng:
                    continue
                need, dneed = {}, {}
                for d in o["deps"]:
                    p = ops[d]
                    if p["dma"]:
                        dneed[p["dsem"]] = max(dneed.get(p["dsem"], 0), p["dval"])
                    else:
                        need[p["eng"]] = max(need.get(p["eng"], 0), p["val"])
                if o["dma"] and o["dval"] > 16:
                    dneed[o["dsem"]] = max(dneed.get(o["dsem"], 0), o["dval"] - 16)
                for pe_, v in need.items():
                    if seen.get(pe_, 0) < v:
                        e.wait_ge(csem[pe_], v)
                        seen[pe_] = v
                for ds_, v in dneed.items():
                    if dseen.get(ds_, 0) < v:
                        e.wait_ge(dsem[ds_], v)
                        dseen[ds_] = v
                ins = o["fn"](e)
                if o["dma"]:
                    ins.then_inc(dsem[o["dsem"]], 16)
                elif o["sig"]:
                    ins.then_inc(csem[eng], 1)
            if eng == "sp":
                for q in ("sp", "act"):
                    k = fin_d[q]
                    for j in range(self.NDS):
                        n = k // self.NDS + (1 if (k % self.NDS) > j else 0)
                        if n:
                            e.wait_ge(dsem[(q, j)], 16 * n)
                for ce in ("pe", "act", "dve", "pool"):
                    if fin_c[ce]:
                        e.wait_ge(csem[ce], fin_c[ce])

        with nc.Block() as block:
            @block.sync
            def _(x):
                run("sp")

            @block.tensor
            def _(x):
                run("pe")

            @block.scalar
            def _(x):
                run("act")

            @block.vector
            def _(x):
                run("dve")

            @block.gpsimd
            def _(x):
                run("pool")
        for r in self.nres:
            r.lw = None
            r.rd = []
        self.nres = []
        self.ops = []


class KB:
    def __init__(self, cfg):
        self.cfg = cfg
        self.nc = bass.Bass("TRN2", target_bir_lowering=False)
        self.gst = contextlib.ExitStack()
        self.S = Sched(self.nc, self.gst)
        self.st = None
        self.dres = {}
        self._alt = 0
        self._psi = 0
        self._psn = 8
        self._uid = 0
        self.inputs = {}
        self.outputs = {}

    def din(self, name, shape, dt=F32):
        ap = self.nc.dram_tensor(name, list(shape), dt, kind="ExternalInput").ap()
        self.inputs[name] = ap
        return ap

    def dout(self, name, shape, dt=F32):
        ap = self.nc.dram_tensor(name, list(shape), dt, kind="ExternalOutput").ap()
        self.outputs[name] = ap
        return ap

    def dscr(self, name, shape, dt):
        if self.cfg.debug:
            return self.dout("dbg_" + name, shape, dt)
        return self.nc.dram_tensor(name, list(shape), dt).ap()

    def DR(self, name, idx=0):
        k = (name, idx)
        if k not in self.dres:
            self.dres[k] = Res("%s_%s" % (name, idx))
        return self.dres[k]

    def sb(self, name, shape, dt):
        self._uid += 1
        return TT(self.st.enter_context(self.nc.sbuf_tensor("%s_u%d" % (name, self._uid), list(shape), dt)), name)

    def sbn(self, name, n, shape, dt):
        return [self.sb("%s%d" % (name, i), shape, dt) for i in range(n)]

    @contextlib.contextmanager
    def phase(self, name):
        self.S.begin()
        self._psi = 0
        with contextlib.ExitStack() as st:
            self.st = st
            self._uid += 1
            self.PS = [TT(st.enter_context(self.nc.psum_tensor("ps%d_u%d" % (i, self._uid), [128, 512], F32)), "ps%d" % i) for i in range(8)]
            epsc = self.sb("epsc", [128, 2], F32)
            self.op("dve", lambda e: e.memset(epsc[:, 0:1], NORM_EPS), [], [epsc])
            self.op("dve", lambda e: e.memset(epsc[:, 1:2], LN_EPS), [], [epsc])
            self.epsc = epsc
            yield
            self.S.emit()
        self.st = None

    def ps(self):
        p = self.PS[self._psi % self._psn]
        self._psi += 1
        return p

    def alt(self):
        self._alt += 1
        return "act" if self._alt % 2 else "dve"

    def op(self, eng, fn, R=(), W=()):
        return self.S.add(eng, fn, R, W)

    def dma(self, out, in_, R=(), W=(), q="sp"):
        return self.S.add(q, lambda e: e.dma_start(out=out, in_=in_), R, W, dma=True)

    def copy(self, eng, out, in_, R, W, scale=None):
        if eng == "act":
            if scale is None:
                self.op("act", lambda e: e.activation(out=out, in_=in_, func=AF.Copy), R, W)
            else:
                self.op("dve", lambda e: e.tensor_scalar(out=out, in0=in_, scalar1=scale, scalar2=None, op0=ALU.mult), R, W)
        else:
            if scale is None:
                self.op(eng, lambda e: e.tensor_copy(out=out, in_=in_), R, W)
            else:
                self.op(eng, lambda e: e.tensor_scalar(out=out, in0=in_, scalar1=scale, scalar2=None, op0=ALU.mult), R, W)


def build_program(cfg):
    kb = KB(cfg)
    nc = kb.nc
    T, L, E = cfg.T, cfg.L, cfg.E
    xf = [kb.din("xf%d" % q, [T[q], D]) for q in range(2)]
    xe = [kb.din("xe%d" % q, [E[q], D]) for q in range(2)]
    ccol = kb.din("ccol", [128, KC, 2])
    ada_w = kb.din("ada_w", [2, D, 6 * D])
    ada_bcol = kb.din("ada_bcol", [128, 2, 96])
    w_in = kb.din("w_in", [D, DIN])
    w_out = kb.din("w_out", [D, D])
    pool_w = kb.din("pool_w", [4 * 512, 512])
    mlp_w1 = kb.din("mlp_w1", [2, D, DFF])
    mlp_w2 = kb.din("mlp_w2", [2, DFF, D])
    convcol = kb.din("convcol", [128, 24, 5])
    alog_row = kb.din("alog_row", [1, 16])
    dtb_row = kb.din("dtb_row", [1, 16])
    gnw_row = kb.din("gnw_row", [1, 128])
    qkw_col = kb.din("qkw_col", [128, 2])
    pscale_col = kb.din("pscale_col", [128, KC])
    lng_col = kb.din("lng_col", [128, 4, KC])
    lnb_col = kb.din("lnb_col", [128, 4, KC])
    cst = kb.din("cst", [128, 9, 128])
    ropeK = [kb.din("ropeK%d" % q, [2, 128, T[q]]) for q in range(2)]
    ropeQ = [kb.din("ropeQ%d" % q, [2, 128, E[q]]) for q in range(2)]
    onehot = kb.din("onehot", [128, 4])
    validr = [kb.din("valid%d" % q, [1, E[q]]) for q in range(2)]
    invcnt = [kb.din("invcnt%d" % q, [4, E[q]]) for q in range(2)]
    y_out = [kb.dout("y%d" % q, [L[q], D]) for q in range(2)]

    Win_b = kb.dscr("Win_b", [D, DIN], BF16) if False else nc.dram_tensor("Win_b", [D, DIN], BF16).ap()
    Wout_b = nc.dram_tensor("Wout_b", [D, D], BF16).ap()
    Pool_b = nc.dram_tensor("Pool_b", [4 * 512, 512], BF16).ap()
    W1_b = nc.dram_tensor("W1_b", [2, D, DFF], BF16).ap()
    W2_s = nc.dram_tensor("W2_s", [2, 16, 128, FC * 128], BF16).ap()
    COND = kb.dscr("COND", [128, 2, 2, 96], F32)
    PRE = [kb.dscr("PRE%d" % q, [3072, T[q]], F32) for q in range(2)]
    BGR = [kb.dscr("BGR%d" % q, [T[q], 32], F32) for q in range(2)]
    BG = [kb.dscr("BG%d" % q, [T[q], 32], F32) for q in range(2)]
    KTa = [kb.dscr("KTa%d" % q, [2, 128, T[q]], BF16) for q in range(2)]
    Va = [kb.dscr("Va%d" % q, [2, 128, T[q] // 128, 128], BF16) for q in range(2)]
    QTg = [kb.dscr("QTg%d" % q, [T[q] // 128, 128, NH, 128], BF16) for q in range(2)]
    KTg = [kb.dscr("KTg%d" % q, [T[q] // 128, 128, NH, 128], BF16) for q in range(2)]
    Ktok = [kb.dscr("Ktok%d" % q, [T[q], 1024], BF16) for q in range(2)]
    Vtok = [kb.dscr("Vtok%d" % q, [T[q], 1024], F32) for q in range(2)]
    OSUM = [kb.dscr("OSUM%d" % q, [T[q], 1024], F32) for q in range(2)]
    OSUMB = [kb.dscr("OSUMB%d" % q, [T[q], 1024], F32) for q in range(2)]
    X1T = [kb.dscr("X1T%d" % q, [D, E[q]], F32) for q in range(2)]
    stop = cfg.stop_after

    def cast_phase(items, tag):
        with kb.phase("W" + tag):
            fb = kb.sbn("wf", 3, [128, 2048], F32)
            bb = kb.sbn("wb", 3, [128, 2048], BF16)
            engs = ["act", "dve", "pool"]
            for i, (src, dstfn, C) in enumerate(items):
                f, b = fb[i % 3], bb[i % 3]
                kb.dma(f[:, 0:C], src, W=[f])
                kb.copy(engs[i % 3], b[:, 0:C], f[:, 0:C], [f], [b])
                dstfn(b)

    items = []
    for rt in range(16):
        for c0 in range(0, DIN, 1888):
            def dst(b, rt=rt, c0=c0):
                kb.dma(Win_b[rt * 128:(rt + 1) * 128, c0:c0 + 1888], b[:, 0:1888], R=[b])
            items.append((w_in[rt * 128:(rt + 1) * 128, c0:c0 + 1888], dst, 1888))
        def dst(b, rt=rt):
            kb.dma(Wout_b[rt * 128:(rt + 1) * 128, :], b[:, :], R=[b])
        items.append((w_out[rt * 128:(rt + 1) * 128, :], dst, 2048))
    for rt in range(16):
        def dst(b, rt=rt):
            kb.dma(Pool_b[rt * 128:(rt + 1) * 128, :], b[:, 0:512], R=[b])
        items.append((pool_w[rt * 128:(rt + 1) * 128, :], dst, 512))
    cast_phase(items, "a")
    for l in range(2):
        items = []
        for rt in range(16):
            for c0 in range(0, DFF, 2048):
                def dst(b, rt=rt, c0=c0, l=l):
                    kb.dma(W1_b[l, rt * 128:(rt + 1) * 128, c0:c0 + 2048], b[:, :], R=[b])
                items.append((mlp_w1[l, rt * 128:(rt + 1) * 128, c0:c0 + 2048], dst, 2048))
        for fc in range(FC):
            def dst(b, fc=fc, l=l):
                kb.dma(W2_s[l][:, :, fc * 128:(fc + 1) * 128].rearrange("o p m -> p o m"),
                       b[:, :].rearrange("p (o m) -> p o m", o=16), R=[b])
            items.append((mlp_w2[l, fc * 128:(fc + 1) * 128, :], dst, 2048))
        cast_phase(items, "m%d" % l)

    if stop == "W":
        return kb
    with kb.phase("C0"):
        cc = kb.sb("cc", [128, KC, 2], F32)
        sc = kb.sb("sc", [128, KC, 2], F32)
        ab = kb.sb("ab", [128, 2, 96], F32)
        cd = kb.sb("cd", [128, 2, 2, 96], F32)
        wg = kb.sbn("wg", 2, [128, KC, 512], F32)
        kb.dma(cc[:], ccol, W=[cc])
        kb.dma(ab[:], ada_bcol, W=[ab])
        kb.op("act", lambda e: e.activation(out=sc[:], in_=cc[:], func=AF.Silu), [cc], [sc])
        gi = 0
        for l in range(2):
            pt = kb.ps()
            for cg in range(24):
                w = wg[gi % 2]
                gi += 1
                kb.dma(w[:], ada_w[l][:, cg * 512:(cg + 1) * 512].rearrange("(kc p) n -> p kc n", p=128), W=[w])
                for jj in range(4):
                    j = cg * 4 + jj
                    for kc in range(KC):
                        kb.op("pe", lambda e, w=w, kc=kc, jj=jj, j=j, pt=pt: e.matmul(
                            pt[:, 2 * j:2 * j + 2], lhsT=w[:, kc, jj * 128:(jj + 1) * 128], rhs=sc[:, kc, :],
                            start=(kc == 0), stop=(kc == KC - 1)), [w, sc], [pt])
            kb.op("dve", lambda e, l=l, pt=pt: e.tensor_tensor(
                out=cd[:, l].rearrange("p q j -> p j q"), in0=pt[:, 0:192].rearrange("p (j q) -> p j q", q=2),
                in1=ab[:, l, :].unsqueeze(2).to_broadcast([128, 96, 2]), op=ALU.add), [pt, ab], [cd])
        kb.dma(COND, cd[:], R=[cd], W=[kb.DR("COND")])
    if stop == "C0":
        return kb

    def load_cols(l, q):
        cd = kb.sb("cdl", [128, 96], F32)
        g4 = kb.sb("lg4", [128, 4, KC], F32)
        b4 = kb.sb("lb4", [128, 4, KC], F32)
        kb.dma(cd[:], COND[:, l, q, :], W=[cd])
        kb.dma(g4[:], lng_col, W=[g4])
        kb.dma(b4[:], lnb_col, W=[b4])
        cols = {}
        cols["cd"], cols["g4"], cols["b4"] = cd, g4, b4
        o1 = kb.sb("opsc", [128, 2, KC], F32)
        kb.op("dve", lambda e: e.tensor_scalar(out=o1[:, 0, :], in0=cd[:, 16:32], scalar1=1.0, scalar2=None, op0=ALU.add), [cd], [o1])
        kb.op("dve", lambda e: e.tensor_scalar(out=o1[:, 1, :], in0=cd[:, 64:80], scalar1=1.0, scalar2=None, op0=ALU.add), [cd], [o1])
        cols["opsc"] = o1
        return cols

    def mod_cols(cols, name, which, gsrc=None, bsrc=None):
        cd, o1 = cols["cd"], cols["opsc"]
        sh = cd[:, 0:16] if which == 0 else cd[:, 48:64]
        hs = kb.sb(name + "hs", [128, KC], F32)
        hb = kb.sb(name + "hb", [128, KC], F32)
        if gsrc is None:
            kb.op("dve", lambda e: e.tensor_copy(out=hs[:], in_=o1[:, which, :]), [o1], [hs])
            kb.op("dve", lambda e: e.tensor_copy(out=hb[:], in_=sh), [cd], [hb])
        else:
            kb.op("dve", lambda e: e.tensor_tensor(out=hs[:], in0=gsrc, in1=o1[:, which, :], op=ALU.mult), [o1, cols["g4"]], [hs])
            kb.op("dve", lambda e: e.tensor_tensor(out=hb[:], in0=bsrc, in1=o1[:, which, :], op=ALU.mult), [o1, cols["b4"]], [hb])
            kb.op("dve", lambda e: e.tensor_tensor(out=hb[:], in0=hb[:], in1=sh, op=ALU.add), [hb, cd], [hb])
        return hs, hb

    def res_cols(cols, name, gsrc=None, bsrc=None):
        rs = kb.sb(name + "rs", [128, KC], F32)
        rb = kb.sb(name + "rb", [128, KC], F32)
        if gsrc is None:
            kb.op("dve", lambda e: e.memset(rs[:], ALPHA), [], [rs])
            kb.op("dve", lambda e: e.memset(rb[:], 0.0), [], [rb])
        else:
            kb.op("dve", lambda e: e.tensor_scalar(out=rs[:], in0=gsrc, scalar1=ALPHA, scalar2=None, op0=ALU.mult), [cols["g4"]], [rs])
            kb.op("dve", lambda e: e.tensor_scalar(out=rb[:], in0=bsrc, scalar1=ALPHA, scalar2=None, op0=ALU.mult), [cols["b4"]], [rb])
        return rs, rb

    def load_consts(names):
        idx = dict(ident=0, ones=1, tri_f=2, tri_b=3, madd_f=4, madd_b=5, strict_f=6, strict_b=7, rotperm=8)
        out = {}
        for n in names:
            t = kb.sb("c_" + n, [128, 128], F32)
            kb.dma(t[:], cst[:, idx[n], :], W=[t])
            out[n] = t
            tb = kb.sb("cb_" + n, [128, 128], BF16)
            kb.op("dve", lambda e, t=t, tb=tb: e.tensor_copy(out=tb[:], in_=t[:]), [t], [tb])
            out[n + "_b"] = tb
        return out

    def phase_A1(q):
        Tq = T[q]
        nb = Tq // 512
        with kb.phase("A1_%d" % q):
            import os
            cols = load_cols(0, q)
            hs, hb = mod_cols(cols, "a1", 0)
            C = load_consts(["ident", "ones", "rotperm"])
            qk = kb.sb("qkw", [128, 2], F32)
            kb.dma(qk[:], qkw_col, W=[qk])
            xt = [kb.sbn("xt%d_" % i, 4, [128, D], F32) for i in range(1)] * 2
            hT = kb.sb("hT", [128, KC, 512], BF16)
            wgb = kb.sbn("wg", 2, [128, KC, 1024], BF16)
            wsm = kb.sb("wsm", [128, KC, 640], BF16)
            stg = kb.sbn("stg", 4, [128, 512], F32)
            cs = kb.sbn("cs", 2, [128, 2, 512], F32)
            sq = kb.sbn("sq", 2, [128, 1024], BF16)
            xw = kb.sbn("xw", 2, [128, 512], F32)
            k1 = kb.sbn("k1", 2, [128, 512], F32)
            k2 = kb.sbn("k2", 2, [128, 512], F32)
            rs_ = kb.sbn("rstd", 2, [128, 512], F32)
            kbf = kb.sbn("kbf", 2, [128, 512], BF16)
            vst = kb.sbn("vst", 2, [128, 256], BF16)
            sst = kb.sbn("sst", 2, [128, 32], F32)
            kb.dma(wsm[:, :, 512:544], Win_b[:, C_B:C_B + 32].rearrange("(kc p) n -> p kc n", p=128), W=[wsm])
            kb.dma(wsm[:, :, 0:512], Win_b[:, C_AK:C_AK + 512].rearrange("(kc p) n -> p kc n", p=128), W=[wsm])
            wi = 0
            si = 0
            for b in range(nb):
                t0 = b * 512
                xb = xt[b % 2]
                for ti in range(4):
                    kb.dma(xb[ti][:], xf[q][t0 + ti * 128:t0 + (ti + 1) * 128, :], W=[xb[ti]])
                kb.dma(cs[b % 2][:], ropeK[q][:, :, t0:t0 + 512].rearrange("c p t -> p c t"), W=[cs[b % 2]])
                for kc in range(KC):
                    pt = kb.ps()
                    for ti in range(4):
                        kb.op("pe", lambda e, pt=pt, ti=ti, kc=kc, xb=xb: e.transpose(
                            out=pt[:, ti * 128:(ti + 1) * 128], in_=xb[ti][:, kc * 128:(kc + 1) * 128], identity=C["ident"][:]),
                            [xb[ti], C["ident"]], [pt])
                    if True:
                        kb.op("dve", lambda e, pt=pt, kc=kc: e.tensor_scalar(out=hT[:, kc, :], in0=pt[:], scalar1=hs[:, kc:kc + 1], scalar2=hb[:, kc:kc + 1],
                                                                            op0=ALU.mult, op1=ALU.add), [pt, hs, hb], [hT])
                    else:
                        kb.op("act", lambda e, pt=pt, kc=kc: e.activation(out=hT[:, kc, :], in_=pt[:], func=AF.Identity,
                                                                         scale=hs[:, kc:kc + 1], bias=hb[:, kc:kc + 1]), [pt, hs, hb], [hT])
                parts = os.environ.get("A1PARTS", "123")
                for g3 in (range(3) if "1" in parts else []):
                    w = wgb[wi % 2]
                    wi += 1
                    kb.dma(w[:], Win_b[:, g3 * 1024:(g3 + 1) * 1024].rearrange("(kc p) n -> p kc n", p=128), W=[w])
                    for jj in range(8):
                        j = g3 * 8 + jj
                        pt = kb.ps()
                        for kc in range(KC):
                            kb.op("pe", lambda e, pt=pt, w=w, kc=kc, jj=jj: e.matmul(
                                pt[:], lhsT=w[:, kc, jj * 128:(jj + 1) * 128], rhs=hT[:, kc, :], start=(kc == 0), stop=(kc == KC - 1)),
                                [w, hT], [pt])
                        s = stg[si % 4]
                        si += 1
                        kb.copy(kb.alt(), s[:], pt[:], [pt], [s])
                        kb.dma(PRE[q][j * 128:(j + 1) * 128, t0:t0 + 512], s[:], R=[s], W=[kb.DR("PRE", (j, b))])
                for ti in (range(4) if "2" in parts else []):
                    pt = kb.ps()
                    for kc in range(KC):
                        kb.op("pe", lambda e, pt=pt, kc=kc, ti=ti: e.matmul(
                            pt[:, 0:32], lhsT=hT[:, kc, ti * 128:(ti + 1) * 128], rhs=wsm[:, kc, 512:544], start=(kc == 0), stop=(kc == KC - 1)),
                            [wsm, hT], [pt])
                    pt2 = kb.ps()
                    for kc in range(KC):
                        kb.op("pe", lambda e, pt2=pt2, kc=kc, ti=ti: e.matmul(
                            pt2[:, 0:256], lhsT=hT[:, kc, ti * 128:(ti + 1) * 128], rhs=wsm[:, kc, 256:512], start=(kc == 0), stop=(kc == KC - 1)),
                            [wsm, hT], [pt2])
                    tg = b * 4 + ti
                    vs, ss = vst[tg % 2], sst[tg % 2]
                    kb.copy("dve", ss[:], pt[:, 0:32], [pt], [ss])
                    kb.copy("act", vs[:], pt2[:, 0:256], [pt2], [vs])
                    kb.dma(BGR[q][tg * 128:(tg + 1) * 128, :], ss[:], R=[ss], W=[kb.DR("BGR", tg)])
                    kb.dma(Va[q][:, :, tg, :].rearrange("k p d -> p k d"), vs[:].rearrange("p (k d) -> p k d", k=2), R=[vs], W=[kb.DR("Va", tg)])
                for kv in (range(2) if "3" in parts else []):
                    i2 = (b * 2 + kv) % 2
                    pt = kb.ps()
                    for kc in range(KC):
                        kb.op("pe", lambda e, pt=pt, kc=kc, kv=kv: e.matmul(
                            pt[:], lhsT=wsm[:, kc, kv * 128:(kv + 1) * 128], rhs=hT[:, kc, :], start=(kc == 0), stop=(kc == KC - 1)),
                            [wsm, hT], [pt])
                    rope_norm(kb, C, pt, 512, qk, 1, cs[b % 2], sq[i2], xw[i2], k1[i2], k2[i2], rs_[i2], kbf[i2], 1.0)
                    kst = os.environ.get("KST", "sp")
                    if kst != "none":
                        kb.dma(KTa[q][kv, :, t0:t0 + 512], kbf[i2][:], R=[kbf[i2]], W=[kb.DR("KTa", (kv, b))], q=kst)

    def rope_norm(kb, C, pt, nt, wT, wi, cs, sq, xw, k1, k2, rstd, outb, outscale):
        kb.op("act", lambda e: e.activation(out=k2[:, 0:nt], in_=pt[:, 0:nt], func=AF.Copy), [pt], [k2])
        kb.op("dve", lambda e: e.tensor_tensor(out=sq[:, 0:nt], in0=k2[:, 0:nt], in1=k2[:, 0:nt], op=ALU.mult), [k2], [sq])
        kb.op("dve", lambda e: e.tensor_scalar(out=xw[:, 0:nt], in0=k2[:, 0:nt], scalar1=wT[:, wi:wi + 1], scalar2=None, op0=ALU.mult), [k2, wT], [xw])
        kb.op("dve", lambda e: e.tensor_copy(out=sq[:, 512:512 + nt], in_=xw[:, 0:nt]), [xw], [sq])
        p2 = kb.ps()
        kb.op("pe", lambda e: e.matmul(p2[:, 0:nt], lhsT=C["ones_b"][:], rhs=sq[:, 0:nt], start=True, stop=True), [C["ones_b"], sq], [p2])
        kb.op("dve", lambda e: e.tensor_scalar(out=rstd[:, 0:nt], in0=p2[:, 0:nt], scalar1=1.0 / 128, scalar2=NORM_EPS, op0=ALU.mult, op1=ALU.add), [p2], [rstd])
        kb.op("act", lambda e: e.activation(out=rstd[:, 0:nt], in_=rstd[:, 0:nt], func=AF.Sqrt), [rstd], [rstd])
        kb.op("dve", lambda e: e.reciprocal(out=rstd[:, 0:nt], in_=rstd[:, 0:nt]), [rstd], [rstd])
        p3 = kb.ps()
        kb.op("pe", lambda e: e.matmul(p3[:, 0:nt], lhsT=C["rotperm_b"][:], rhs=sq[:, 512:512 + nt], start=True, stop=True), [C["rotperm_b"], sq], [p3])
        kb.op("dve", lambda e: e.tensor_tensor(out=k1[:, 0:nt], in0=xw[:, 0:nt], in1=cs[:, 0, 0:nt], op=ALU.mult), [xw, cs], [k1])
        kb.op("dve", lambda e: e.tensor_tensor(out=k2[:, 0:nt], in0=p3[:, 0:nt], in1=cs[:, 1, 0:nt], op=ALU.mult), [p3, cs], [k2])
        kb.op("dve", lambda e: e.tensor_tensor(out=k1[:, 0:nt], in0=k1[:, 0:nt], in1=k2[:, 0:nt], op=ALU.add), [k1, k2], [k1])
        if outscale != 1.0:
            kb.op("dve", lambda e: e.scalar_tensor_tensor(out=outb[:, 0:nt], in0=k1[:, 0:nt], scalar=float(outscale), in1=rstd[:, 0:nt], op0=ALU.mult, op1=ALU.mult), [k1, rstd], [outb])
        else:
            kb.op("dve", lambda e: e.tensor_tensor(out=outb[:, 0:nt], in0=k1[:, 0:nt], in1=rstd[:, 0:nt], op=ALU.mult), [k1, rstd], [outb])

    for q in range(2):
        phase_A1(q)
    if stop == "A1":
        return kb

    def phase_A2(q):
        Tq = T[q]
        nb = Tq // 512
        with kb.phase("A2_%d" % q):
            C = load_consts(["ident", "ones"])
            cw = kb.sb("cw", [128, 24, 5], F32)
            kb.dma(cw[:], convcol, W=[cw])
            alr = kb.sb("alr", [128, 16], F32)
            dtr = kb.sb("dtr", [128, 16], F32)
            kb.dma(alr[:], alog_row.partition_broadcast(128), W=[alr])
            kb.dma(dtr[:], dtb_row.partition_broadcast(128), W=[dtr])
            nea = kb.sb("nea", [128, 16], F32)
            kb.op("act", lambda e: e.activation(out=nea[:], in_=alr[:], func=AF.Exp), [alr], [nea])
            kb.op("dve", lambda e: e.tensor_scalar(out=nea[:], in0=nea[:], scalar1=-1.0, scalar2=None, op0=ALU.mult), [nea], [nea])
            pre = kb.sbn("pre", 3, [128, 516], F32)
            acc = kb.sbn("acc", 2, [128, 512], F32)
            sg = kb.sbn("sg", 2, [128, 512], F32)
            sqb = kb.sbn("sqb", 2, [128, 512], BF16)
            rst = kb.sbn("rst", 2, [128, 512], F32)
            qst = kb.sb("qst", [128, NH, 512], BF16)
            kst = kb.sb("kst", [128, NH, 512], BF16)
            vbf = kb.sbn("vbf", 2, [128, 512], F32)
            ktok = kb.sbn("ktok", 4, [128, 1024], BF16)
            vtok = kb.sbn("vtok", 4, [128, 1024], F32)
            br = kb.sbn("br", 2, [128, 32], F32)
            bo = kb.sbn("bo", 2, [128, 32], F32)
            t1 = kb.sbn("t1_", 2, [128, 16], F32)
            t2 = kb.sbn("t2_", 2, [128, 16], F32)
            for tg in range(Tq // 128):
                b_, o_, a1, a2 = br[tg % 2], bo[tg % 2], t1[tg % 2], t2[tg % 2]
                kb.dma(b_[:], BGR[q][tg * 128:(tg + 1) * 128, :], W=[b_])
                kb.op("act", lambda e, b_=b_, o_=o_: e.activation(out=o_[:, 0:16], in_=b_[:, 0:16], func=AF.Exp, scale=-1.0), [b_], [o_])
                kb.op("dve", lambda e, o_=o_: e.tensor_scalar(out=o_[:, 0:16], in0=o_[:, 0:16], scalar1=1.0, scalar2=None, op0=ALU.add), [o_], [o_])
                kb.op("dve", lambda e, o_=o_: e.reciprocal(out=o_[:, 0:16], in_=o_[:, 0:16]), [o_], [o_])
                kb.op("dve", lambda e, b_=b_, a1=a1: e.tensor_tensor(out=a1[:], in0=b_[:, 16:32], in1=dtr[:], op=ALU.add), [b_, dtr], [a1])
                kb.op("dve", lambda e, a1=a1, a2=a2: e.tensor_scalar(out=a2[:], in0=a1[:], scalar1=-1.0, scalar2=None, op0=ALU.mult), [a1], [a2])
                kb.op("dve", lambda e, a1=a1, a2=a2: e.tensor_tensor(out=a2[:], in0=a2[:], in1=a1[:], op=ALU.max), [a1, a2], [a2])
                kb.op("act", lambda e, a2=a2: e.activation(out=a2[:], in_=a2[:], func=AF.Exp, scale=-1.0), [a2], [a2])
                kb.op("dve", lambda e, a2=a2: e.tensor_scalar(out=a2[:], in0=a2[:], scalar1=1.0, scalar2=None, op0=ALU.add), [a2], [a2])
                kb.op("act", lambda e, a2=a2: e.activation(out=a2[:], in_=a2[:], func=AF.Ln), [a2], [a2])
                kb.op("dve", lambda e, a1=a1: e.tensor_scalar(out=a1[:], in0=a1[:], scalar1=0.0, scalar2=None, op0=ALU.max), [a1], [a1])
                kb.op("dve", lambda e, a1=a1, a2=a2: e.tensor_tensor(out=a1[:], in0=a1[:], in1=a2[:], op=ALU.add), [a1, a2], [a1])
                kb.op("dve", lambda e, a1=a1, o_=o_: e.tensor_tensor(out=o_[:, 16:32], in0=a1[:], in1=nea[:], op=ALU.mult), [a1, nea], [o_])
                kb.dma(BG[q][tg * 128:(tg + 1) * 128, :], o_[:], R=[o_], W=[kb.DR("BG", tg)])
            pi = 0
            for b in range(nb):
                t0 = b * 512
                for j in range(24):
                    p_ = pre[pi % 3]
                    pi += 1
                    lo = 2 if b == 0 else 0
                    hi = 514 if b == nb - 1 else 516
                    if b == 0:
                        kb.op("dve", lambda e, p_=p_: e.memset(p_[:, 0:2], 0.0), [], [p_])
                    if b == nb - 1:
                        kb.op("dve", lambda e, p_=p_: e.memset(p_[:, 514:516], 0.0), [], [p_])
                    kb.dma(p_[:, lo:hi], PRE[q][j * 128:(j + 1) * 128, t0 - 2 + lo:t0 - 2 + hi], W=[p_])
                    a_ = acc[j % 2]
                    kb.op("dve", lambda e, p_=p_, a_=a_, j=j: e.tensor_scalar(out=a_[:], in0=p_[:, 0:512], scalar1=cw[:, j, 0:1], scalar2=None, op0=ALU.mult), [p_, cw], [a_])
                    for jj in range(1, 5):
                        kb.op("dve", lambda e, p_=p_, a_=a_, j=j, jj=jj: e.scalar_tensor_tensor(
                            out=a_[:], in0=p_[:, jj:jj + 512], scalar=cw[:, j, jj:jj + 1], in1=a_[:], op0=ALU.mult, op1=ALU.add), [p_, cw, a_], [a_])
                    s_ = sg[j % 2]
                    kb.op("act", lambda e, a_=a_, s_=s_: e.activation(out=s_[:], in_=a_[:], func=AF.Silu), [a_], [s_])
                    if j < 16:
                        h = j % 8
                        q_b, r_ = sqb[j % 2], rst[j % 2]
                        kb.op("dve", lambda e, s_=s_, q_b=q_b: e.tensor_tensor(out=q_b[:], in0=s_[:], in1=s_[:], op=ALU.mult), [s_], [q_b])
                        p2 = kb.ps()
                        kb.op("pe", lambda e, p2=p2, q_b=q_b: e.matmul(p2[:], lhsT=C["ones_b"][:], rhs=q_b[:], start=True, stop=True), [C["ones_b"], q_b], [p2])
                        kb.op("dve", lambda e, p2=p2, r_=r_: e.tensor_scalar(out=r_[:], in0=p2[:], scalar1=NORM_EPS, scalar2=None, op0=ALU.add), [p2], [r_])
                        kb.op("act", lambda e, r_=r_: e.activation(out=r_[:], in_=r_[:], func=AF.Sqrt), [r_], [r_])
                        kb.op("dve", lambda e, r_=r_: e.reciprocal(out=r_[:], in_=r_[:]), [r_], [r_])
                        dst = qst if j < 8 else kst
                        sc_ = (128 ** -0.5) if j < 8 else 1.0
                        kb.op("dve", lambda e, s_=s_, r_=r_, dst=dst, h=h, sc_=sc_: e.scalar_tensor_tensor(
                            out=dst[:, h, :], in0=s_[:], scalar=float(sc_), in1=r_[:], op0=ALU.mult, op1=ALU.mult), [s_, r_], [dst])
                        if j >= 8:
                            for ti in range(4):
                                pb = kb.ps()
                                pbb = pb[:].bitcast(BF16)
                                kb.op("pe", lambda e, pbb=pbb, ti=ti, h=h: e.transpose(out=pbb[:, 0:128], in_=kst[:, h, ti * 128:(ti + 1) * 128], identity=C["ident_b"][:]),
                                      [kst, C["ident_b"]], [pb])
                                kb.op("dve", lambda e, pbb=pbb, ti=ti, h=h: e.tensor_copy(out=ktok[ti][:, h * 128:(h + 1) * 128], in_=pbb[:, 0:128]), [pb], [ktok[ti]])
                    else:
                        h = j - 16
                        for ti in range(4):
                            pb = kb.ps()
                            kb.op("pe", lambda e, pb=pb, ti=ti, s_=s_: e.transpose(out=pb[:, 0:128], in_=s_[:, ti * 128:(ti + 1) * 128], identity=C["ident"][:]),
                                  [s_, C["ident"]], [pb])
                            kb.op("act", lambda e, pb=pb, ti=ti, h=h: e.activation(out=vtok[ti][:, h * 128:(h + 1) * 128], in_=pb[:, 0:128], func=AF.Copy), [pb], [vtok[ti]])
                for ti in range(4):
                    tg = b * 4 + ti
                    kb.dma(QTg[q][tg], qst[:, :, ti * 128:(ti + 1) * 128], R=[qst], W=[kb.DR("QTg", tg)])
                    kb.dma(KTg[q][tg], kst[:, :, ti * 128:(ti + 1) * 128], R=[kst], W=[kb.DR("KTg", tg)])
                    kb.dma(Ktok[q][tg * 128:(tg + 1) * 128, :], ktok[ti][:], R=[ktok[ti]], W=[kb.DR("Ktok", tg)])
                    kb.dma(Vtok[q][tg * 128:(tg + 1) * 128, :], vtok[ti][:], R=[vtok[ti]], W=[kb.DR("Vtok", tg)])

    for q in range(2):
        phase_A2(q)
    if stop == "A2":
        return kb

    def phase_G(q):
        Tq = T[q]
        ntile = Tq // 128
        with kb.phase("G_%d" % q):
          CC = load_consts(["ident", "ones", "tri_f", "madd_f", "strict_f", "tri_b", "madd_b", "strict_b"])
          streams = []
          for d in range(2):
            sfx = "_f" if d == 0 else "_b"
            C = CC
            TRI, MADD, STRICT = C["tri" + sfx + "_b"], C["madd" + sfx], C["strict" + sfx]
            IDB, ONB = C["ident_b"], C["ones_b"]
            qT = kb.sbn("gqT%d_" % d, 2, [128, NH, 128], BF16)
            kT = kb.sbn("gkT%d_" % d, 2, [128, NH, 128], BF16)
            kt_ = kb.sbn("gkt%d_" % d, 2, [128, 1024], BF16)
            vt_ = kb.sbn("gvt%d_" % d, 2, [128, 1024], F32)
            bg = kb.sbn("gbg%d_" % d, 2, [128, 32], F32)
            osl = kb.sbn("gos%d_" % d, 2, [128, 1024], F32)
            ost = kb.sbn("gost%d_" % d, 2, [128, 1024], F32)
            S32 = kb.sb("S32_%d" % d, [128, NH, 128], F32)
            Sb = kb.sb("Sb_%d" % d, [128, NH, 128], BF16)
            kb.op("dve", lambda e: e.memset(S32[:], 0.0), [], [S32])
            kb.op("dve", lambda e: e.memset(Sb[:], 0.0), [], [Sb])
            ghl = kb.sbn("ghl%d_" % d, 2, [128, 16], BF16)
            gtmp = kb.sbn("gtmp%d_" % d, 2, [128, 8], F32)
            pcsb = kb.sbn("pcsb%d_" % d, 2, [128, 32], F32)
            sc3 = kb.sbn("sc3%d_" % d, 2, [128, 24], F32)
            ex3 = kb.sbn("ex3%d_" % d, 2, [128, 24], F32)
            neg = kb.sbn("neg%d_" % d, 2, [128, 16], F32)
            decI = kb.sbn("decI%d_" % d, 2, [128, 512], F32)
            decS = kb.sbn("decS%d_" % d, 2, [128, 512], F32)
            Dm = kb.sbn("Dm%d_" % d, 2, [128, 512], F32)
            M = kb.sbn("M%d_" % d, 2, [128, 512], BF16)
            MT = kb.sbn("MT%d_" % d, 2, [128, 512], BF16)
            Y = kb.sbn("Y%d_" % d, 2, [128, 512], BF16)
            Wb = kb.sbn("Wb%d_" % d, 2, [128, 512], BF16)
            vn = kb.sbn("vn%d_" % d, 2, [128, 512], BF16)
            ktl = kb.sbn("ktl%d_" % d, 2, [128, 512], BF16)
            PTm = kb.sbn("PTm%d_" % d, 2, [128, 512], BF16)
            o1s = kb.sbn("o1s%d_" % d, 2, [128, 512], F32)
            def step(it, tg, d=d, sfx=sfx, TRI=TRI, MADD=MADD, STRICT=STRICT, IDB=IDB, ONB=ONB, qT=qT, kT=kT, kt_=kt_, vt_=vt_, bg=bg, osl=osl, ost=ost, S32=S32, Sb=Sb, ghl=ghl, gtmp=gtmp, pcsb=pcsb, sc3=sc3, ex3=ex3, neg=neg, decI=decI, decS=decS, Dm=Dm, M=M, MT=MT, Y=Y, Wb=Wb, vn=vn, ktl=ktl, PTm=PTm, o1s=o1s):
                i2 = it % 2
                kb.dma(qT[i2][:], QTg[q][tg], W=[qT[i2]])
                kb.dma(kT[i2][:], KTg[q][tg], W=[kT[i2]])
                kb.dma(kt_[i2][:], Ktok[q][tg * 128:(tg + 1) * 128, :], W=[kt_[i2]])
                kb.dma(vt_[i2][:], Vtok[q][tg * 128:(tg + 1) * 128, :], W=[vt_[i2]])
                kb.dma(bg[i2][:], BG[q][tg * 128:(tg + 1) * 128, :], W=[bg[i2]])
                gcol = bg[i2][:, 16 + d * 8:16 + d * 8 + 8]
                bcol = bg[i2][:, d * 8:d * 8 + 8]
                gh, gt_, s3, e3, ng = ghl[i2], gtmp[i2], sc3[i2], ex3[i2], neg[i2]
                kb.op("dve", lambda e, gh=gh, gcol=gcol: e.tensor_copy(out=gh[:, 0:8], in_=gcol), [bg[i2]], [gh])
                kb.op("dve", lambda e, gh=gh, gt_=gt_: e.tensor_copy(out=gt_[:], in_=gh[:, 0:8]), [gh], [gt_])
                kb.op("dve", lambda e, gh=gh, gt_=gt_, gcol=gcol: e.tensor_tensor(out=gh[:, 8:16], in0=gcol, in1=gt_[:], op=ALU.subtract), [bg[i2], gt_], [gh])
                pc = kb.ps()
                kb.op("pe", lambda e, pc=pc, gh=gh: e.matmul(pc[:, 0:16], lhsT=TRI[:], rhs=gh[:], start=True, stop=True), [TRI, gh], [pc])
                kb.op("pe", lambda e, pc=pc, gh=gh: e.matmul(pc[:, 16:32], lhsT=ONB[:], rhs=gh[:], start=True, stop=True), [ONB, gh], [pc])
                pcs = pcsb[i2]
                kb.op("dve", lambda e: e.tensor_copy(out=pcs[:], in_=pc[:, 0:32]), [pc], [pcs])
                kb.op("dve", lambda e: e.tensor_tensor(out=s3[:, 0:8], in0=pcs[:, 0:8], in1=pcs[:, 8:16], op=ALU.add), [pcs], [s3])
                kb.op("dve", lambda e: e.tensor_tensor(out=s3[:, 8:16], in0=pcs[:, 16:24], in1=pcs[:, 24:32], op=ALU.add), [pcs], [s3])
                kb.op("dve", lambda e, s3=s3: e.tensor_tensor(out=s3[:, 16:24], in0=s3[:, 8:16], in1=s3[:, 0:8], op=ALU.subtract), [s3], [s3])
                kb.op("act", lambda e, s3=s3, e3=e3: e.activation(out=e3[:], in_=s3[:], func=AF.Exp), [s3], [e3])
                kb.op("dve", lambda e, ng=ng, bcol=bcol: e.tensor_scalar(out=ng[:, 0:8], in0=bcol, scalar1=-1.0, scalar2=None, op0=ALU.mult), [bg[i2]], [ng])
                kb.op("dve", lambda e, ng=ng, e3=e3: e.tensor_scalar(out=ng[:, 8:16], in0=e3[:, 0:8], scalar1=-1.0, scalar2=None, op0=ALU.mult), [e3], [ng])
                for grp in range(2):
                    gi_ = (it * 2 + grp) % 2
                    dI, dS, D_, M_, MT_, Y_, W_, vn_, ktl_, PT_, o1_ = decI[gi_], decS[gi_], Dm[gi_], M[gi_], MT[gi_], Y[gi_], Wb[gi_], vn[gi_], ktl[gi_], PTm[gi_], o1s[gi_]
                    hs_ = [grp * 4 + k for k in range(4)]
                    pg = kb.ps()
                    for k, h in enumerate(hs_):
                        for part in range(2):
                            kb.op("pe", lambda e, pg=pg, k=k, h=h, part=part, gh=gh: e.matmul(
                                pg[:, k * 128:(k + 1) * 128], lhsT=gh[:, part * 8 + h:part * 8 + h + 1].to_broadcast([128, 128]), rhs=TRI[:],
                                start=(part == 0), stop=(part == 1)), [gh, TRI], [pg])
                    for k, h in enumerate(hs_):
                        kb.op("dve", lambda e, pg=pg, k=k, h=h, D_=D_, s3=s3: e.scalar_tensor_tensor(
                            out=D_[:, k * 128:(k + 1) * 128], in0=pg[:, k * 128:(k + 1) * 128], scalar=s3[:, h:h + 1], in1=MADD[:],
                            op0=ALU.subtract, op1=ALU.add), [pg, s3, MADD], [D_])
                    kb.op("act", lambda e, D_=D_, dI=dI: e.activation(out=dI[:], in_=D_[:], func=AF.Exp), [D_], [dI])
                    kb.op("dve", lambda e, dI=dI, dS=dS: e.tensor_tensor(out=dS[:].rearrange("p (k c) -> p k c", k=4), in0=dI[:].rearrange("p (k c) -> p k c", k=4),
                                                                        in1=STRICT[:].unsqueeze(1).to_broadcast([128, 4, 128]), op=ALU.mult), [dI, STRICT], [dS])
                    pk = kb.ps()
                    for k, h in enumerate(hs_):
                        kb.op("pe", lambda e, pk=pk, k=k, h=h: e.matmul(pk[:, k * 128:(k + 1) * 128], lhsT=kT[i2][:, h, :], rhs=kT[i2][:, h, :], start=True, stop=True), [kT[i2]], [pk])
                    for k, h in enumerate(hs_):
                        kb.op("dve", lambda e, pk=pk, k=k, h=h, M_=M_, dS=dS, ng=ng: e.scalar_tensor_tensor(
                            out=M_[:, k * 128:(k + 1) * 128], in0=pk[:, k * 128:(k + 1) * 128], scalar=ng[:, h:h + 1], in1=dS[:, k * 128:(k + 1) * 128],
                            op0=ALU.mult, op1=ALU.mult), [pk, ng, dS], [M_])
                    pm = kb.ps()
                    pmb = pm[:].bitcast(BF16)
                    for k in range(4):
                        kb.op("pe", lambda e, pmb=pmb, k=k, M_=M_: e.transpose(out=pmb[:, k * 128:(k + 1) * 128], in_=M_[:, k * 128:(k + 1) * 128], identity=IDB[:]), [M_, IDB], [pm])
                    kb.op("act", lambda e, pmb=pmb, MT_=MT_: e.activation(out=MT_[:], in_=pmb[:, 0:512], func=AF.Copy), [pm], [MT_])
                    kb.op("dve", lambda e, M_=M_, Y_=Y_: e.tensor_tensor(out=Y_[:].rearrange("p (k c) -> p k c", k=4), in0=M_[:].rearrange("p (k c) -> p k c", k=4),
                                                                        in1=IDB[:].unsqueeze(1).to_broadcast([128, 4, 128]), op=ALU.add), [M_, IDB], [Y_])
                    for step in range(6):
                        pa, pbk, py = kb.ps(), kb.ps(), kb.ps()
                        lastst = step == 5
                        for k in range(4):
                            sl = slice(k * 128, (k + 1) * 128)
                            kb.op("pe", lambda e, pbk=pbk, sl=sl, M_=M_, MT_=MT_: e.matmul(pbk[:, sl], lhsT=M_[:, sl], rhs=MT_[:, sl], start=True, stop=True), [M_, MT_], [pbk])
                            if not lastst:
                                kb.op("pe", lambda e, pa=pa, sl=sl, M_=M_, MT_=MT_: e.matmul(pa[:, sl], lhsT=MT_[:, sl], rhs=M_[:, sl], start=True, stop=True), [M_, MT_], [pa])
                        kb.op("act", lambda e, pbk=pbk, MT_=MT_: e.activation(out=MT_[:], in_=pbk[:], func=AF.Copy), [pbk], [MT_])
                        if not lastst:
                            kb.op("dve", lambda e, pa=pa, M_=M_: e.tensor_copy(out=M_[:], in_=pa[:]), [pa], [M_])
                        for k in range(4):
                            sl = slice(k * 128, (k + 1) * 128)
                            kb.op("pe", lambda e, py=py, sl=sl, Y_=Y_: e.matmul(py[:, sl], lhsT=IDB[:], rhs=Y_[:, sl], start=True, stop=False, skip_group_check=True), [IDB, Y_], [py])
                            kb.op("pe", lambda e, py=py, sl=sl, Y_=Y_, MT_=MT_: e.matmul(py[:, sl], lhsT=MT_[:, sl], rhs=Y_[:, sl], start=False, stop=True, skip_group_check=True), [MT_, Y_], [py])
                        kb.op("dve", lambda e, py=py, Y_=Y_: e.tensor_copy(out=Y_[:], in_=py[:]), [py], [Y_])
                    pks = kb.ps()
                    po1 = kb.ps()
                    ppt = kb.ps()
                    for k, h in enumerate(hs_):
                        sl = slice(k * 128, (k + 1) * 128)
                        kb.op("pe", lambda e, pks=pks, sl=sl, h=h: e.matmul(pks[:, sl], lhsT=kT[i2][:, h, :], rhs=Sb[:, h, :], start=True, stop=True), [kT[i2], Sb], [pks])
                        kb.op("pe", lambda e, po1=po1, sl=sl, h=h: e.matmul(po1[:, sl], lhsT=qT[i2][:, h, :], rhs=Sb[:, h, :], start=True, stop=True), [qT[i2], Sb], [po1])
                        kb.op("pe", lambda e, ppt=ppt, sl=sl, h=h: e.matmul(ppt[:, sl], lhsT=kT[i2][:, h, :], rhs=qT[i2][:, h, :], start=True, stop=True), [kT[i2], qT[i2]], [ppt])
                    for k, h in enumerate(hs_):
                        sl = slice(k * 128, (k + 1) * 128)
                        kb.op("dve", lambda e, pks=pks, sl=sl, h=h, W_=W_, ng=ng: e.scalar_tensor_tensor(
                            out=W_[:, sl], in0=pks[:, sl], scalar=ng[:, 8 + h:9 + h], in1=vt_[i2][:, h * 128:(h + 1) * 128], op0=ALU.mult, op1=ALU.add), [pks, ng, vt_[i2]], [W_])
                        kb.op("dve", lambda e, po1=po1, sl=sl, h=h, o1_=o1_, e3=e3: e.tensor_scalar(
                            out=o1_[:, sl], in0=po1[:, sl], scalar1=e3[:, h:h + 1], scalar2=None, op0=ALU.mult), [po1, e3], [o1_])
                        kb.op("dve", lambda e, sl=sl, h=h, ktl_=ktl_, e3=e3: e.tensor_scalar(
                            out=ktl_[:, sl], in0=kt_[i2][:, h * 128:(h + 1) * 128], scalar1=e3[:, 16 + h:17 + h], scalar2=None, op0=ALU.mult), [kt_[i2], e3], [ktl_])
                    kb.op("dve", lambda e, ppt=ppt, PT_=PT_, dI=dI: e.tensor_tensor(out=PT_[:], in0=ppt[:], in1=dI[:], op=ALU.mult), [ppt, dI], [PT_])
                    pv = kb.ps()
                    for k, h in enumerate(hs_):
                        sl = slice(k * 128, (k + 1) * 128)
                        kb.op("pe", lambda e, pv=pv, sl=sl, Y_=Y_, W_=W_: e.matmul(pv[:, sl], lhsT=Y_[:, sl], rhs=W_[:, sl], start=True, stop=True), [Y_, W_], [pv])
                    for k, h in enumerate(hs_):
                        sl = slice(k * 128, (k + 1) * 128)
                        kb.op("dve", lambda e, pv=pv, sl=sl, h=h, vn_=vn_: e.tensor_scalar(
                            out=vn_[:, sl], in0=pv[:, sl], scalar1=bg[i2][:, d * 8 + h:d * 8 + h + 1], scalar2=None, op0=ALU.mult), [pv, bg[i2]], [vn_])
                    po2 = kb.ps()
                    pds = kb.ps()
                    for k, h in enumerate(hs_):
                        sl = slice(k * 128, (k + 1) * 128)
                        kb.op("pe", lambda e, po2=po2, sl=sl, PT_=PT_, vn_=vn_: e.matmul(po2[:, sl], lhsT=PT_[:, sl], rhs=vn_[:, sl], start=True, stop=True), [PT_, vn_], [po2])
                        kb.op("pe", lambda e, pds=pds, sl=sl, ktl_=ktl_, vn_=vn_: e.matmul(pds[:, sl], lhsT=ktl_[:, sl], rhs=vn_[:, sl], start=True, stop=True), [ktl_, vn_], [pds])
                    os_ = ost[i2]
                    kb.op("dve", lambda e, po2=po2, o1_=o1_, os_=os_, grp=grp: e.tensor_tensor(out=os_[:, grp * 512:(grp + 1) * 512], in0=po2[:], in1=o1_[:], op=ALU.add), [po2, o1_], [os_])
                    for k, h in enumerate(hs_):
                        sl = slice(k * 128, (k + 1) * 128)
                        kb.op("dve", lambda e, pds=pds, sl=sl, h=h, e3=e3: e.scalar_tensor_tensor(
                            out=S32[:, h, :], in0=S32[:, h, :], scalar=e3[:, 8 + h:9 + h], in1=pds[:, sl], op0=ALU.mult, op1=ALU.add), [S32, e3, pds], [S32])
                    kb.op("act", lambda e, grp=grp: e.activation(out=Sb[:, grp * 4:(grp + 1) * 4, :], in_=S32[:, grp * 4:(grp + 1) * 4, :], func=AF.Copy), [S32], [Sb])
                kb.dma((OSUM if d == 0 else OSUMB)[q][tg * 128:(tg + 1) * 128, :], ost[i2][:], R=[ost[i2]], W=[kb.DR("OSUM%d" % d, tg)])

            streams.append(step)
          for it in range(ntile):
            for d in range(2):
              streams[d](it, it if d == 0 else ntile - 1 - it)

    for q in range(2):
        phase_G(q)
    if stop == "G":
        return kb

    XRES = [kb.dscr("XRES%d" % q, [D, 512], F32) for q in range(2)]
    OTS = [kb.dscr("OTS%d" % q, [D, 512], BF16) for q in range(2)]

    def layernorm(XT, nt, C):
        xb = kb.sbn("lnxb", 2, [128, 512], BF16)
        sb_ = kb.sbn("lnsq", 2, [128, 512], BF16)
        mean = kb.sb("lnmean", [128, 512], F32)
        msq = kb.sb("lnmsq", [128, 512], F32)
        rstd = kb.sb("lnrstd", [128, 512], F32)
        nmr = kb.sb("lnnmr", [128, 512], F32)
        pa, pb = kb.ps(), kb.ps()
        for kc in range(KC):
            a_, b_ = xb[kc % 2], sb_[kc % 2]
            kb.op("dve", lambda e: e.tensor_copy(out=a_[:, 0:nt], in_=XT[:, kc, 0:nt]), [XT], [a_])
            kb.op("dve", lambda e: e.tensor_tensor(out=b_[:, 0:nt], in0=XT[:, kc, 0:nt], in1=XT[:, kc, 0:nt], op=ALU.mult), [XT], [b_])
            kb.op("pe", lambda e: e.matmul(pa[:, 0:nt], lhsT=C["ones_b"][:], rhs=a_[:, 0:nt], start=(kc == 0), stop=(kc == KC - 1)), [C["ones_b"], a_], [pa])
            kb.op("pe", lambda e: e.matmul(pb[:, 0:nt], lhsT=C["ones_b"][:], rhs=b_[:, 0:nt], start=(kc == 0), stop=(kc == KC - 1)), [C["ones_b"], b_], [pb])
        kb.op("dve", lambda e: e.tensor_scalar(out=mean[:, 0:nt], in0=pa[:, 0:nt], scalar1=1.0 / D, scalar2=None, op0=ALU.mult), [pa], [mean])
        kb.op("dve", lambda e: e.tensor_tensor(out=msq[:, 0:nt], in0=mean[:, 0:nt], in1=mean[:, 0:nt], op=ALU.mult), [mean], [msq])
        kb.op("dve", lambda e: e.scalar_tensor_tensor(out=rstd[:, 0:nt], in0=pb[:, 0:nt], scalar=1.0 / D, in1=msq[:, 0:nt], op0=ALU.mult, op1=ALU.subtract), [pb, msq], [rstd])
        kb.op("dve", lambda e: e.tensor_scalar(out=rstd[:, 0:nt], in0=rstd[:, 0:nt], scalar1=LN_EPS, scalar2=None, op0=ALU.add), [rstd], [rstd])
        kb.op("act", lambda e: e.activation(out=rstd[:, 0:nt], in_=rstd[:, 0:nt], func=AF.Sqrt), [rstd], [rstd])
        kb.op("dve", lambda e: e.reciprocal(out=rstd[:, 0:nt], in_=rstd[:, 0:nt]), [rstd], [rstd])
        kb.op("dve", lambda e: e.scalar_tensor_tensor(out=nmr[:, 0:nt], in0=mean[:, 0:nt], scalar=-1.0, in1=rstd[:, 0:nt], op0=ALU.mult, op1=ALU.mult), [mean, rstd], [nmr])
        for kc in range(KC):
            kb.op("dve", lambda e: e.tensor_tensor(out=XT[:, kc, 0:nt], in0=XT[:, kc, 0:nt], in1=rstd[:, 0:nt], op=ALU.mult), [XT, rstd], [XT])
            kb.op("dve", lambda e: e.tensor_tensor(out=XT[:, kc, 0:nt], in0=XT[:, kc, 0:nt], in1=nmr[:, 0:nt], op=ALU.add), [XT, nmr], [XT])

    def ffn_part(l, XT, hT, nt, cols, C, li):
        g4, b4, cd = cols["g4"], cols["b4"], cols["cd"]
        hs, hb = mod_cols(cols, "f%d" % li, 1, g4[:, 2 * l, :], b4[:, 2 * l, :])
        rs, rb = res_cols(cols, "f%d" % li, g4[:, 2 * l, :], b4[:, 2 * l, :])
        for kc in range(KC):
            kb.op("dve", lambda e: e.tensor_scalar(out=hT[:, kc, 0:nt], in0=XT[:, kc, 0:nt], scalar1=hs[:, kc:kc + 1], scalar2=hb[:, kc:kc + 1], op0=ALU.mult, op1=ALU.add), [XT, hs, hb], [hT])
            kb.op("dve", lambda e: e.tensor_scalar(out=XT[:, kc, 0:nt], in0=XT[:, kc, 0:nt], scalar1=rs[:, kc:kc + 1], scalar2=rb[:, kc:kc + 1], op0=ALU.mult, op1=ALU.add), [XT, rs, rb], [XT])
        uT = kb.sb("uT", [128, 32, 512], BF16)
        w1b = kb.sbn("w1b", 2, [128, KC, 512], BF16)
        w2b = kb.sbn("w2b", 2, [128, 32 * 128], BF16)
        ust = kb.sbn("ust", 2, [128, 512], F32)
        wi = 0
        for half in range(2):
            for fg in range(8):
                w = w1b[wi % 2]
                wi += 1
                c0 = (half * 8 + fg) * 512
                kb.dma(w[:], W1_b[l][:, c0:c0 + 512].rearrange("(kc p) n -> p kc n", p=128), W=[w])
                for fcl in range(4):
                    fci = fg * 4 + fcl
                    pt = kb.ps()
                    for kc in range(KC):
                        kb.op("pe", lambda e: e.matmul(pt[:, 0:nt], lhsT=w[:, kc, fcl * 128:(fcl + 1) * 128], rhs=hT[:, kc, 0:nt], start=(kc == 0), stop=(kc == KC - 1)), [w, hT], [pt])
                    u_ = ust[fci % 2]
                    kb.op("act", lambda e: e.activation(out=u_[:, 0:nt], in_=pt[:, 0:nt], func=AF.Copy), [pt], [u_])
                    kb.op("dve", lambda e: e.scalar_tensor_tensor(out=uT[:, fci, 0:nt], in0=u_[:, 0:nt], scalar=0.0, in1=u_[:, 0:nt], op0=ALU.max, op1=ALU.mult), [u_], [uT])
            for o in range(KC):
                w2 = w2b[o % 2]
                kb.dma(w2[:], W2_s[l, o][:, half * 4096:(half + 1) * 4096], W=[w2])
                pt = kb.ps()
                for fc in range(32):
                    kb.op("pe", lambda e: e.matmul(pt[:, 0:nt], lhsT=w2[:, fc * 128:(fc + 1) * 128], rhs=uT[:, fc, 0:nt], start=(fc == 0), stop=(fc == 31)), [w2, uT], [pt])
                kb.op("dve", lambda e: e.scalar_tensor_tensor(out=XT[:, o, 0:nt], in0=pt[:, 0:nt], scalar=cd[:, 80 + o:81 + o], in1=XT[:, o, 0:nt], op0=ALU.mult, op1=ALU.add), [pt, cd, XT], [XT])
        layernorm(XT, nt, C)

    def phase_B1(q, e0, nt, uid):
        Tq, Lq = T[q], L[q]
        nq = nt // 128
        with kb.phase("B1_%d_%d" % (q, uid)):
            cols = load_cols(0, q)
            hs, hb = mod_cols(cols, "b1", 0)
            C = load_consts(["ident", "ones", "rotperm"])
            qk = kb.sb("qkw", [128, 2], F32)
            kb.dma(qk[:], qkw_col, W=[qk])
            oh = kb.sb("oh", [128, 4], F32)
            kb.dma(oh[:], onehot, W=[oh])
            gnw = kb.sb("gnw", [128, 128], F32)
            kb.dma(gnw[:], gnw_row.partition_broadcast(128), W=[gnw])
            xt = kb.sbn("bxt", 2, [128, D], F32)
            XT = kb.sb("XT", [128, KC, 512], F32)
            hT = kb.sb("hT", [128, KC, 512], BF16)
            OT = kb.sb("OT", [128, KC, 512], BF16)
            QT = kb.sb("QT", [128, NH, 512], BF16)
            WB = kb.sb("WB", [128, KC, 1024], BF16)
            cs = kb.sb("cs", [128, 2, 512], F32)
            kb.dma(cs[:, :, 0:nt], ropeQ[q][:, :, e0:e0 + nt].rearrange("c p t -> p c t"), W=[cs])
            sq = kb.sbn("sq", 1, [128, 1024], BF16) * 2
            xw = kb.sbn("xw", 1, [128, 512], F32) * 2
            k1 = kb.sbn("k1", 1, [128, 512], F32) * 2
            k2 = kb.sbn("k2", 1, [128, 512], F32) * 2
            rs_ = kb.sbn("rstd", 1, [128, 512], F32) * 2
            for ti in range(nq):
                x_ = xt[ti % 2]
                kb.dma(x_[:], xe[q][e0 + ti * 128:e0 + (ti + 1) * 128, :], W=[x_])
                for kc4 in range(4):
                    pt = kb.ps()
                    for k in range(4):
                        kc = kc4 * 4 + k
                        kb.op("pe", lambda e: e.transpose(out=pt[:, k * 128:(k + 1) * 128], in_=x_[:, kc * 128:(kc + 1) * 128], identity=C["ident"][:]), [x_, C["ident"]], [pt])
                    kb.op("act", lambda e: e.activation(out=XT[:, kc4 * 4:(kc4 + 1) * 4, ti * 128:(ti + 1) * 128], in_=pt[:].rearrange("p (k t) -> p k t", k=4), func=AF.Copy), [pt], [XT])
            for kc in range(KC):
                kb.op("dve", lambda e: e.tensor_scalar(out=hT[:, kc, 0:nt], in0=XT[:, kc, 0:nt], scalar1=hs[:, kc:kc + 1], scalar2=hb[:, kc:kc + 1], op0=ALU.mult, op1=ALU.add), [XT, hs, hb], [hT])
            kb.dma(XRES[q][:, 0:nt].rearrange("(kc p) t -> p kc t", p=128), XT[:, :, 0:nt], R=[XT], W=[kb.DR("XRES")])
            kb.dma(WB[:], Win_b[:, C_AQ:C_AQ + 1024].rearrange("(kc p) n -> p kc n", p=128), W=[WB])
            for h in range(NH):
                pt = kb.ps()
                for kc in range(KC):
                    kb.op("pe", lambda e: e.matmul(pt[:, 0:nt], lhsT=WB[:, kc, h * 128:(h + 1) * 128], rhs=hT[:, kc, 0:nt], start=(kc == 0), stop=(kc == KC - 1)), [WB, hT], [pt])
                i2 = h % 2
                qv = TT(QT.t[:, h, :], "QTv")
                qv.r = QT.r
                rope_norm(kb, C, pt, nt, qk, 0, cs, sq[i2], xw[i2], k1[i2], k2[i2], rs_[i2], qv, 128 ** -0.5)
            kb.dma(WB[:], Win_b[:, C_Z:C_Z + 1024].rearrange("(kc p) n -> p kc n", p=128), W=[WB])
            zs = kb.sbn("zs", 1, [128, 1024], F32) * 2
            cand = kb.sbn("cand", 2, [128, 1024], F32) * 2
            osel = kb.sbn("osel", 1, [128, 1024], F32) * 2
            osq = kb.sb("osq", [128, 1024], F32)
            ss = kb.sbn("ss", 2, [128, 8], F32)
            ogb = kb.sbn("ogb", 2, [128, 1024], BF16)
            for ti in range(nq):
                z_, os_, s_, og_ = zs[ti % 2], osel[ti % 2], ss[ti % 2], ogb[ti % 2]
                for hf in range(2):
                    pt = kb.ps()
                    for kc in range(KC):
                        kb.op("pe", lambda e: e.matmul(pt[:], lhsT=hT[:, kc, ti * 128:(ti + 1) * 128], rhs=WB[:, kc, hf * 512:(hf + 1) * 512], start=(kc == 0), stop=(kc == KC - 1)), [WB, hT], [pt])
                    kb.op("act", lambda e: e.activation(out=z_[:, hf * 512:(hf + 1) * 512], in_=pt[:], func=AF.Silu), [pt], [z_])
                ee = e0 + ti * 128
                if ee < Lq:
                    cl = [(j, j * Lq + ee) for j in range(4)]
                elif ee == Lq:
                    cl = [(j, j * Lq - 128) for j in range(1, 4)]
                else:
                    cl = [(j, (j + 1) * Lq) for j in range(0, 3)]
                ci = 0
                for (j, row) in cl:
                    for src_ in (OSUM, OSUMB):
                        kb.dma(cand[ci % 2][:], src_[q][row:row + 128, :], W=[cand[ci % 2]])
                        if ci == 0:
                            kb.op("dve", lambda e: e.tensor_scalar(out=os_[:], in0=cand[ci % 2][:], scalar1=oh[:, j:j + 1], scalar2=None, op0=ALU.mult), [cand[ci % 2], oh], [os_])
                        else:
                            kb.op("dve", lambda e: e.scalar_tensor_tensor(out=os_[:], in0=cand[ci % 2][:], scalar=oh[:, j:j + 1], in1=os_[:], op0=ALU.mult, op1=ALU.add), [cand[ci % 2], oh, os_], [os_])
                        ci += 1
                kb.op("dve", lambda e: e.tensor_tensor(out=osq[:], in0=os_[:], in1=os_[:], op=ALU.mult), [os_], [osq])
                kb.op("dve", lambda e: e.reduce_sum(out=s_[:], in_=osq[:].rearrange("p (h d) -> p h d", h=8), axis=AX.X), [osq], [s_])
                kb.op("dve", lambda e: e.tensor_scalar(out=s_[:], in0=s_[:], scalar1=1.0 / 128, scalar2=NORM_EPS, op0=ALU.mult, op1=ALU.add), [s_], [s_])
                kb.op("act", lambda e: e.activation(out=s_[:], in_=s_[:], func=AF.Sqrt), [s_], [s_])
                kb.op("dve", lambda e: e.reciprocal(out=s_[:], in_=s_[:]), [s_], [s_])
                v3 = lambda a: a[:].rearrange("p (h d) -> p h d", h=8)
                kb.op("dve", lambda e: e.tensor_tensor(out=v3(os_), in0=v3(os_), in1=s_[:].unsqueeze(2).to_broadcast([128, 8, 128]), op=ALU.mult), [os_, s_], [os_])
                kb.op("dve", lambda e: e.tensor_tensor(out=v3(os_), in0=v3(os_), in1=gnw[:].unsqueeze(1).to_broadcast([128, 8, 128]), op=ALU.mult), [os_, gnw], [os_])
                kb.op("dve", lambda e: e.tensor_tensor(out=og_[:], in0=os_[:], in1=z_[:], op=ALU.mult), [os_, z_], [og_])
                for hp in range(2):
                    pb_ = kb.ps()
                    pbb = pb_[:].bitcast(BF16)
                    for k in range(4):
                        h = hp * 4 + k
                        kb.op("pe", lambda e: e.transpose(out=pbb[:, k * 128:(k + 1) * 128], in_=og_[:, h * 128:(h + 1) * 128], identity=C["ident_b"][:]), [og_, C["ident_b"]], [pb_])
                    kb.op("dve", lambda e: e.tensor_copy(out=OT[:, hp * 4:(hp + 1) * 4, ti * 128:(ti + 1) * 128], in_=pbb[:, 0:512].rearrange("p (k t) -> p k t", k=4)), [pb_], [OT])
            G = min(16, Tq // 128)
            ng = Tq // (128 * G)
            ktb = kb.sbn("ktb", 2, [128, G * 128], BF16)
            vab = kb.sbn("vab", 2, [128, G, 129], BF16)
            for v_ in vab:
                kb.op("dve", lambda e: e.memset(v_[:, :, 128:129], 1.0), [], [v_])
            ptb = kb.sbn("ptb", 4, [128, 512], BF16)
            rden = kb.sbn("rden", 2, [128, 1], F32)
            onb = kb.sbn("onb", 2, [128, 128], BF16)
            kb._psn = 5
            ACC = kb.PS[5:8]
            gi_ = 0
            pi_ = 0
            for pr in range(4):
                kvh = pr // 2
                first = [True, True, True]
                pend = []
                for kg in range(ng):
                    kt_, va_ = ktb[gi_ % 2], vab[gi_ % 2]
                    gi_ += 1
                    kb.dma(kt_[:], KTa[q][kvh][:, kg * G * 128:(kg + 1) * G * 128], W=[kt_])
                    kb.dma(va_[:, :, 0:128], Va[q][kvh][:, kg * G:(kg + 1) * G, :], W=[va_])
                    for kc in range(G):
                        for hh in range(2):
                            head = pr * 2 + hh
                            pt = kb.ps()
                            kb.op("pe", lambda e: e.matmul(pt[:, 0:nt], lhsT=kt_[:, kc * 128:(kc + 1) * 128], rhs=QT[:, head, 0:nt], start=True, stop=True), [kt_, QT], [pt])
                            p_ = ptb[pi_ % 4]
                            pi_ += 1
                            kb.op("act", lambda e: e.activation(out=p_[:, 0:nt], in_=pt[:, 0:nt], func=AF.Exp), [pt], [p_])

                            def pv(p_=p_, va_=va_, kc=kc, hh=hh, lastk=(kg == ng - 1 and kc == G - 1)):
                                for qi in range(nq):
                                    sl = hh * nq + qi
                                    bank, off = sl // 3, (sl % 3) * 129
                                    st_ = first[bank]
                                    first[bank] = False
                                    kb.op("pe", lambda e: e.matmul(ACC[bank][:, off:off + 129], lhsT=p_[:, qi * 128:(qi + 1) * 128], rhs=va_[:, kc, :],
                                                                   start=st_, stop=lastk, skip_group_check=True), [p_, va_], [ACC[bank]])
                            if len(pend) >= 2:
                                pend.pop(0)()
                            pend.append(pv)
                while pend:
                    pend.pop(0)()
                for hh in range(2):
                    head = pr * 2 + hh
                    pb_ = kb.ps()
                    pbb = pb_[:].bitcast(BF16)
                    for qi in range(nq):
                        sl = hh * nq + qi
                        bank, off = sl // 3, (sl % 3) * 129
                        r_, o_ = rden[sl % 2], onb[sl % 2]
                        kb.op("dve", lambda e: e.reciprocal(out=r_[:], in_=ACC[bank][:, off + 128:off + 129]), [ACC[bank]], [r_])
                        kb.op("dve", lambda e: e.tensor_scalar(out=o_[:], in0=ACC[bank][:, off:off + 128], scalar1=r_[:, 0:1], scalar2=None, op0=ALU.mult), [ACC[bank], r_], [o_])
                        kb.op("pe", lambda e: e.transpose(out=pbb[:, qi * 128:(qi + 1) * 128], in_=o_[:], identity=C["ident_b"][:]), [o_, C["ident_b"]], [pb_])
                    kb.op("dve", lambda e: e.tensor_copy(out=OT[:, 8 + head, 0:nt], in_=pbb[:, 0:nt]), [pb_], [OT])
            kb._psn = 8
            kb.dma(OTS[q][:, 0:nt].rearrange("(kc p) t -> p kc t", p=128), OT[:, :, 0:nt], R=[OT], W=[kb.DR("OTS")])

    def phase_B2(q, xcols, nt, uid):
        with kb.phase("B2_%d_%d" % (q, uid)):
            cols = load_cols(0, q)
            cd, g4, b4 = cols["cd"], cols["g4"], cols["b4"]
            C = load_consts(["ones"])
            XT = kb.sb("XT", [128, KC, 512], F32)
            OT = kb.sb("OT", [128, KC, 512], BF16)
            hT = kb.sb("hT", [128, KC, 512], BF16)
            kb.dma(XT[:, :, 0:nt], XRES[q][:, 0:nt].rearrange("(kc p) t -> p kc t", p=128), W=[XT])
            kb.dma(OT[:, :, 0:nt], OTS[q][:, 0:nt].rearrange("(kc p) t -> p kc t", p=128), W=[OT])
            wob = kb.sbn("wob", 2, [128, KC, 512], BF16)
            for og in range(4):
                w = wob[og % 2]
                kb.dma(w[:], Wout_b[:, og * 512:(og + 1) * 512].rearrange("(kc p) n -> p kc n", p=128), W=[w])
                for oc in range(4):
                    o = og * 4 + oc
                    pt = kb.ps()
                    for kc in range(KC):
                        kb.op("pe", lambda e: e.matmul(pt[:, 0:nt], lhsT=w[:, kc, oc * 128:(oc + 1) * 128], rhs=OT[:, kc, 0:nt], start=(kc == 0), stop=(kc == KC - 1)), [w, OT], [pt])
                    kb.op("dve", lambda e: e.tensor_scalar(out=XT[:, o, 0:nt], in0=XT[:, o, 0:nt], scalar1=ALPHA, scalar2=None, op0=ALU.mult), [XT], [XT])
                    kb.op("dve", lambda e: e.scalar_tensor_tensor(out=XT[:, o, 0:nt], in0=pt[:, 0:nt], scalar=cd[:, 32 + o:33 + o], in1=XT[:, o, 0:nt], op0=ALU.mult, op1=ALU.add), [pt, cd, XT], [XT])
            layernorm(XT, nt, C)
            ffn_part(0, XT, hT, nt, cols, C, uid)
            for kc in range(KC):
                kb.op("dve", lambda e: e.tensor_scalar(out=XT[:, kc, 0:nt], in0=XT[:, kc, 0:nt], scalar1=g4[:, 1, kc:kc + 1], scalar2=b4[:, 1, kc:kc + 1], op0=ALU.mult, op1=ALU.add), [XT, g4, b4], [XT])
            for (c0, t0_, n_) in xcols:
                kb.dma(X1T[q][:, c0:c0 + n_].rearrange("(kc p) t -> p kc t", p=128), XT[:, :, t0_:t0_ + n_], R=[XT], W=[kb.DR("X1T", c0)])

    def phase_C(q, blk, nt, uid):
        Lq = L[q]
        c0 = 128 + blk * 512
        n = nt + 16
        with kb.phase("C_%d_%d" % (q, uid)):
            cols = load_cols(1, q)
            cd, g4, b4 = cols["cd"], cols["g4"], cols["b4"]
            hs, hb = mod_cols(cols, "c", 0)
            C = load_consts(["ident", "ones"])
            psc = kb.sb("psc", [128, KC], F32)
            kb.dma(psc[:], pscale_col, W=[psc])
            gp = kb.sb("gp", [128, KC], F32)
            kb.op("dve", lambda e: e.tensor_tensor(out=gp[:], in0=psc[:], in1=cd[:, 32:48], op=ALU.mult), [psc, cd], [gp])
            vr = kb.sb("vr", [128, 528], F32)
            kb.dma(vr[:, 0:n], validr[q][:, c0 - 8:c0 - 8 + n].partition_broadcast(128), W=[vr])
            ic = kb.sb("ic", [128, 4, 512], F32)
            for gi in range(4):
                kb.dma(ic[:, gi, 0:nt], invcnt[q][gi:gi + 1, c0:c0 + nt].partition_broadcast(128), W=[ic])
            pw = kb.sb("pw", [128, 4, 4, 512], BF16)
            for gi in range(4):
                kb.dma(pw[:, gi], Pool_b[gi * 512:(gi + 1) * 512, :].rearrange("(kc p) n -> p kc n", p=128), W=[pw])
            XT = kb.sb("XT", [128, KC, 512], F32)
            hT = kb.sb("hT", [128, KC, 512], BF16)
            xh = kb.sbn("xh", 2, [128, 528], F32)
            hm = kb.sbn("hm", 2, [128, 528], F32)
            A_ = kb.sbn("pA", 1, [128, 528], F32) * 2
            B_ = kb.sbn("pB", 1, [128, 528], F32) * 2
            for kc in range(KC):
                gi = kc // 4
                x_, h_, a_, b_ = xh[kc % 2], hm[kc % 2], A_[kc % 2], B_[kc % 2]
                kb.dma(x_[:, 0:n], X1T[q][kc * 128:(kc + 1) * 128, c0 - 8:c0 - 8 + n], W=[x_])
                kb.op("dve", lambda e: e.tensor_scalar(out=XT[:, kc, 0:nt], in0=x_[:, 8:8 + nt], scalar1=ALPHA, scalar2=None, op0=ALU.mult), [x_], [XT])
                kb.op("dve", lambda e: e.tensor_scalar(out=h_[:, 0:n], in0=x_[:, 0:n], scalar1=hs[:, kc:kc + 1], scalar2=hb[:, kc:kc + 1], op0=ALU.mult, op1=ALU.add), [x_, hs, hb], [h_])
                kb.op("dve", lambda e: e.tensor_tensor(out=h_[:, 0:n], in0=h_[:, 0:n], in1=vr[:, 0:n], op=ALU.mult), [h_, vr], [h_])
                kb.op("dve", lambda e: e.tensor_tensor(out=a_[:, 1:n], in0=h_[:, 1:n], in1=h_[:, 0:n - 1], op=ALU.add), [h_], [a_])
                src = a_
                if gi >= 1:
                    kb.op("dve", lambda e: e.tensor_tensor(out=b_[:, 2:n - 1], in0=a_[:, 1:n - 2], in1=a_[:, 3:n], op=ALU.add), [a_], [b_])
                    src = b_
                if gi >= 2:
                    kb.op("dve", lambda e: e.tensor_tensor(out=a_[:, 4:n - 3], in0=b_[:, 2:n - 5], in1=b_[:, 6:n - 1], op=ALU.add), [b_], [a_])
                    src = a_
                if gi >= 3:
                    kb.op("dve", lambda e: e.tensor_tensor(out=b_[:, 8:n - 7], in0=a_[:, 4:n - 11], in1=a_[:, 12:n - 3], op=ALU.add), [a_], [b_])
                    src = b_
                dst = a_ if src is b_ else b_
                kb.op("dve", lambda e: e.tensor_tensor(out=dst[:, 8:8 + nt], in0=src[:, 8:8 + nt], in1=ic[:, gi, 0:nt], op=ALU.mult), [src, ic], [dst])
                kb.op("dve", lambda e: e.tensor_tensor(out=hT[:, kc, 0:nt], in0=dst[:, 8:8 + nt], in1=h_[:, 8:8 + nt], op=ALU.subtract), [dst, h_], [hT])
            for gi in range(4):
                for oc in range(4):
                    o = gi * 4 + oc
                    pt = kb.ps()
                    for k4 in range(4):
                        kb.op("pe", lambda e: e.matmul(pt[:, 0:nt], lhsT=pw[:, gi, k4, oc * 128:(oc + 1) * 128], rhs=hT[:, gi * 4 + k4, 0:nt], start=(k4 == 0), stop=(k4 == 3)), [pw, hT], [pt])
                    kb.op("dve", lambda e: e.scalar_tensor_tensor(out=XT[:, o, 0:nt], in0=pt[:, 0:nt], scalar=gp[:, o:o + 1], in1=XT[:, o, 0:nt], op0=ALU.mult, op1=ALU.add), [pt, gp, XT], [XT])
            layernorm(XT, nt, C)
            ffn_part(1, XT, hT, nt, cols, C, uid)
            for kc in range(KC):
                kb.op("dve", lambda e: e.tensor_scalar(out=XT[:, kc, 0:nt], in0=XT[:, kc, 0:nt], scalar1=g4[:, 3, kc:kc + 1], scalar2=b4[:, 3, kc:kc + 1], op0=ALU.mult, op1=ALU.add), [XT, g4, b4], [XT])
            yt = kb.sbn("yt", 1, [128, D], F32) * 2
            for ti in range(nt // 128):
                y_ = yt[ti % 2]
                for kc4 in range(4):
                    pt = kb.ps()
                    for k in range(4):
                        kc = kc4 * 4 + k
                        kb.op("pe", lambda e: e.transpose(out=pt[:, k * 128:(k + 1) * 128], in_=XT[:, kc, ti * 128:(ti + 1) * 128], identity=C["ident"][:]), [XT, C["ident"]], [pt])
                    kb.op("act", lambda e: e.activation(out=y_[:, kc4 * 512:(kc4 + 1) * 512], in_=pt[:], func=AF.Copy), [pt], [y_])
                r0 = blk * 512 + ti * 128
                kb.dma(y_out[q][r0:r0 + 128, :], y_[:], R=[y_], W=[kb.DR("y", (q, r0))])

    uid = 0
    for q in range(2):
        Lq = L[q]
        bs = min(512, Lq)
        for blk in range(Lq // bs):
            uid += 1
            phase_B1(q, blk * bs, bs, uid)
            phase_B2(q, [(128 + blk * bs, 0, bs)], bs, uid)
        uid += 1
        phase_B1(q, Lq, 256, uid)
        phase_B2(q, [(0, 0, 128), (128 + Lq, 128, 128)], 256, uid)
    if stop == "B":
        return kb
    for q in range(2):
        Lq = L[q]
        bs = min(512, Lq)
        for blk in range(Lq // bs):
            uid += 1
            phase_C(q, blk, bs, uid)
    return kb


def _col(v):
    v = np.asarray(v, np.float32)
    return np.ascontiguousarray(v.reshape(-1, 128).T)


def rope_tables(pos):
    pos = np.asarray(pos)
    row = (pos // 64).astype(np.float32)
    col = (pos % 64).astype(np.float32)
    half = 64
    inv_freq = (np.float32(10000.0) ** (-np.arange(0, half, 2, dtype=np.float32) / np.float32(half))).astype(np.float32)
    ar = row[:, None] * inv_freq
    ac = col[:, None] * inv_freq
    ang = np.concatenate([ar, ar, ac, ac], -1).astype(np.float32)
    return np.cos(ang).astype(np.float32), np.sin(ang).astype(np.float32)


def make_consts():
    i = np.arange(128)
    ident = np.eye(128, dtype=np.float32)
    ones = np.ones((128, 128), np.float32)
    tri_f = (i[:, None] <= i[None, :]).astype(np.float32)
    tri_b = (i[:, None] >= i[None, :]).astype(np.float32)
    madd_f = np.where(i[:, None] <= i[None, :], 0.0, NEGBIG).astype(np.float32)
    madd_b = np.where(i[:, None] >= i[None, :], 0.0, NEGBIG).astype(np.float32)
    strict_f = (i[:, None] < i[None, :]).astype(np.float32)
    strict_b = (i[:, None] > i[None, :]).astype(np.float32)
    rot = np.zeros((128, 128), np.float32)
    for k in range(32):
        rot[32 + k, k] = -1.0
        rot[k, 32 + k] = 1.0
        rot[96 + k, 64 + k] = -1.0
        rot[64 + k, 96 + k] = 1.0
    c = np.stack([ident, ones, tri_f, tri_b, madd_f, madd_b, strict_f, strict_b, rot], 1)
    return np.ascontiguousarray(c)


def ext_positions(Tq, Lq, s):
    own = np.arange(s * Lq, (s + 1) * Lq)
    left = np.arange(s * Lq - 128, s * Lq)
    right = np.arange((s + 1) * Lq, (s + 1) * Lq + 128)
    return own, left, right


def prepare_inputs(cfg, inp):
    T, L, E = cfg.T, cfg.L, cfg.E
    xs = [np.asarray(inp["x_sample"], np.float32), np.asarray(inp["x_prompt"], np.float32)]
    cs = [np.asarray(inp["c_sample"], np.float32), np.asarray(inp["c_prompt"], np.float32)]
    consts = make_consts()
    w_in = np.ascontiguousarray(np.asarray(inp["w_in"], np.float32)[0])
    conv = np.asarray(inp["conv_w"], np.float32)[0]
    convcol = np.ascontiguousarray(conv.T.reshape(24, 128, 5).transpose(1, 0, 2))
    ada_b = np.asarray(inp["ada_b"], np.float32)
    ada_bcol = np.ascontiguousarray(ada_b.reshape(2, 96, 128).transpose(2, 0, 1))
    lng = np.asarray(inp["ln_g"], np.float32).reshape(4, KC, 128).transpose(2, 0, 1)
    lnb = np.asarray(inp["ln_b"], np.float32).reshape(4, KC, 128).transpose(2, 0, 1)
    common = dict(
        ada_w=np.ascontiguousarray(np.asarray(inp["ada_w"], np.float32)),
        ada_bcol=ada_bcol, w_in=w_in,
        w_out=np.ascontiguousarray(np.asarray(inp["w_out"], np.float32)[0]),
        pool_w=np.ascontiguousarray(np.asarray(inp["pool_w"], np.float32)[0].reshape(2048, 512)),
        mlp_w1=np.ascontiguousarray(np.asarray(inp["mlp_w1"], np.float32)),
        mlp_w2=np.ascontiguousarray(np.asarray(inp["mlp_w2"], np.float32)),
        convcol=convcol,
        alog_row=np.ascontiguousarray(np.asarray(inp["a_log"], np.float32)[0].reshape(1, 16)),
        dtb_row=np.ascontiguousarray(np.asarray(inp["dt_bias"], np.float32)[0].reshape(1, 16)),
        gnw_row=np.ascontiguousarray(np.asarray(inp["gdn_norm_w"], np.float32)[0].reshape(1, 128)),
        qkw_col=np.ascontiguousarray(np.stack([np.asarray(inp["q_norm_w"], np.float32)[0], np.asarray(inp["k_norm_w"], np.float32)[0]], 1)),
        pscale_col=_col(np.asarray(inp["pool_scale"], np.float32)[0]),
        lng_col=np.ascontiguousarray(lng), lnb_col=np.ascontiguousarray(lnb),
        cst=consts,
    )
    ropeK = []
    for q in range(2):
        c, s_ = rope_tables(np.arange(T[q]))
        ropeK.append(np.ascontiguousarray(np.stack([c.T, s_.T], 0)))
    maps = []
    for core in range(8):
        g, s = core // 4, core % 4
        m = dict(common)
        m["ccol"] = np.ascontiguousarray(np.stack([_col(cs[0][g]), _col(cs[1][g])], 2))
        oh = np.zeros((128, 4), np.float32)
        oh[:, s] = 1.0
        m["onehot"] = oh
        for q in range(2):
            Tq, Lq = T[q], L[q]
            m["xf%d" % q] = np.ascontiguousarray(xs[q][g])
            own, left, right = ext_positions(Tq, Lq, s)
            pos = np.concatenate([own, left, right])
            ok = (pos >= 0) & (pos < Tq)
            xe = np.zeros((E[q], D), np.float32)
            xe[ok] = xs[q][g][pos[ok]]
            m["xe%d" % q] = xe
            m["ropeK%d" % q] = ropeK[q]
            c, s_ = rope_tables(np.clip(pos, 0, Tq - 1))
            m["ropeQ%d" % q] = np.ascontiguousarray(np.stack([c.T, s_.T], 0))
            posn = np.concatenate([left, own, right])
            okn = ((posn >= 0) & (posn < Tq)).astype(np.float32)
            m["valid%d" % q] = np.ascontiguousarray(okn.reshape(1, -1))
            ic = np.zeros((4, E[q]), np.float32)
            pc = np.clip(posn, 0, Tq - 1)
            for gi, win in enumerate(POOL_WINDOWS):
                lo = np.clip(pc - win // 2, 0, Tq - 1)
                hi = np.clip(pc + (win - 1 - win // 2), 0, Tq - 1)
                ic[gi] = 1.0 / (hi - lo + 1).astype(np.float32)
            m["invcnt%d" % q] = ic
        maps.append(m)
    return maps


_CACHE = {}


def kernel(**inputs):
    cfg = Cfg()
    kb = build_program(cfg)
    maps = prepare_inputs(cfg, inputs)
    maps = [{k: v for k, v in m.items() if k in kb.inputs} for m in maps]
    res = run_bass_kernel_spmd(kb.nc, maps, core_ids=list(range(8)))
    ys = np.zeros((2, cfg.T[0], D), np.float32)
    yp = np.zeros((2, cfg.T[1], D), np.float32)
    for core in range(8):
        g, s = core // 4, core % 4
        r = res.results[core]
        ys[g, s * cfg.L[0]:(s + 1) * cfg.L[0]] = r["y0"]
        yp[g, s * cfg.L[1]:(s + 1) * cfg.L[1]] = r["y1"]
    return (yp, ys)
```
